# Optimizing a Trainium2 kernel written in Bass

```python
import jax, jax.numpy as jnp
from jax import lax
import numpy as np

D_MODEL = 1024
BATCH = 8
SEQ = 2048
DEPTH = 4
DEC_BATCH = 8
DEC_SEQ = 64
PAST_LEN = 4096

CHUNK = 64
N_A = DEPTH // 2
N_B = DEPTH - N_A
EXPAND_A = 2
E_A = EXPAND_A * D_MODEL
POOL_WINDOWS = (2, 4, 8, 16)
N_POOL_GROUPS = len(POOL_WINDOWS)
G_A = E_A // N_POOL_GROUPS
POOL_HIST = max(POOL_WINDOWS) - 1
HEAD_DIM = 64
N_HEADS = D_MODEL // HEAD_DIM
E_B = N_HEADS * HEAD_DIM
N_LEFT_CHUNKS = 8
KV_ROWS = N_LEFT_CHUNKS * CHUNK
BAND = KV_ROWS + CHUNK
MAX_REL = 128
N_REL = 2 * MAX_REL + 1
EPS = 1e-6
NEG_INF = -1e30

kernel_name = "pool_yoco_chunk_band_encoder"


def rms_norm(x, g):
    xf = x.astype(jnp.float32)
    y = xf * lax.rsqrt(jnp.mean(xf * xf, axis=-1, keepdims=True) + EPS)
    return (y * g.astype(jnp.float32)).astype(x.dtype)


def multiscale_pool(u_pad, pos):
    T = pos.shape[0]
    uf = u_pad.astype(jnp.float32)
    B = uf.shape[0]
    cs = jnp.concatenate([jnp.zeros((B, 1, E_A), jnp.float32), jnp.cumsum(uf, axis=1)], axis=1)
    end = cs[:, POOL_HIST + 1:]
    outs = []
    for g, w in enumerate(POOL_WINDOWS):
        lo, hi = g * G_A, (g + 1) * G_A
        start = cs[:, POOL_HIST + 1 - w:POOL_HIST + 1 - w + T, lo:hi]
        cnt = jnp.minimum(pos + 1, w).astype(jnp.float32)[None, :, None]
        outs.append((end[..., lo:hi] - start) / cnt)
    pooled = jnp.concatenate(outs, axis=-1)
    return (pooled - uf[:, POOL_HIST:]).astype(u_pad.dtype)


def pool_mixer_layer(x, u_hist, pos, g_norm, w_in, w_grp, scale, w_out):
    h = rms_norm(x, g_norm)
    uz = h @ w_in
    u, z = uz[..., :E_A], uz[..., E_A:]
    u_pad = jnp.concatenate([u_hist.astype(u.dtype), u], axis=1)
    p = multiscale_pool(u_pad, pos)
    B, T, _ = p.shape
    p = jnp.einsum('btgc,gcd->btgd', p.reshape(B, T, N_POOL_GROUPS, G_A), w_grp).reshape(B, T, E_A) * scale
    y = p * jax.nn.silu(z)
    return x + y @ w_out, u_pad[:, -POOL_HIST:]


def shared_kv(x, g_kv, w_kv, g_k):
    h = rms_norm(x, g_kv)
    kv = h @ w_kv
    B, T, _ = kv.shape
    k = rms_norm(kv[..., :E_B].reshape(B, T, N_HEADS, HEAD_DIM), g_k)
    v = kv[..., E_B:].reshape(B, T, N_HEADS, HEAD_DIM)
    return k, v


def band_mask_rel(qpos, kpos):
    qc = qpos[:, None] // CHUNK
    kc = kpos[None, :] // CHUNK
    valid = (kc <= qc) & (kc >= qc - N_LEFT_CHUNKS) & (kpos[None, :] >= 0)
    rel = jnp.clip(qpos[:, None] - kpos[None, :], -MAX_REL, MAX_REL) + MAX_REL
    return valid, rel


def attend(q, k, v, qpos, kpos, rel_bias):
    valid, rel = band_mask_rel(qpos, kpos)
    s = jnp.einsum('bqhd,bkhd->bhqk', q, k).astype(jnp.float32) * (HEAD_DIM ** -0.5)
    bias = jnp.transpose(rel_bias[rel], (2, 0, 1)).astype(jnp.float32)
    s = jnp.where(valid[None, None], s + bias[None], NEG_INF)
    p = jax.nn.softmax(s, axis=-1).astype(v.dtype)
    return jnp.einsum('bhqk,bkhd->bqhd', p, v)


def chunk_band_attention_prompt(q, k, v, rel_bias):
    B, S = q.shape[0], q.shape[1]
    nc = S // CHUNK
    pad = jnp.zeros((B, KV_ROWS, N_HEADS, HEAD_DIM), k.dtype)
    k_pad = jnp.concatenate([pad, k], axis=1)
    v_pad = jnp.concatenate([pad.astype(v.dtype), v], axis=1)

    def one_chunk(c):
        start = c * CHUNK
        q_c = lax.dynamic_slice_in_dim(q, start, CHUNK, axis=1)
        k_b = lax.dynamic_slice_in_dim(k_pad, start, BAND, axis=1)
        v_b = lax.dynamic_slice_in_dim(v_pad, start, BAND, axis=1)
        qpos = start + jnp.arange(CHUNK, dtype=jnp.int32)
        kpos = start - KV_ROWS + jnp.arange(BAND, dtype=jnp.int32)
        return attend(q_c, k_b, v_b, qpos, kpos, rel_bias)

    out = lax.map(one_chunk, jnp.arange(nc, dtype=jnp.int32))
    return jnp.transpose(out, (1, 0, 2, 3, 4)).reshape(B, S, N_HEADS, HEAD_DIM)


def attn_mixer_layer(x, band_fn, g_norm, w_in, g_q, rel_bias, w_out):
    h = rms_norm(x, g_norm)
    qz = h @ w_in
    B, T, _ = qz.shape
    q = rms_norm(qz[..., :E_B].reshape(B, T, N_HEADS, HEAD_DIM), g_q)
    z = qz[..., E_B:]
    o = band_fn(q, rel_bias).reshape(B, T, E_B)
    return x + (o * jax.nn.silu(z)) @ w_out


def run_trunk(x, pool_hist, cache_k, cache_v, pos0,
              norm_a, w_in_a, w_grp_a, scale_a, w_out_a,
              norm_kv, w_kv, g_k, norm_b, w_in_b, g_q, rel_bias_b, w_out_b):
    T = x.shape[1]
    pos = pos0 + jnp.arange(T, dtype=jnp.int32)
    new_hist = []
    k = v = None
    band_fn = None
    for layer in range(DEPTH):
        if layer < N_A:
            x, hist = pool_mixer_layer(x, pool_hist[layer], pos, norm_a[layer], w_in_a[layer],
                                       w_grp_a[layer], scale_a[layer], w_out_a[layer])
            new_hist.append(hist)
            if layer == N_A - 1:
                k, v = shared_kv(x, norm_kv, w_kv, g_k)
                if cache_k is None:
                    band_fn = lambda qq, rb, k=k, v=v: chunk_band_attention_prompt(qq, k, v, rb)
                else:
                    lc = cache_k.shape[1]
                    k_b = jnp.concatenate([cache_k.astype(k.dtype), k], axis=1)
                    v_b = jnp.concatenate([cache_v.astype(v.dtype), v], axis=1)
                    kpos = pos0 - lc + jnp.arange(lc + T, dtype=jnp.int32)
                    band_fn = lambda qq, rb, k_b=k_b, v_b=v_b, kpos=kpos: attend(qq, k_b, v_b, pos, kpos, rb)
        else:
            j = layer - N_A
            x = attn_mixer_layer(x, band_fn, norm_b[j], w_in_b[j], g_q[j], rel_bias_b[j], w_out_b[j])
    return x, jnp.stack(new_hist, axis=0), k, v


def setup_inputs(seed: int = 0) -> dict:
    key = jax.random.key(seed)
    ks = jax.random.split(key, 20)
    nrm = jax.random.normal
    f32 = jnp.float32
    lc = min(KV_ROWS, PAST_LEN)
    return {
        "x_prompt": nrm(ks[0], (BATCH, SEQ, D_MODEL), f32),
        "x_sample": nrm(ks[1], (DEC_BATCH, DEC_SEQ, D_MODEL), f32),
        "state_pool": nrm(ks[2], (N_A, DEC_BATCH, POOL_HIST, E_A), f32),
        "cache_k": nrm(ks[3], (DEC_BATCH, lc, N_HEADS, HEAD_DIM), f32),
        "cache_v": nrm(ks[4], (DEC_BATCH, lc, N_HEADS, HEAD_DIM), f32),
        "norm_a": 1.0 + 0.05 * nrm(ks[5], (N_A, D_MODEL), f32),
        "w_in_a": nrm(ks[6], (N_A, D_MODEL, 2 * E_A), f32) * D_MODEL ** -0.5,
        "w_grp_a": nrm(ks[7], (N_A, N_POOL_GROUPS, G_A, G_A), f32) * G_A ** -0.5,
        "scale_a": 1.0 + 0.1 * nrm(ks[8], (N_A, E_A), f32),
        "w_out_a": nrm(ks[9], (N_A, E_A, D_MODEL), f32) * E_A ** -0.5,
        "norm_kv": 1.0 + 0.05 * nrm(ks[10], (D_MODEL,), f32),
        "w_kv": nrm(ks[11], (D_MODEL, 2 * E_B), f32) * D_MODEL ** -0.5,
        "g_k": 1.0 + 0.05 * nrm(ks[12], (HEAD_DIM,), f32),
        "norm_b": 1.0 + 0.05 * nrm(ks[13], (N_B, D_MODEL), f32),
        "w_in_b": nrm(ks[14], (N_B, D_MODEL, 2 * E_B), f32) * D_MODEL ** -0.5,
        "g_q": 1.0 + 0.05 * nrm(ks[15], (N_B, HEAD_DIM), f32),
        "rel_bias_b": 0.5 * nrm(ks[16], (N_B, N_REL, N_HEADS), f32),
        "w_out_b": nrm(ks[17], (N_B, E_B, D_MODEL), f32) * E_B ** -0.5,
    }


def reference(x_prompt, x_sample, state_pool, cache_k, cache_v,
              norm_a, w_in_a, w_grp_a, scale_a, w_out_a,
              norm_kv, w_kv, g_k, norm_b, w_in_b, g_q, rel_bias_b, w_out_b):
    weights = (norm_a, w_in_a, w_grp_a, scale_a, w_out_a,
               norm_kv, w_kv, g_k, norm_b, w_in_b, g_q, rel_bias_b, w_out_b)
    zero_hist = jnp.zeros((N_A, x_prompt.shape[0], POOL_HIST, E_A), x_prompt.dtype)
    y_prompt, new_pool_p, k_p, v_p = run_trunk(x_prompt, zero_hist, None, None, 0, *weights)
    y_sample, new_pool_s, new_k_s, new_v_s = run_trunk(x_sample, state_pool, cache_k, cache_v, PAST_LEN, *weights)
    lp = min(KV_ROWS, x_prompt.shape[1])
    new_k_p = k_p[:, -lp:]
    new_v_p = v_p[:, -lp:]
    return (y_prompt, y_sample, new_pool_p, new_pool_s, new_k_p, new_v_p, new_k_s, new_v_s)
```

```python
import numpy as np
import ml_dtypes
from contextlib import ExitStack
import concourse.bass as bass
import concourse.mybir as mybir
from concourse.bass_utils import run_bass_kernel_spmd

F32 = mybir.dt.float32
BF16 = mybir.dt.bfloat16
AF = mybir.ActivationFunctionType
ALU = mybir.AluOpType
AX = mybir.AxisListType
EPS = 1e-6
T = 2112
TP = 2176
TILES = [(0, 512), (512, 512), (1024, 512), (1536, 512), (2048, 64)]
POOL_W = (2, 4, 8, 16)

C_ONES, C_OBD, C_JBD, C_M = 0, 128, 256, 384
NCB = 384 + 12 * 144
V_NA, V_SC, V_NKV, V_NB, V_GK, V_GQ, V_GQS, V_IC, NV = 0, 16, 48, 56, 72, 73, 76, 80, 144

ENGS = ("pe", "act", "dve", "pool", "sp")
SAME_ENGINE_WAR_WAW = True


class Buf:
    __slots__ = ("name", "w", "r", "last")

    def __init__(self, name):
        self.name = name
        self.w = None
        self.r = {}
        self.last = 0


class Prog:
    def __init__(self):
        self.q = {e: [] for e in ENGS}
        self.cnt = {e: 0 for e in ENGS}
        self.seen = {e: {} for e in ENGS}
        self.stream_tot = {}
        self.seq = 0

    def _deps(self, eng, reads, writes, stream=None):
        need = {}

        def add(tok, raw):
            if tok is None:
                return
            k, v = tok
            if k == stream:
                return
            if k == eng and (eng in ("pe", "sp") or not (raw or SAME_ENGINE_WAR_WAW)):
                return
            if need.get(k, 0) < v:
                need[k] = v

        for b in reads:
            add(b.w, True)
        for b in writes:
            add(b.w, False)
            for k, v in b.r.items():
                add((k, v), False)
        waits = []
        for k, v in need.items():
            if self.seen[eng].get(k, 0) < v:
                self.seen[eng][k] = v
                waits.append((k, v))
        return waits

    def _mark(self, tok, reads, writes):
        k, v = tok
        self.seq += 1
        for b in reads:
            b.last = self.seq
        for b in writes:
            b.last = self.seq
        for b in reads:
            if b.r.get(k, 0) < v:
                b.r[k] = v
        for b in writes:
            b.w = tok
            b.r = {}

    def op(self, eng, fn, reads=(), writes=()):
        waits = self._deps(eng, reads, writes)
        self.cnt[eng] += 1
        tok = (eng, self.cnt[eng])
        self.q[eng].append((waits, fn, eng, 1))
        self._mark(tok, reads, writes)
        return tok

    def dma(self, eng, stream, fn, reads=(), writes=(), after_tokens=()):
        waits = self._deps(eng, reads, writes, stream=stream)
        for tok in after_tokens:
            if tok is None:
                continue
            k, v = tok
            if k != eng and self.seen[eng].get(k, 0) < v:
                self.seen[eng][k] = v
                waits.append((k, v))
        self.stream_tot[stream] = self.stream_tot.get(stream, 0) + 16
        tok = (stream, self.stream_tot[stream])
        self.q[eng].append((waits, fn, stream, 16))
        self._mark(tok, reads, writes)
        return tok

    def barrier(self):
        for e in ENGS:
            waits = []
            for k in list(ENGS) + list(self.stream_tot.keys()):
                v = self.cnt[k] if k in self.cnt else self.stream_tot[k]
                if k == e or v == 0:
                    continue
                if self.seen[e].get(k, 0) < v:
                    self.seen[e][k] = v
                    waits.append((k, v))
            if waits:
                self.q[e].append((waits, None, None, 0))


class _StopBuild(Exception):
    pass


_STOP = [None]
_DBG = [None]


def build_program():
    nc = bass.Bass("TRN2", target_bir_lowering=False)
    try:
        _build_body(nc)
    except _StopBuild:
        pass
    return nc


def _build_body(nc):

    def din(name, shape, dt=F32):
        return nc.dram_tensor(name, list(shape), dt, kind="ExternalInput").ap()

    def dout(name, shape):
        return nc.dram_tensor(name, list(shape), F32, kind="ExternalOutput").ap()

    xp = din("xp", [2048, 1024])
    xs = din("xs", [64, 1024])
    hist = din("hist", [2, 15, 2048])
    ck = din("ck", [512, 1024])
    cv = din("cv", [512, 1024])
    w_in_a = din("w_in_a", [2, 1024, 4096])
    w_grp_a = din("w_grp_a", [2, 4, 512, 512])
    w_out_a = din("w_out_a", [2, 2048, 1024])
    w_kv = din("w_kv", [1024, 2048])
    wqz = din("wqz", [2, 16, 128, 1024])
    w_out_b = din("w_out_b", [2, 1024, 1024])
    rel_bias = din("rel_bias", [2, 257, 16])
    vecs_d = din("vecs", [128, NV])
    gkrow_d = din("gkrow", [128, 64])
    ident_d = din("ident", [128, 128])
    cbf_d = din("cbf", [128, NCB], BF16)

    yp = dout("yp", [2048, 1024])
    ys = dout("ys", [64, 1024])
    npp = dout("npp", [2, 15, 2048])
    nps = dout("nps", [2, 15, 2048])
    nkp = dout("nkp", [512, 1024])
    nvp = dout("nvp", [512, 1024])
    nks = dout("nks", [64, 1024])
    nvs = dout("nvs", [64, 1024])
    rbx_d = nc.dram_tensor("rbx_scr", [2, 16, 384], F32, kind="Internal").ap()
    cst_d = nc.dram_tensor("cst_scr", [2, 16], F32, kind="Internal").ap()

    P = Prog()
    es = ExitStack()

    def sb(name, shape, dt):
        return es.enter_context(nc.sbuf_tensor(name, list(shape), dt))

    def emit():
        keys = list(ENGS) + list(P.stream_tot.keys())
        sems = {k: es.enter_context(nc.semaphore("s_" + k)) for k in keys}
        block = es.enter_context(nc.Block())
        for ename, attr in (("pe", "tensor"), ("act", "scalar"), ("dve", "vector"), ("pool", "gpsimd"), ("sp", "sync")):
            ops = P.q[ename]

            def body(e, ops=ops):
                for waits, fn, inc_key, inc in ops:
                    for k, v in waits:
                        e.wait_ge(sems[k], v)
                    if fn is None:
                        continue
                    inst = fn(e)
                    inst.then_inc(sems[inc_key], inc)
            getattr(block, attr)(body)

    def stop_at(tag):
        if _STOP[0] == tag:
            P.barrier()
            emit()
            es.close()
            raise _StopBuild()

    xT = sb("xT", [128, 8, TP], F32)
    ident = sb("ident_sb", [128, 128], F32)
    cbf = sb("cbf_sb", [128, NCB], BF16)
    vecs = sb("vecs_sb", [128, NV], F32)
    gkrow = sb("gkrow_sb", [128, 64], F32)
    lnv = sb("lnv", [128, 512], F32)
    rstd = sb("rstd", [128, 512], F32)
    rden = sb("rden", [128, 512], F32)
    stage = [sb("stage0", [128, 1024], F32), sb("stage1", [128, 1024], F32)]
    small = sb("small", [128, 64], F32)
    ARENA = 61000
    arena = sb("arena", [128, ARENA], BF16)
    banks = [es.enter_context(nc.psum_tensor("bank%d" % i, [128, 512], F32)) for i in range(8)]

    ones128 = cbf[:, C_ONES:C_ONES + 128]
    onesbd = cbf[:, C_OBD:C_OBD + 128]
    jbd = cbf[:, C_JBD:C_JBD + 128]

    def mmat(g, v, c0, c1):
        o = C_M + (g * 3 + v) * 144
        return cbf[:, o + c0:o + c1]

    xTB = [[Buf("xT%d_%d" % (i, c)) for c in range(8)] for i in range(5)]
    bankB = [Buf("bank%d" % i) for i in range(8)]
    constB = Buf("const")
    stageB = [Buf("stage0"), Buf("stage1")]
    lnvB, rstdB, rdenB, smallB = Buf("lnv"), Buf("rstd"), Buf("rden"), Buf("small")
    bank_rr = [0]
    norm_excl = [()]

    def next_bank(exclude=()):
        best = None
        for b in range(8):
            if b in exclude:
                continue
            key = (bankB[b].last, b)
            if best is None or key < best[0]:
                best = (key, b)
        b = best[1]
        P.seq += 1
        bankB[b].last = P.seq
        return b

    def av(off, shape):
        n = 1
        for s in shape:
            n *= s
        v = arena[:, off:off + n]
        if len(shape) == 1:
            return v
        if len(shape) == 2:
            return v.rearrange("p (a b) -> p a b", b=shape[1])
        return v.rearrange("p (a b c) -> p a b c", b=shape[1], c=shape[2])

    O_HT = 0
    hT = av(O_HT, [8, TP])
    O_S0 = 17408
    SLOT = 14848
    O_UT = O_S0 + 2 * SLOT
    utok = av(O_UT, [8, 512])
    O_SZ = O_UT + 4096
    szA = av(O_SZ, [4, 512])
    ptA = av(O_SZ + 2048, [4, 512])
    ytA = av(O_SZ + 4096, [4, 512])
    O_SQ = O_SZ + 6144
    sq = av(O_SQ, [4, 512])
    assert O_SQ + 2048 <= ARENA

    def slot_views(s):
        o = O_S0 + s * SLOT
        return dict(wu=av(o, [8, 512]), wz=av(o + 4096, [8, 512]), wg=av(o + 8192, [4, 512]),
                    wo=av(o + 10240, [4, 1024]), hist=av(o + 14336, [512]))

    slots = [slot_views(0), slot_views(1)]
    slotB = [Buf("slot0"), Buf("slot1")]
    hTB = [Buf("hT%d" % i) for i in range(5)]
    utokB = [Buf("utok%d" % i) for i in range(8)]
    szB, ptB, ytB, sqB = Buf("sz"), Buf("pt"), Buf("yt"), Buf("sq")
    stagePoolB = Buf("stagepool")

    O_WKV = O_S0
    wkv = av(O_WKV, [8, 1024])
    O_KT = 25600
    KT = av(O_KT, [8, T])
    O_VH = O_KT + 8 * T
    Vh = av(O_VH, [33, 8, 64])
    O_SQ2 = O_VH + 8 * T
    assert O_SQ2 + 512 <= ARENA
    sq1 = av(O_SQ2, [512])
    KTB = [Buf("KT%d" % i) for i in range(5)]
    VhB = [Buf("Vh%d" % i) for i in range(33)]
    hTt = [av(0, [8, 512]), av(4096, [8, 512])]
    hTtB = [Buf("hTt0"), Buf("hTt1")]
    _off = [8192]

    def take(n):
        o = _off[0]
        _off[0] += n
        return o
    QTi = [av(take(512), [512]) for _ in range(2)]
    szi = [av(take(512), [512]) for _ in range(2)]
    yg = [av(take(512), [512]) for _ in range(4)]
    Pb = [av(take(512), [512]) for _ in range(3)]
    Bt = [av(take(192), [192]) for _ in range(2)]
    cKT = [av(take(512), [512]), av(O_SQ2, [512])]
    cVh = [av(take(512), [8, 64]), av(O_SQ2 + 512, [8, 64])]
    assert O_SQ2 + 1024 <= ARENA
    wB = [dict(wq=av(take(1024), [8, 128]), wz=av(take(1024), [8, 128])) for _ in range(2)]
    wo_v = [av(take(1024), [1024]) for _ in range(4)]
    sqq = av(take(512), [512])
    sq2 = av(take(1024), [2, 512])
    assert _off[0] <= O_KT, _off[0]
    QTiB = [Buf("QTi0"), Buf("QTi1")]
    sziB = [Buf("szi0"), Buf("szi1")]
    ygB = [Buf("yg%d" % i) for i in range(4)]
    PbB = [Buf("Pb%d" % i) for i in range(3)]
    BtB = [Buf("Bt0"), Buf("Bt1")]
    cKTB = [Buf("cKT0"), Buf("cKT1")]
    cVhB = [Buf("cVh0"), Buf("cVh1")]
    wBB = [Buf("wB0"), Buf("wB1")]
    sqqB = Buf("sqq")
    cvec = small[:, 0:2]
    cvecB = [Buf("cvec0"), Buf("cvec1")]

    P.dma("sp", "cst", lambda e: e.dma_start(out=ident[:], in_=ident_d[:, :]), writes=[constB])
    P.dma("sp", "cst", lambda e: e.dma_start(out=cbf[:], in_=cbf_d[:, :]), writes=[constB])
    P.dma("sp", "cst", lambda e: e.dma_start(out=vecs[:], in_=vecs_d[:, :]), writes=[constB])
    P.dma("sp", "cst", lambda e: e.dma_start(out=gkrow[:], in_=gkrow_d[:, :]), writes=[constB])
    for tt in range(5):
        pass
    P.op("dve", lambda e: e.memset(hT[:, :, T:TP], 0.0), writes=[hTB[4]])
    for s in range(2):
        P.op("dve", lambda e, s=s: e.memset(slots[s]["hist"], 0.0), writes=[slotB[s]])
    P.op("dve", lambda e: e.tensor_scalar(out=vecs[:, V_GQS:V_GQS + 2], in0=vecs[:, V_GQ:V_GQ + 2],
                                          scalar1=0.125, scalar2=None, op0=ALU.mult),
         reads=[constB], writes=[smallB])

    stop_at('p0a')
    def load_group(l, g, after=()):
        s = (l * 4 + g) % 2
        sv = slots[s]
        wi = w_in_a[l].rearrange("(k p) n -> p k n", p=128)
        P.dma("pool", "w%d" % s, lambda e: e.dma_start(out=sv["wu"], in_=wi[:, :, g * 512:(g + 1) * 512]),
              writes=[slotB[s]], after_tokens=[b_.w for b_ in after])
        P.dma("pool", "w%d" % s,
              lambda e: e.dma_start(out=sv["wz"], in_=wi[:, :, 2048 + g * 512:2048 + (g + 1) * 512]),
              writes=[slotB[s]])
        wgv = w_grp_a[l, g].rearrange("(k p) n -> p k n", p=128)
        P.dma("pool", "w%d" % s, lambda e: e.dma_start(out=sv["wg"], in_=wgv), writes=[slotB[s]])
        wov = w_out_a[l, g * 512:(g + 1) * 512, :].rearrange("(k p) n -> p k n", p=128)
        P.dma("pool", "w%d" % s, lambda e: e.dma_start(out=sv["wo"], in_=wov), writes=[slotB[s]])
        P.dma("pool", "w%d" % s,
              lambda e: e.dma_start(out=sv["hist"][0:15, :], in_=hist[l, :, g * 512:(g + 1) * 512]),
              writes=[slotB[s]])

    load_group(0, 0)
    stop_at('p0b')

    xst = [stage[0], stage[1]] + [arena[:, O_UT + 2048 * k:O_UT + 2048 * (k + 1)].bitcast(F32) for k in range(4)]
    xstB = [stageB[0], stageB[1]] + [Buf("xst%d" % k) for k in range(4)]
    for st in range(17):
        rows = 128 if st < 16 else 64
        slot = st % 6
        tt = st // 4
        src = xp[st * 128:(st + 1) * 128, :] if st < 16 else xs[:, :]
        P.dma("sp", "xld%d" % slot, lambda e, slot=slot, rows=rows, src=src:
              e.dma_start(out=xst[slot][0:rows, :], in_=src), writes=[xstB[slot]])
        for half in range(2):
            b = next_bank()

            def fn(e, b=b, half=half, slot=slot, rows=rows):
                last = None
                for c4 in range(4):
                    c = half * 4 + c4
                    last = e.transpose(out=banks[b][:, c4 * 128:(c4 + 1) * 128],
                                       in_=xst[slot][:, c * 128:(c + 1) * 128],
                                       identity=ident[:, :])
                return last
            P.op("pe", fn, reads=[xstB[slot], constB], writes=[bankB[b]])
            o_ap = xT[:, half * 4:(half + 1) * 4, st * 128:st * 128 + rows]
            i_ap = banks[b][:, :].rearrange("p (c t) -> p c t", t=128)[:, :, 0:rows]
            if half == 0:
                P.op("act", lambda e, o_ap=o_ap, i_ap=i_ap: e.copy(out=o_ap, in_=i_ap),
                     reads=[bankB[b]], writes=xTB[tt][half * 4:(half + 1) * 4])
            else:
                P.op("dve", lambda e, o_ap=o_ap, i_ap=i_ap: e.tensor_copy(out=o_ap, in_=i_ap),
                     reads=[bankB[b]], writes=xTB[tt][half * 4:(half + 1) * 4])

    load_group(0, 1, after=xTB[4])
    stop_at('p0')
    def norm_tile(tt, gcol, dst, dstB, sqv, sqvB, nch=4):
        c0, w = TILES[tt]
        b = next_bank(exclude=norm_excl[0])
        npass = 8 // nch
        for ps in range(npass):
            P.op("act", lambda e, ps=ps: e.activation(out=sqv[:, :, 0:w],
                                                      in_=xT[:, ps * nch:(ps + 1) * nch, c0:c0 + w],
                                                      func=AF.Square),
                 reads=xTB[tt][ps * nch:(ps + 1) * nch], writes=[sqvB])

            def fn(e, ps=ps):
                last = None
                for c4 in range(nch):
                    last = e.matmul(banks[b][:, 0:w], lhsT=ones128, rhs=sqv[:, c4, 0:w],
                                    start=(ps == 0 and c4 == 0), stop=(ps == npass - 1 and c4 == nch - 1))
                return last
            P.op("pe", fn, reads=[sqvB, constB], writes=[bankB[b]])
        P.op("act", lambda e: e.activation(out=lnv[:, 0:w], in_=banks[b][:, 0:w], func=AF.Ln,
                                           bias=EPS, scale=1.0 / 1024.0),
             reads=[bankB[b]], writes=[lnvB])
        P.op("act", lambda e: e.activation(out=rstd[:, 0:w], in_=lnv[:, 0:w], func=AF.Exp, scale=-0.5),
             reads=[lnvB], writes=[rstdB])
        for c in range(8):
            P.op("dve", lambda e, c=c: e.scalar_tensor_tensor(
                out=dst(c), in0=xT[:, c, c0:c0 + w], scalar=vecs[:, gcol + c:gcol + c + 1],
                in1=rstd[:, 0:w], op0=ALU.mult, op1=ALU.mult),
                reads=[xTB[tt][c], rstdB, constB], writes=[dstB])

    def normA(tt, gcol):
        c0, w = TILES[tt]
        norm_tile(tt, gcol, lambda c: hT[:, c, c0:c0 + w], hTB[tt], sq, sqB)

    def stage_UZ(l, g, tt):
        s = (l * 4 + g) % 2
        sv = slots[s]
        c0, w = TILES[tt]
        nsub = 4 if tt < 4 else 1
        for si in range(nsub):
            b = next_bank()
            ring = (4 * tt + si) % 8
            tc0 = c0 + si * 128

            def fn(e, b=b, tc0=tc0):
                last = None
                for k in range(8):
                    last = e.matmul(banks[b][:, :], lhsT=hT[:, k, tc0:tc0 + 128], rhs=sv["wu"][:, k, :],
                                    start=(k == 0), stop=(k == 7))
                return last
            P.op("pe", fn, reads=[hTB[tt], slotB[s]], writes=[bankB[b]])
            P.op("act", lambda e, b=b, ring=ring: e.copy(out=utok[:, ring, :], in_=banks[b][:, :]),
                 reads=[bankB[b]], writes=[utokB[ring]])
            if (tt == 3 and si == 3) or tt == 4:
                p0, pn = (64, 64) if tt == 3 else (32, 32)
                r0 = 113 if tt == 3 else 49
                dst = npp if tt == 3 else nps
                P.op("dve", lambda e, b=b, p0=p0, pn=pn: e.tensor_copy(out=stage[0][p0:p0 + pn, 0:512],
                                                                      in_=banks[b][p0:p0 + pn, :]),
                     reads=[bankB[b], utokB[ring]], writes=[stageB[0]])
                P.dma("sp", "npo", lambda e, r0=r0, dst=dst: e.dma_start(
                    out=dst[l, :, g * 512:(g + 1) * 512], in_=stage[0][r0:r0 + 15, 0:512]),
                    reads=[stageB[0]])
        for c in range(4):
            b = next_bank()

            def fn(e, b=b, c=c):
                last = None
                for k in range(8):
                    last = e.matmul(banks[b][:, 0:w], lhsT=sv["wz"][:, k, c * 128:(c + 1) * 128],
                                    rhs=hT[:, k, c0:c0 + w], start=(k == 0), stop=(k == 7))
                return last
            P.op("pe", fn, reads=[hTB[tt], slotB[s]], writes=[bankB[b]])
            P.op("act", lambda e, b=b, c=c: e.activation(out=szA[:, c, 0:w], in_=banks[b][:, 0:w],
                                                         func=AF.Silu),
                 reads=[bankB[b]], writes=[szB])

    def stage_POOL(l, g, tt):
        s = (l * 4 + g) % 2
        sv = slots[s]
        c0, w = TILES[tt]
        for c in range(4):
            b = next_bank()
            srcs = []
            if tt == 4:
                srcs.append((sv["hist"][:, c * 128:(c + 1) * 128], mmat(g, 2, 0, 64), 0, 64, [slotB[s]]))
                ring = (4 * tt) % 8
                srcs.append((utok[:, ring, c * 128:(c + 1) * 128], mmat(g, 0, 0, 64), 0, 64, [utokB[ring]]))
            else:
                if tt > 0:
                    ring = (4 * tt - 1) % 8
                    srcs.append((utok[:, ring, c * 128:(c + 1) * 128], mmat(g, 0, 128, 143), 0, 15,
                                 [utokB[ring]]))
                for si in range(4):
                    ring = (4 * tt + si) % 8
                    var = 1 if (tt == 0 and si == 0) else 0
                    n = 143 if si < 3 else 128
                    srcs.append((utok[:, ring, c * 128:(c + 1) * 128], mmat(g, var, 0, n), si * 128, n,
                                 [utokB[ring]]))
            rd = [constB]
            for sr in srcs:
                rd += sr[4]

            def fn(e, b=b, srcs=srcs):
                last = None
                for i, (l_ap, r_ap, oc, n, _) in enumerate(srcs):
                    last = e.matmul(banks[b][:, oc:oc + n], lhsT=l_ap, rhs=r_ap, start=(i == 0),
                                    stop=(i == len(srcs) - 1), skip_group_check=True)
                return last
            P.op("pe", fn, reads=rd, writes=[bankB[b]])
            if tt == 0:
                P.op("dve", lambda e, b=b, c=c: e.tensor_tensor(
                    out=ptA[:, c, 0:16], in0=banks[b][:, 0:16], in1=vecs[:, V_IC + g * 16:V_IC + g * 16 + 16],
                    op=ALU.mult), reads=[bankB[b], constB], writes=[ptB])
                P.op("dve", lambda e, b=b, c=c: e.tensor_copy(out=ptA[:, c, 16:w], in_=banks[b][:, 16:w]),
                     reads=[bankB[b]], writes=[ptB])
            else:
                P.op("dve", lambda e, b=b, c=c: e.tensor_copy(out=ptA[:, c, 0:w], in_=banks[b][:, 0:w]),
                     reads=[bankB[b]], writes=[ptB])

    def stage_GRP(l, g, tt):
        s = (l * 4 + g) % 2
        sv = slots[s]
        c0, w = TILES[tt]
        for co in range(4):
            b = next_bank()

            def fn(e, b=b, co=co):
                last = None
                for ci in range(4):
                    last = e.matmul(banks[b][:, 0:w], lhsT=sv["wg"][:, ci, co * 128:(co + 1) * 128],
                                    rhs=ptA[:, ci, 0:w], start=(ci == 0), stop=(ci == 3))
                return last
            P.op("pe", fn, reads=[ptB, slotB[s]], writes=[bankB[b]])
            col = V_SC + l * 16 + g * 4 + co
            P.op("dve", lambda e, b=b, co=co, col=col: e.scalar_tensor_tensor(
                out=ytA[:, co, 0:w], in0=banks[b][:, 0:w], scalar=vecs[:, col:col + 1],
                in1=szA[:, co, 0:w], op0=ALU.mult, op1=ALU.mult),
                reads=[bankB[b], szB, constB], writes=[ytB])

    def stage_OUT(l, g, tt):
        s = (l * 4 + g) % 2
        sv = slots[s]
        c0, w = TILES[tt]
        for oc in range(8):
            b = next_bank()

            def fn(e, b=b, oc=oc):
                last = None
                for ci in range(4):
                    last = e.matmul(banks[b][:, 0:w], lhsT=sv["wo"][:, ci, oc * 128:(oc + 1) * 128],
                                    rhs=ytA[:, ci, 0:w], start=(ci == 0), stop=(ci == 3))
                return last
            P.op("pe", fn, reads=[ytB, slotB[s]], writes=[bankB[b]])
            P.op("dve", lambda e, b=b, oc=oc: e.tensor_tensor(
                out=xT[:, oc, c0:c0 + w], in0=banks[b][:, 0:w], in1=xT[:, oc, c0:c0 + w], op=ALU.add),
                reads=[bankB[b], xTB[tt][oc]], writes=[xTB[tt][oc]])

    def load_wkv(part):
        wv_ = w_kv.rearrange("(k p) n -> p k n", p=128)
        P.dma("pool", "w0", lambda e: e.dma_start(out=wkv, in_=wv_[:, :, part * 1024:(part + 1) * 1024]),
              writes=[slotB[0]])

    for tt in range(5):
        normA(tt, V_NA)
    for l in range(2):
        next_gcol = V_NA + 8 if l == 0 else V_NKV
        stop_at('A0')
        prev = None
        for g in range(4):
            for tt in range(5):
                stage_UZ(l, g, tt)
                stop_at('A1')
                stage_POOL(l, g, tt)
                stop_at('A2')
                if prev is not None:
                    stage_OUT(*prev)
                    if prev[1] == 3:
                        normA(prev[2], next_gcol)
                    stop_at('A4')
                if tt == 0 and not (l == 0 and g == 0):
                    if l == 1 and g == 3:
                        load_wkv(0)
                    else:
                        ng = (l * 4 + g + 1)
                        load_group(ng // 4, ng % 4)
                stage_GRP(l, g, tt)
                stop_at('A3')
                stop_at('I%d' % (l * 20 + g * 5 + tt))
                prev = (l, g, tt)
        stage_OUT(*prev)
        normA(prev[2], next_gcol)

    stop_at('A')
    P.barrier()
    stage_rr = [0]

    def head_norm_T(b, w, gcol, dst_ap, dstBuf, sqbuf, sqbufB):
        P.op("act", lambda e: e.activation(out=sqbuf[:, 0:w], in_=banks[b][:, 0:w], func=AF.Square),
             reads=[bankB[b]], writes=[sqbufB])
        b2 = next_bank(exclude=(b,))
        P.op("pe", lambda e: e.matmul(banks[b2][:, 0:w], lhsT=onesbd, rhs=sqbuf[:, 0:w], start=True, stop=True),
             reads=[sqbufB, constB], writes=[bankB[b2]])
        P.op("act", lambda e: e.activation(out=lnv[:, 0:w], in_=banks[b2][:, 0:w], func=AF.Ln,
                                           bias=EPS, scale=1.0 / 64.0),
             reads=[bankB[b2]], writes=[lnvB])
        P.op("act", lambda e: e.activation(out=rstd[:, 0:w], in_=lnv[:, 0:w], func=AF.Exp, scale=-0.5),
             reads=[lnvB], writes=[rstdB])
        P.op("dve", lambda e: e.scalar_tensor_tensor(
            out=dst_ap, in0=banks[b][:, 0:w], scalar=vecs[:, gcol:gcol + 1], in1=rstd[:, 0:w],
            op0=ALU.mult, op1=ALU.mult),
            reads=[bankB[b], rstdB, constB, smallB], writes=[dstBuf])

    kheld = set()
    kjobs = [(tt, i) for tt in range(5) for i in range(8)]
    kbank = {}

    def k_mm(idx):
        tt, i = kjobs[idx]
        c0, w = TILES[tt]
        b = next_bank(exclude=kheld)
        kheld.add(b)
        kbank[idx] = b

        def fn(e):
            last = None
            for k in range(8):
                last = e.matmul(banks[b][:, 0:w], lhsT=wkv[:, k, i * 128:(i + 1) * 128],
                                rhs=hT[:, k, c0:c0 + w], start=(k == 0), stop=(k == 7))
            return last
        P.op("pe", fn, reads=[hTB[tt], slotB[0]], writes=[bankB[b]])
        sqb = sqk[idx % 2]
        P.op("act", lambda e: e.activation(out=sqb[:, 0:w], in_=banks[b][:, 0:w], func=AF.Square),
             reads=[bankB[b]], writes=[sqkB[idx % 2]])

    def k_norm(idx):
        tt, i = kjobs[idx]
        c0, w = TILES[tt]
        b = kbank[idx]
        sqb = sqk[idx % 2]
        b2 = next_bank(exclude=kheld)
        P.op("pe", lambda e: e.matmul(banks[b2][:, 0:w], lhsT=onesbd, rhs=sqb[:, 0:w], start=True, stop=True),
             reads=[sqkB[idx % 2], constB], writes=[bankB[b2]])
        lb = lnk[idx % 2]
        P.op("act", lambda e: e.activation(out=lb[:, 0:w], in_=banks[b2][:, 0:w], func=AF.Ln,
                                           bias=EPS, scale=1.0 / 64.0),
             reads=[bankB[b2]], writes=[lnkB[idx % 2]])
        P.op("act", lambda e: e.activation(out=lb[:, 0:w], in_=lb[:, 0:w], func=AF.Exp, scale=-0.5),
             reads=[lnkB[idx % 2]], writes=[lnkB[idx % 2]])
        P.op("dve", lambda e: e.scalar_tensor_tensor(
            out=KT[:, i, c0:c0 + w], in0=banks[b][:, 0:w], scalar=vecs[:, V_GK:V_GK + 1], in1=lb[:, 0:w],
            op0=ALU.mult, op1=ALU.mult),
            reads=[bankB[b], lnkB[idx % 2], constB], writes=[KTB[tt]])
        kheld.discard(b)

    sqk = [sq1, av(O_SQ2 + 512, [512])]
    assert O_SQ2 + 1024 <= ARENA
    sqkB = [Buf("sqk0"), Buf("sqk1")]
    lnk = [lnv, rstd]
    lnkB = [lnvB, rstdB]
    k_mm(0)
    for idx in range(len(kjobs)):
        if idx + 1 < len(kjobs):
            k_mm(idx + 1)
        k_norm(idx)

    def k_tokmajor(st):
        tt = st // 4
        tc0 = st * 128
        rows = 128 if st < 16 else 64
        sl = stage_rr[0] % 2
        stage_rr[0] += 1
        bb = []
        for half in range(2):
            b = next_bank(exclude=tuple(bb))
            bb.append(b)

            def fn(e, b=b, half=half):
                last = None
                for k in range(8):
                    last = e.matmul(banks[b][:, :], lhsT=hT[:, k, tc0:tc0 + 128],
                                    rhs=wkv[:, k, half * 512:(half + 1) * 512], start=(k == 0), stop=(k == 7))
                return last
            P.op("pe", fn, reads=[hTB[tt], slotB[0]], writes=[bankB[b]])
            P.op("act", lambda e, b=b, half=half: e.activation(
                out=stage[sl][:, half * 512:(half + 1) * 512], in_=banks[b][:, :], func=AF.Square),
                reads=[bankB[b]], writes=[stageB[sl]])
        P.op("dve", lambda e: e.tensor_reduce(out=small[:, 16:32],
                                              in_=stage[sl][:, :].rearrange("p (h d) -> p h d", d=64),
                                              axis=AX.X, op=ALU.add),
             reads=[stageB[sl]], writes=[smallB])
        P.op("act", lambda e: e.activation(out=small[:, 32:48], in_=small[:, 16:32], func=AF.Ln,
                                           bias=EPS, scale=1.0 / 64.0), reads=[smallB], writes=[smallB])
        P.op("act", lambda e: e.activation(out=small[:, 48:64], in_=small[:, 32:48], func=AF.Exp, scale=-0.5),
             reads=[smallB], writes=[smallB])
        for half in range(2):
            b = bb[half]
            o_ap = stage[sl][:, half * 512:(half + 1) * 512].rearrange("p (h d) -> p h d", d=64)
            P.op("dve", lambda e, b=b, half=half, o_ap=o_ap: e.tensor_tensor(
                out=o_ap, in0=banks[b][:, :].rearrange("p (h d) -> p h d", d=64),
                in1=small[:, 48 + half * 8:48 + half * 8 + 8].unsqueeze(2).broadcast_to([128, 8, 64]),
                op=ALU.mult), reads=[bankB[b], smallB], writes=[stageB[sl]])
            P.op("dve", lambda e, o_ap=o_ap: e.tensor_tensor(
                out=o_ap, in0=o_ap, in1=gkrow[:, :].unsqueeze(1).broadcast_to([128, 8, 64]), op=ALU.mult),
                reads=[constB, stageB[sl]], writes=[stageB[sl]])
        dst = nkp[(st - 12) * 128:(st - 11) * 128, :] if st < 16 else nks[:, :]
        P.dma("sp", "ost%d" % sl, lambda e: e.dma_start(out=dst, in_=stage[sl][0:rows, :]),
              reads=[stageB[sl]])

    for st in (12, 13, 14, 15, 16):
        k_tokmajor(st)

    load_wkv(1)
    for st in range(17):
        tt = st // 4
        tc0 = st * 128
        rows = 128 if st < 16 else 64
        want_out = st >= 12
        if want_out:
            sl = stage_rr[0] % 2
            stage_rr[0] += 1
        for half in range(2):
            b = next_bank()

            def fn(e, b=b, half=half, tc0=tc0):
                last = None
                for k in range(8):
                    last = e.matmul(banks[b][:, :], lhsT=hT[:, k, tc0:tc0 + 128],
                                    rhs=wkv[:, k, half * 512:(half + 1) * 512], start=(k == 0), stop=(k == 7))
                return last
            P.op("pe", fn, reads=[hTB[tt], slotB[0]], writes=[bankB[b]])
            bv = banks[b][:, :].rearrange("p (i h d) -> p i h d", h=2, d=64)
            ncp = 2 if st < 16 else 1
            for cp in range(ncp):
                for hh in range(2):
                    ch = 2 * st + cp
                    o_ap = Vh[hh * 64:(hh + 1) * 64, ch, half * 4:(half + 1) * 4, :]
                    i_ap = bv[cp * 64:(cp + 1) * 64, :, hh, :]
                    if half == 0:
                        P.op("act", lambda e, o_ap=o_ap, i_ap=i_ap: e.copy(out=o_ap, in_=i_ap),
                             reads=[bankB[b]], writes=[VhB[ch]])
                    else:
                        P.op("dve", lambda e, o_ap=o_ap, i_ap=i_ap: e.tensor_copy(out=o_ap, in_=i_ap),
                             reads=[bankB[b]], writes=[VhB[ch]])
            if want_out and half == 0:
                P.op("act", lambda e, b=b, half=half, sl=sl: e.copy(
                    out=stage[sl][:, half * 512:(half + 1) * 512], in_=banks[b][:, :]),
                    reads=[bankB[b]], writes=[stageB[sl]])
            elif want_out:
                P.op("dve", lambda e, b=b, half=half, sl=sl: e.tensor_copy(
                    out=stage[sl][:, half * 512:(half + 1) * 512], in_=banks[b][:, :]),
                    reads=[bankB[b]], writes=[stageB[sl]])
        if want_out:
            dst = nvp[(st - 12) * 128:(st - 11) * 128, :] if st < 16 else nvs[:, :]
            P.dma("sp", "ost%d" % sl, lambda e, dst=dst, sl=sl, rows=rows: e.dma_start(
                out=dst, in_=stage[sl][0:rows, :]), reads=[stageB[sl]])

    stop_at('KV')
    P.barrier()
    rbT = stage[0][0:16, 0:384]
    rbX = stage[0][0:16, 512:896]
    rbIn = stage[1]

    def prep_bias(j):
        P.dma("sp", "rb", lambda e: e.dma_start(out=rbIn[:, 0:16], in_=rel_bias[j, 0:128, :]), writes=[stageB[1]])
        P.dma("sp", "rb", lambda e: e.dma_start(out=rbIn[:, 128:144], in_=rel_bias[j, 128:256, :]),
              writes=[stageB[1]])
        P.dma("sp", "rb", lambda e: e.dma_start(out=rbIn[0:1, 256:272], in_=rel_bias[j, 256:257, :]),
              writes=[stageB[1]])
        b = next_bank()

        def fn(e):
            e.transpose(out=banks[b][:, 0:128], in_=rbIn[:, 0:128], identity=ident[:, :])
            e.transpose(out=banks[b][:, 128:256], in_=rbIn[:, 128:256], identity=ident[:, :])
            return e.transpose(out=banks[b][:, 256:384], in_=rbIn[:, 256:384], identity=ident[:, :])
        P.op("pe", fn, reads=[stageB[1], constB], writes=[bankB[b]])
        P.op("dve", lambda e: e.tensor_copy(out=rbT[:, 0:257], in_=banks[b][0:16, 0:257]),
             reads=[bankB[b]], writes=[stageB[0]])
        P.op("dve", lambda e: e.tensor_copy(out=rbT[:, 257:384],
                                            in_=rbT[:, 256:257].to_broadcast([16, 127])),
             reads=[stageB[0]], writes=[stageB[0]])
        P.op("dve", lambda e: e.tensor_scalar(out=rbX, in0=rbT, scalar1=rbT[:, 256:257], scalar2=None,
                                              op0=ALU.subtract),
             reads=[stageB[0]], writes=[stageB[0]])
        P.dma("sp", "rbo", lambda e: e.dma_start(out=rbx_d[j], in_=rbX), reads=[stageB[0]], writes=[rbxB])
        P.dma("sp", "rbo", lambda e: e.dma_start(out=cst_d[j].rearrange("(h o) -> h o", o=1),
                                                 in_=rbT[:, 256:257]), reads=[stageB[0]], writes=[rbxB])

    rbxB = Buf("rbx")
    prep_bias(0)
    prep_bias(1)

    def key_list(tt):
        ents = []
        if tt < 4:
            for kc in range(max(0, 8 * tt - 8), 8 * tt + 8):
                qa = max(kc, 8 * tt) - 8 * tt
                qb = min(kc + 8, 8 * tt + 7) - 8 * tt + 1
                ents.append(("n", kc, qa, qb, 8 * tt + qa - kc))
        else:
            for jc in range(8):
                ents.append(("c", jc, 0, 1, 8 - jc))
            ents.append(("n", 32, 0, 1, 0))
        return ents

    wqzB = [Buf("wqz0"), Buf("wqz1")]
    woB = [Buf("wo%d" % i) for i in range(4)]
    held = set()
    norm_excl[0] = held
    steps = [(j, tt, i) for j in range(2) for tt in range(5) for i in range(8)]
    NSTEP = len(steps) if _STOP[0] is None or not str(_STOP[0]).startswith('N') else int(_STOP[0][1:])
    hb_of = {}
    _hb = 0
    for j in range(2):
        for tt in range(5):
            hb_of[(j, tt)] = _hb
            _hb ^= 1
    st8 = {}
    ez = stage[1][:, 512:1024]
    ezB = Buf("ez")

    def b_loads(n):
        j, tt, i = steps[n]
        s = n % 2
        s3 = n % 4
        P.dma("pool", "wb%d" % s, lambda e: e.dma_start(
            out=wB[s]["wq"], in_=wqz[j, i].rearrange("p (k n) -> p k n", n=128)), writes=[wqzB[s]])
        P.dma("pool", "wb%d" % s, lambda e: e.dma_start(
            out=wB[s]["wz"], in_=wqz[j, 8 + i].rearrange("p (k n) -> p k n", n=128)), writes=[wqzB[s]])
        for hh in range(2):
            src = bass.AP(tensor=rbx_d.tensor, offset=(j * 16 + 2 * i + hh) * 384 + 65, ap=[[1, 64], [1, 192]])
            P.dma("pool", "bt%d" % s, lambda e, src=src, hh=hh: e.dma_start(
                out=Bt[s][hh * 64:(hh + 1) * 64, :], in_=src), reads=[rbxB], writes=[BtB[s]])
            csrc = bass.AP(tensor=cst_d.tensor, offset=j * 16 + 2 * i + hh, ap=[[0, 64], [1, 1]])
            P.dma("sp", "cv%d" % s, lambda e, csrc=csrc, hh=hh: e.dma_start(
                out=small[hh * 64:(hh + 1) * 64, s:s + 1], in_=csrc), reads=[rbxB], writes=[cvecB[s]])
        return

    def b_load_wo(n):
        j, tt, i = steps[n]
        s3 = n % 4
        P.dma("pool", "wo%d" % s3, lambda e: e.dma_start(
            out=wo_v[s3], in_=w_out_b[j, i * 128:(i + 1) * 128, :]), writes=[woB[s3]])

    def b_cache(n):
        j, tt, i = steps[n]
        s = n % 2
        if tt == 4:
            P.dma("sp", "ck", lambda e: e.dma_start(
                out=stage[1][:, 0:512].rearrange("p (t c) -> p t c", c=128),
                in_=ck[:, i * 128:(i + 1) * 128].rearrange("(t p) c -> p t c", p=128)), writes=[stageB[1]])
            bt = next_bank(exclude=held)

            def fnT(e):
                last = None
                for t4 in range(4):
                    last = e.transpose(out=banks[bt][:, t4 * 128:(t4 + 1) * 128],
                                       in_=stage[1][:, t4 * 128:(t4 + 1) * 128], identity=ident[:, :])
                return last
            P.op("pe", fnT, reads=[stageB[1], constB], writes=[bankB[bt]])
            P.op("dve", lambda e: e.tensor_copy(out=cKT[s], in_=banks[bt][:, :]), reads=[bankB[bt]],
                 writes=[cKTB[s]])
            for hh in range(2):
                h = 2 * i + hh
                P.dma("pool", "cvh%d" % s, lambda e, hh=hh, h=h: e.dma_start(
                    out=cVh[s][hh * 64:(hh + 1) * 64, :, :],
                    in_=cv[:, h * 64:(h + 1) * 64].rearrange("(jc k) d -> k jc d", k=64)), writes=[cVhB[s]])

    qc = stage[0][:, 0:512]
    ssc = stage[0][:, 512:1024]
    qcB, sscB = Buf("qc"), Buf("ssc")

    def b_proj(n):
        j, tt, i = steps[n]
        c0, w = TILES[tt]
        s = n % 2
        hbuf = hb_of[(j, tt)]
        bq = next_bank(exclude=held)

        def fnq(e):
            last = None
            for k in range(8):
                last = e.matmul(banks[bq][:, 0:w], lhsT=wB[s]["wq"][:, k, :], rhs=hTt[hbuf][:, k, 0:w],
                                start=(k == 0), stop=(k == 7))
            return last
        P.op("pe", fnq, reads=[hTtB[hbuf], wqzB[s]], writes=[bankB[bq]])
        P.op("dve", lambda e: e.tensor_copy(out=qc[:, 0:w], in_=banks[bq][:, 0:w]),
             reads=[bankB[bq]], writes=[qcB, stageB[0]])
        P.op("dve", lambda e: e.tensor_tensor(out=sqq[:, 0:w], in0=banks[bq][:, 0:w], in1=qc[:, 0:w], op=ALU.mult),
             reads=[bankB[bq], qcB], writes=[sqqB])
        bz = next_bank(exclude=held)

        def fnz(e):
            last = None
            for k in range(8):
                last = e.matmul(banks[bz][:, 0:w], lhsT=wB[s]["wz"][:, k, :], rhs=hTt[hbuf][:, k, 0:w],
                                start=(k == 0), stop=(k == 7))
            return last
        P.op("pe", fnz, reads=[hTtB[hbuf], wqzB[s]], writes=[bankB[bz]])
        P.op("dve", lambda e: e.tensor_copy(out=szi[s][:, 0:w], in_=banks[bz][:, 0:w]),
             reads=[bankB[bz]], writes=[sziB[s]])

    def b_ss(n):
        j, tt, i = steps[n]
        c0, w = TILES[tt]
        b2 = next_bank(exclude=held)
        P.op("pe", lambda e: e.matmul(banks[b2][:, 0:w], lhsT=onesbd, rhs=sqq[:, 0:w], start=True, stop=True),
             reads=[sqqB, constB], writes=[bankB[b2]])
        P.op("dve", lambda e: e.tensor_copy(out=ssc[:, 0:w], in_=banks[b2][:, 0:w]),
             reads=[bankB[b2]], writes=[sscB, stageB[0]])

    def b_qnorm_act(n):
        j, tt, i = steps[n]
        c0, w = TILES[tt]
        P.op("act", lambda e: e.activation(out=lnv[:, 0:w], in_=ssc[:, 0:w], func=AF.Ln,
                                           bias=EPS, scale=1.0 / 64.0),
             reads=[sscB], writes=[lnvB])
        P.op("act", lambda e: e.activation(out=rstd[:, 0:w], in_=lnv[:, 0:w], func=AF.Exp, scale=-0.5),
             reads=[lnvB], writes=[rstdB])

    def b_qnorm_dve(n):
        j, tt, i = steps[n]
        c0, w = TILES[tt]
        s = n % 2
        gcol = V_GQS + j
        P.op("dve", lambda e: e.scalar_tensor_tensor(
            out=QTi[s][:, 0:w], in0=qc[:, 0:w], scalar=vecs[:, gcol:gcol + 1], in1=rstd[:, 0:w],
            op0=ALU.mult, op1=ALU.mult),
            reads=[qcB, rstdB, constB, smallB], writes=[QTiB[s]])

    def b_ez(n):
        j, tt, i = steps[n]
        c0, w = TILES[tt]
        s = n % 2
        P.op("act", lambda e: e.activation(out=ez[:, 0:w], in_=szi[s][:, 0:w], func=AF.Exp, scale=-1.0),
             reads=[sziB[s]], writes=[ezB, stageB[1]])

    def pack_groups(ents):
        rem = sorted(range(len(ents)), key=lambda e: -(ents[e][3] - ents[e][2]))
        groups = []
        while rem:
            g = [rem.pop(0)]
            tot = (ents[g[0]][3] - ents[g[0]][2]) * 64
            k = 0
            while k < len(rem):
                nn = (ents[rem[k]][3] - ents[rem[k]][2]) * 64
                if tot + nn <= 512:
                    g.append(rem.pop(k))
                    tot += nn
                else:
                    k += 1
            groups.append(g)
        return groups

    def b_sloop(n):
        j, tt, i = steps[n]
        c0, w = TILES[tt]
        s = n % 2
        ents = key_list(tt)
        groups = pack_groups(ents)
        info = {}

        def alloc():
            bo = next_bank(exclude=held)
            held.add(bo)
            bd = next_bank(exclude=held)
            held.add(bd)
            st8[n] = dict(bo=bo, bd=bd)

        def emit_S(gi):
            bs = next_bank(exclude=held)
            held.add(bs)
            offs = []
            off = 0
            rd = [QTiB[s], BtB[s], constB]
            mm = []
            for e_ in groups[gi]:
                kind, idx, qa, qb, d0 = ents[e_]
                q0, q1 = qa * 64, qb * 64
                nq = q1 - q0
                nb = max(0, min(3 - d0, qb - qa)) * 64
                if kind == "n":
                    kcol = idx * 64
                    k_lo, k_hi = KT[0:64, i, kcol:kcol + 64], KT[64:128, i, kcol:kcol + 64]
                    rd.append(KTB[4] if idx == 32 else KTB[idx // 8])
                else:
                    k_lo, k_hi = cKT[s][0:64, idx * 64:(idx + 1) * 64], cKT[s][64:128, idx * 64:(idx + 1) * 64]
                    rd.append(cKTB[s])
                mm.append((off, q0, q1, nb, d0, k_lo, k_hi))
                offs.append(off)
                off += nq
            info[gi] = (bs, offs, off)

            def fns(e):
                last = None
                for (o_, q0, q1, nb, d0, k_lo, k_hi) in mm:
                    nq = q1 - q0
                    e.matmul(banks[bs][0:64, o_:o_ + nq], lhsT=k_lo, rhs=QTi[s][0:64, q0:q1], start=True,
                             stop=(nb == 0), skip_group_check=True)
                    last = e.matmul(banks[bs][64:128, o_:o_ + nq], lhsT=k_hi, rhs=QTi[s][64:128, q0:q1],
                                    start=True, stop=(nb == 0), skip_group_check=True)
                    if nb > 0:
                        e.matmul(banks[bs][0:64, o_:o_ + nb], lhsT=jbd[0:64, 0:64],
                                 rhs=Bt[s][0:64, d0 * 64:d0 * 64 + nb], start=False, stop=True,
                                 skip_group_check=True)
                        last = e.matmul(banks[bs][64:128, o_:o_ + nb], lhsT=jbd[64:128, 64:128],
                                        rhs=Bt[s][64:128, d0 * 64:d0 * 64 + nb], start=False, stop=True,
                                        skip_group_check=True)
                return last
            P.op("pe", fns, reads=rd, writes=[bankB[bs]])

        def emit_PV(gi):
            bs, offs, tot = info[gi]
            bo, bd = st8[n]["bo"], st8[n]["bd"]
            pb = gi % 3
            P.op("act", lambda e: e.activation(
                out=Pb[pb][:, 0:tot], in_=banks[bs][:, 0:tot], func=AF.Exp, bias=small[:, s:s + 1], scale=1.0),
                reads=[bankB[bs], cvecB[s]], writes=[PbB[pb]])
            held.discard(bs)
            rd = [PbB[pb], constB]
            mm = []
            for e_, o_ in zip(groups[gi], offs):
                kind, idx, qa, qb, d0 = ents[e_]
                q0, q1 = qa * 64, qb * 64
                if kind == "n":
                    v_lo, v_hi = Vh[0:64, idx, i, :], Vh[64:128, idx, i, :]
                    rd.append(VhB[idx])
                else:
                    v_lo, v_hi = cVh[s][0:64, idx, :], cVh[s][64:128, idx, :]
                    rd.append(cVhB[s])
                mm.append((o_, q0, q1, v_lo, v_hi))

            def fno(e):
                last = None
                for k_, (o_, q0, q1, v_lo, v_hi) in enumerate(mm):
                    nq = q1 - q0
                    st_ = (gi == 0 and k_ == 0)
                    sp_ = (gi == len(groups) - 1 and k_ == len(mm) - 1)
                    e.matmul(banks[bo][0:64, q0:q1], lhsT=v_lo, rhs=Pb[pb][0:64, o_:o_ + nq], start=st_, stop=sp_,
                             skip_group_check=True)
                    e.matmul(banks[bo][64:128, q0:q1], lhsT=v_hi, rhs=Pb[pb][64:128, o_:o_ + nq], start=st_,
                             stop=sp_, skip_group_check=True)
                    e.matmul(banks[bd][0:64, q0:q1], lhsT=onesbd[0:64, 0:64], rhs=Pb[pb][0:64, o_:o_ + nq],
                             start=st_, stop=sp_, skip_group_check=True)
                    last = e.matmul(banks[bd][64:128, q0:q1], lhsT=onesbd[64:128, 64:128],
                                    rhs=Pb[pb][64:128, o_:o_ + nq], start=st_, stop=sp_, skip_group_check=True)
                return last
            P.op("pe", fno, reads=rd, writes=[bankB[bo], bankB[bd]])

        AHEAD = 2
        th = [alloc]
        for gi in range(min(AHEAD, len(groups))):
            th.append(lambda gi=gi: emit_S(gi))
        for gi in range(len(groups)):
            if gi + AHEAD < len(groups):
                th.append(lambda gi=gi: emit_S(gi + AHEAD))
            th.append(lambda gi=gi: emit_PV(gi))
        return th

    def b_gate(n):
        j, tt, i = steps[n]
        c0, w = TILES[tt]
        s = n % 2
        P.op("act", lambda e: e.activation(out=ez[:, 0:w], in_=szi[s][:, 0:w], func=AF.Exp, scale=-1.0),
             reads=[sziB[s]], writes=[ezB, stageB[1]])

    def b_tail_D(n):
        j, tt, i = steps[n]
        c0, w = TILES[tt]
        bd = st8[n]["bd"]
        P.op("dve", lambda e: e.scalar_tensor_tensor(
            out=rden[:, 0:w], in0=ez[:, 0:w], scalar=1.0, in1=banks[bd][:, 0:w], op0=ALU.add, op1=ALU.mult),
            reads=[ezB, bankB[bd]], writes=[rdenB])

    def b_tail_t0(n):
        j, tt, i = steps[n]
        c0, w = TILES[tt]
        s = n % 2
        bo = st8[n]["bo"]
        P.op("dve", lambda e: e.tensor_tensor(out=ez[:, 0:w], in0=banks[bo][:, 0:w], in1=szi[s][:, 0:w],
                                              op=ALU.mult),
             reads=[bankB[bo], sziB[s]], writes=[ezB, stageB[1]])

    def b_tail_rd(n):
        j, tt, i = steps[n]
        c0, w = TILES[tt]
        P.op("act", lambda e: e.activation(out=rden[:, 0:w], in_=rden[:, 0:w], func=AF.Ln),
             reads=[rdenB], writes=[rdenB])
        P.op("act", lambda e: e.activation(out=rden[:, 0:w], in_=rden[:, 0:w], func=AF.Exp, scale=-1.0),
             reads=[rdenB], writes=[rdenB])

    def b_tail_y(n):
        j, tt, i = steps[n]
        c0, w = TILES[tt]
        y4 = n % 4
        P.op("dve", lambda e: e.tensor_tensor(out=yg[y4][:, 0:w], in0=ez[:, 0:w], in1=rden[:, 0:w],
                                              op=ALU.mult),
             reads=[rdenB, ezB], writes=[ygB[y4]])

    def b_outproj_group(g, half):
        prs = [p for p in (2 * g, 2 * g + 1) if p < NSTEP]
        j, tt, _ = steps[prs[0]]
        c0, w = TILES[tt]
        for oc in range(half * 4, half * 4 + 4):
            b = next_bank(exclude=held)

            def fn(e, b=b, oc=oc):
                last = None
                for ki, p in enumerate(prs):
                    last = e.matmul(banks[b][:, 0:w], lhsT=wo_v[p % 4][:, oc * 128:(oc + 1) * 128],
                                    rhs=yg[p % 4][:, 0:w], start=(ki == 0), stop=(ki == len(prs) - 1))
                return last
            P.op("pe", fn, reads=[ygB[p % 4] for p in prs] + [woB[p % 4] for p in prs], writes=[bankB[b]])
            P.op("dve", lambda e, b=b, oc=oc: e.tensor_tensor(
                out=xT[:, oc, c0:c0 + w], in0=banks[b][:, 0:w], in1=xT[:, oc, c0:c0 + w], op=ALU.add),
                reads=[bankB[b], xTB[tt][oc]], writes=[xTB[tt][oc]])

    def b_norm(j, tt):
        c0, w = TILES[tt]
        hbuf = hb_of[(j, tt)]
        norm_tile(tt, V_NB + j * 8, lambda c: hTt[hbuf][:, c, 0:w], hTtB[hbuf], sq2, sqB, nch=2)

    b_norm(0, 0)
    b_loads(0)
    b_proj(0)
    b_ss(0)
    b_qnorm_act(0)
    b_qnorm_dve(0)
    ngroups = (NSTEP + 1) // 2
    oq = [(g, h) for g in range(ngroups) for h in range(2)]
    oqi = 0
    late_release = []
    for n in range(NSTEP):
        j, tt, i = steps[n]
        while oqi < len(oq) and 2 * oq[oqi][0] + 3 + oq[oqi][1] <= n:
            b_outproj_group(*oq[oqi])
            oqi += 1
        b_load_wo(n)
        if n >= 1:
            b_tail_y(n - 1)
        if n + 1 < NSTEP:
            b_loads(n + 1)
            b_cache(n + 1)
            b_proj(n + 1)
        for b_ in late_release:
            held.discard(b_)
        late_release = []
        if i == 5 and n + 3 < NSTEP:
            jn, ttn, _ = steps[n + 3]
            b_norm(jn, ttn)
        th = b_sloop(n)
        cut = min(len(th), 1 + 2 + 2 * 2)
        for t_ in th[:cut]:
            t_()
        b_gate(n)
        for t_ in th[cut:]:
            t_()
        b_tail_D(n)
        if n + 1 < NSTEP:
            b_ss(n + 1)
        b_tail_t0(n)
        if n + 1 < NSTEP:
            b_qnorm_act(n + 1)
            b_qnorm_dve(n + 1)
        b_tail_rd(n)
        late_release = [st8[n]["bo"], st8[n]["bd"]]
    b_tail_y(NSTEP - 1)
    for b_ in late_release:
        held.discard(b_)
    while oqi < len(oq):
        b_outproj_group(*oq[oqi])
        oqi += 1
    stop_at('B')
    P.barrier()
    fst = [stage[0], stage[1]] + [arena[:, 2048 * k:2048 * (k + 1)].bitcast(F32) for k in range(6)]
    fstB = [stageB[0], stageB[1]] + [Buf("fst%d" % k) for k in range(6)]
    for st in range(17):
        rows = 128 if st < 16 else 64
        tt = st // 4
        sl = st % 8
        for half in range(2):
            b = next_bank()

            def fn(e, b=b, half=half, st=st, rows=rows):
                last = None
                for c4 in range(4):
                    c = half * 4 + c4
                    last = e.transpose(out=banks[b][:, c4 * 128:(c4 + 1) * 128],
                                       in_=xT[:, c, st * 128:(st + 1) * 128], identity=ident[:, :])
                return last
            P.op("pe", fn, reads=xTB[tt][half * 4:(half + 1) * 4] + [constB], writes=[bankB[b]])
            if half == 0:
                P.op("act", lambda e, b=b, sl=sl, rows=rows: e.copy(out=fst[sl][0:rows, 0:512],
                                                                   in_=banks[b][0:rows, :]),
                     reads=[bankB[b]], writes=[fstB[sl]])
            else:
                P.op("dve", lambda e, b=b, sl=sl, rows=rows: e.tensor_copy(out=fst[sl][0:rows, 512:1024],
                                                                          in_=banks[b][0:rows, :]),
                     reads=[bankB[b]], writes=[fstB[sl]])
        dst = yp[st * 128:(st + 1) * 128, :] if st < 16 else ys[:, :]
        P.dma("sp", "fst%d" % sl, lambda e, dst=dst, sl=sl, rows=rows: e.dma_start(
            out=dst, in_=fst[sl][0:rows, :]), reads=[fstB[sl]])
    P.barrier()

    emit()
    es.close()


def _consts():
    bf = ml_dtypes.bfloat16
    cb = np.zeros((128, NCB), np.float32)
    cb[:, C_ONES:C_ONES + 128] = 1.0
    cb[0:64, C_OBD:C_OBD + 64] = 1.0
    cb[64:128, C_OBD + 64:C_OBD + 128] = 1.0
    for r in range(64):
        cb[r, C_JBD + 63 - r] = 1.0
        cb[64 + r, C_JBD + 64 + 63 - r] = 1.0
    s = np.arange(128)[:, None]
    t = np.arange(144)[None, :]
    dt = t - s
    for g, wdw in enumerate(POOL_W):
        band = ((dt >= 0) & (dt <= wdw - 1)).astype(np.float32)
        dlt = (dt == 0).astype(np.float32)
        o = C_M + (g * 3) * 144
        cb[:, o:o + 144] = band / wdw - dlt
        cnt = np.minimum(t + 1, wdw).astype(np.float32)
        first = band / wdw - dlt
        first[:, :16] = (band - cnt * dlt)[:, :16]
        cb[:, o + 144:o + 288] = first
        dth = t + 15 - s
        bh = ((dth >= 1) & (dth <= wdw - 1) & (s < 15)).astype(np.float32)
        cb[:, o + 288:o + 432] = bh / wdw
    return cb.astype(bf), np.eye(128, dtype=np.float32)


_NC_CACHE = {}


def kernel(x_prompt, x_sample, state_pool, cache_k, cache_v, norm_a, w_in_a, w_grp_a, scale_a, w_out_a,
           norm_kv, w_kv, g_k, norm_b, w_in_b, g_q, rel_bias_b, w_out_b):
    f = lambda a: np.ascontiguousarray(np.asarray(a, dtype=np.float32))
    x_prompt, x_sample, state_pool, cache_k, cache_v = map(f, (x_prompt, x_sample, state_pool, cache_k, cache_v))
    norm_a, w_in_a, w_grp_a, scale_a, w_out_a = map(f, (norm_a, w_in_a, w_grp_a, scale_a, w_out_a))
    norm_kv, w_kv, g_k, norm_b, w_in_b, g_q, rel_bias_b, w_out_b = map(
        f, (norm_kv, w_kv, g_k, norm_b, w_in_b, g_q, rel_bias_b, w_out_b))
    cbf_np, ident_np = _consts()
    vecs = np.zeros((128, NV), np.float32)
    for l in range(2):
        vecs[:, V_NA + l * 8:V_NA + l * 8 + 8] = norm_a[l].reshape(8, 128).T
        vecs[:, V_SC + l * 16:V_SC + l * 16 + 16] = scale_a[l].reshape(16, 128).T
        vecs[:, V_NB + l * 8:V_NB + l * 8 + 8] = norm_b[l].reshape(8, 128).T
        vecs[:, V_GQ + l] = np.concatenate([g_q[l], g_q[l]])
    vecs[:, V_NKV:V_NKV + 8] = norm_kv.reshape(8, 128).T
    vecs[:, V_GK] = np.concatenate([g_k, g_k])
    for g, wdw in enumerate(POOL_W):
        vecs[:, V_IC + g * 16:V_IC + g * 16 + 16] = (1.0 / np.minimum(np.arange(16) + 1, wdw)).astype(np.float32)[None, :]
    gkrow = np.ascontiguousarray(np.broadcast_to(g_k[None, :], (128, 64)))
    wqz = np.ascontiguousarray(w_in_b.reshape(2, 8, 128, 16, 128).transpose(0, 3, 2, 1, 4).reshape(2, 16, 128, 1024))
    if "nc" not in _NC_CACHE:
        _NC_CACHE["nc"] = build_program()
    nc = _NC_CACHE["nc"]
    in_maps = []
    for b in range(8):
        in_maps.append({
            "xp": x_prompt[b], "xs": x_sample[b], "hist": np.ascontiguousarray(state_pool[:, b]),
            "ck": cache_k[b].reshape(512, 1024), "cv": cache_v[b].reshape(512, 1024),
            "w_in_a": w_in_a, "w_grp_a": w_grp_a, "w_out_a": w_out_a, "w_kv": w_kv, "wqz": wqz,
            "w_out_b": w_out_b, "rel_bias": rel_bias_b, "vecs": vecs, "gkrow": gkrow, "ident": ident_np,
            "cbf": cbf_np,
        })
    res = run_bass_kernel_spmd(nc, in_maps, core_ids=list(range(8)))
    r = res.results
    y_prompt = np.stack([r[b]["yp"] for b in range(8)]).astype(np.float32)
    y_sample = np.stack([r[b]["ys"] for b in range(8)]).astype(np.float32)
    new_pool_p = np.stack([r[b]["npp"] for b in range(8)], axis=1).astype(np.float32)
    new_pool_s = np.stack([r[b]["nps"] for b in range(8)], axis=1).astype(np.float32)
    new_k_p = np.stack([r[b]["nkp"] for b in range(8)]).reshape(8, 512, 16, 64).astype(np.float32)
    new_v_p = np.stack([r[b]["nvp"] for b in range(8)]).reshape(8, 512, 16, 64).astype(np.float32)
    new_k_s = np.stack([r[b]["nks"] for b in range(8)]).reshape(8, 64, 16, 64).astype(np.float32)
    new_v_s = np.stack([r[b]["nvs"] for b in range(8)]).reshape(8, 64, 16, 64).astype(np.float32)
    return (y_prompt, y_sample, new_pool_p, new_pool_s, new_k_p, new_v_p, new_k_s, new_v_s)
```

```python
import numpy as np
import ml_dtypes
from contextlib import ExitStack
import concourse.bass as bass
import concourse.mybir as mybir
from concourse.bass_utils import run_bass_kernel_spmd

F32 = mybir.dt.float32
BF16 = mybir.dt.bfloat16
AF = mybir.ActivationFunctionType
ALU = mybir.AluOpType
AX = mybir.AxisListType
EPS = 1e-6
T = 2112
TP = 2176
TILES = [(0, 512), (512, 512), (1024, 512), (1536, 512), (2048, 64)]
POOL_W = (2, 4, 8, 16)

C_ONES, C_OBD, C_JBD, C_M = 0, 128, 256, 384
NCB = 384 + 12 * 144
V_NA, V_SC, V_NKV, V_NB, V_GK, V_GQ, V_GQS, V_IC, NV = 0, 16, 48, 56, 72, 73, 76, 80, 144

ENGS = ("pe", "act", "dve", "pool", "sp")
SAME_ENGINE_WAR_WAW = True


class Buf:
    __slots__ = ("name", "w", "r", "last")

    def __init__(self, name):
        self.name = name
        self.w = None
        self.r = {}
        self.last = 0


class Prog:
    def __init__(self):
        self.q = {e: [] for e in ENGS}
        self.cnt = {e: 0 for e in ENGS}
        self.seen = {e: {} for e in ENGS}
        self.stream_tot = {}
        self.seq = 0

    def _deps(self, eng, reads, writes, stream=None):
        need = {}

        def add(tok, raw):
            if tok is None:
                return
            k, v = tok
            if k == stream:
                return
            if k == eng and (eng in ("pe", "sp") or not (raw or SAME_ENGINE_WAR_WAW)):
                return
            if need.get(k, 0) < v:
                need[k] = v

        for b in reads:
            add(b.w, True)
        for b in writes:
            add(b.w, False)
            for k, v in b.r.items():
                add((k, v), False)
        waits = []
        for k, v in need.items():
            if self.seen[eng].get(k, 0) < v:
                self.seen[eng][k] = v
                waits.append((k, v))
        return waits

    def _mark(self, tok, reads, writes):
        k, v = tok
        self.seq += 1
        for b in reads:
            b.last = self.seq
        for b in writes:
            b.last = self.seq
        for b in reads:
            if b.r.get(k, 0) < v:
                b.r[k] = v
        for b in writes:
            b.w = tok
            b.r = {}

    def op(self, eng, fn, reads=(), writes=()):
        waits = self._deps(eng, reads, writes)
        self.cnt[eng] += 1
        tok = (eng, self.cnt[eng])
        self.q[eng].append((waits, fn, eng, 1))
        self._mark(tok, reads, writes)
        return tok

    def dma(self, eng, stream, fn, reads=(), writes=(), after_tokens=()):
        waits = self._deps(eng, reads, writes, stream=stream)
        for tok in after_tokens:
            if tok is None:
                continue
            k, v = tok
            if k != eng and self.seen[eng].get(k, 0) < v:
                self.seen[eng][k] = v
                waits.append((k, v))
        self.stream_tot[stream] = self.stream_tot.get(stream, 0) + 16
        tok = (stream, self.stream_tot[stream])
        self.q[eng].append((waits, fn, stream, 16))
        self._mark(tok, reads, writes)
        return tok

    def barrier(self):
        for e in ENGS:
            waits = []
            for k in list(ENGS) + list(self.stream_tot.keys()):
                v = self.cnt[k] if k in self.cnt else self.stream_tot[k]
                if k == e or v == 0:
                    continue
                if self.seen[e].get(k, 0) < v:
                    self.seen[e][k] = v
                    waits.append((k, v))
            if waits:
                self.q[e].append((waits, None, None, 0))


class _StopBuild(Exception):
    pass


_STOP = [None]
_DBG = [None]


def build_program():
    nc = bass.Bass("TRN2", target_bir_lowering=False)
    try:
        _build_body(nc)
    except _StopBuild:
        pass
    return nc


def _build_body(nc):

    def din(name, shape, dt=F32):
        return nc.dram_tensor(name, list(shape), dt, kind="ExternalInput").ap()

    def dout(name, shape):
        return nc.dram_tensor(name, list(shape), F32, kind="ExternalOutput").ap()

    xp = din("xp", [2048, 1024])
    xs = din("xs", [64, 1024])
    hist = din("hist", [2, 15, 2048])
    ck = din("ck", [512, 1024])
    cv = din("cv", [512, 1024])
    w_in_a = din("w_in_a", [2, 1024, 4096])
    w_grp_a = din("w_grp_a", [2, 4, 512, 512])
    w_out_a = din("w_out_a", [2, 2048, 1024])
    w_kv = din("w_kv", [1024, 2048])
    wqz = din("wqz", [2, 16, 128, 1024])
    w_out_b = din("w_out_b", [2, 1024, 1024])
    rel_bias = din("rel_bias", [2, 257, 16])
    vecs_d = din("vecs", [128, NV])
    gkrow_d = din("gkrow", [128, 64])
    ident_d = din("ident", [128, 128])
    cbf_d = din("cbf", [128, NCB], BF16)

    yp = dout("yp", [2048, 1024])
    ys = dout("ys", [64, 1024])
    npp = dout("npp", [2, 15, 2048])
    nps = dout("nps", [2, 15, 2048])
    nkp = dout("nkp", [512, 1024])
    nvp = dout("nvp", [512, 1024])
    nks = dout("nks", [64, 1024])
    nvs = dout("nvs", [64, 1024])
    rbx_d = nc.dram_tensor("rbx_scr", [2, 16, 384], F32, kind="Internal").ap()
    cst_d = nc.dram_tensor("cst_scr", [2, 16], F32, kind="Internal").ap()

    P = Prog()
    es = ExitStack()

    def sb(name, shape, dt):
        return es.enter_context(nc.sbuf_tensor(name, list(shape), dt))

    def emit():
        keys = list(ENGS) + list(P.stream_tot.keys())
        sems = {k: es.enter_context(nc.semaphore("s_" + k)) for k in keys}
        block = es.enter_context(nc.Block())
        for ename, attr in (("pe", "tensor"), ("act", "scalar"), ("dve", "vector"), ("pool", "gpsimd"), ("sp", "sync")):
            ops = P.q[ename]

            def body(e, ops=ops):
                for waits, fn, inc_key, inc in ops:
                    for k, v in waits:
                        e.wait_ge(sems[k], v)
                    if fn is None:
                        continue
                    inst = fn(e)
                    inst.then_inc(sems[inc_key], inc)
            getattr(block, attr)(body)

    def stop_at(tag):
        if _STOP[0] == tag:
            P.barrier()
            emit()
            es.close()
            raise _StopBuild()

    xT = sb("xT", [128, 8, TP], F32)
    ident = sb("ident_sb", [128, 128], F32)
    cbf = sb("cbf_sb", [128, NCB], BF16)
    vecs = sb("vecs_sb", [128, NV], F32)
    gkrow = sb("gkrow_sb", [128, 64], F32)
    lnv = sb("lnv", [128, 512], F32)
    rstd = sb("rstd", [128, 512], F32)
    rden = sb("rden", [128, 512], F32)
    stage = [sb("stage0", [128, 1024], F32), sb("stage1", [128, 1024], F32)]
    small = sb("small", [128, 64], F32)
    ARENA = 61000
    arena = sb("arena", [128, ARENA], BF16)
    banks = [es.enter_context(nc.psum_tensor("bank%d" % i, [128, 512], F32)) for i in range(8)]

    ones128 = cbf[:, C_ONES:C_ONES + 128]
    onesbd = cbf[:, C_OBD:C_OBD + 128]
    jbd = cbf[:, C_JBD:C_JBD + 128]

    def mmat(g, v, c0, c1):
        o = C_M + (g * 3 + v) * 144
        return cbf[:, o + c0:o + c1]

    xTB = [[Buf("xT%d_%d" % (i, c)) for c in range(8)] for i in range(5)]
    bankB = [Buf("bank%d" % i) for i in range(8)]
    constB = Buf("const")
    stageB = [Buf("stage0"), Buf("stage1")]
    lnvB, rstdB, rdenB, smallB = Buf("lnv"), Buf("rstd"), Buf("rden"), Buf("small")
    bank_rr = [0]
    norm_excl = [()]

    def next_bank(exclude=()):
        best = None
        for b in range(8):
            if b in exclude:
                continue
            key = (bankB[b].last, b)
            if best is None or key < best[0]:
                best = (key, b)
        b = best[1]
        P.seq += 1
        bankB[b].last = P.seq
        return b

    def av(off, shape):
        n = 1
        for s in shape:
            n *= s
        v = arena[:, off:off + n]
        if len(shape) == 1:
            return v
        if len(shape) == 2:
            return v.rearrange("p (a b) -> p a b", b=shape[1])
        return v.rearrange("p (a b c) -> p a b c", b=shape[1], c=shape[2])

    O_HT = 0
    hT = av(O_HT, [8, TP])
    O_S0 = 17408
    SLOT = 14848
    O_UT = O_S0 + 2 * SLOT
    utok = av(O_UT, [8, 512])
    O_SZ = O_UT + 4096
    szA = av(O_SZ, [4, 512])
    ptA = av(O_SZ + 2048, [4, 512])
    ytA = av(O_SZ + 4096, [4, 512])
    O_SQ = O_SZ + 6144
    sq = av(O_SQ, [4, 512])
    assert O_SQ + 2048 <= ARENA

    def slot_views(s):
        o = O_S0 + s * SLOT
        return dict(wu=av(o, [8, 512]), wz=av(o + 4096, [8, 512]), wg=av(o + 8192, [4, 512]),
                    wo=av(o + 10240, [4, 1024]), hist=av(o + 14336, [512]))

    slots = [slot_views(0), slot_views(1)]
    slotB = [Buf("slot0"), Buf("slot1")]
    hTB = [Buf("hT%d" % i) for i in range(5)]
    utokB = [Buf("utok%d" % i) for i in range(8)]
    szB, ptB, ytB, sqB = Buf("sz"), Buf("pt"), Buf("yt"), Buf("sq")
    stagePoolB = Buf("stagepool")

    O_WKV = O_S0
    wkv = av(O_WKV, [8, 1024])
    O_KT = 25600
    KT = av(O_KT, [8, T])
    O_VH = O_KT + 8 * T
    Vh = av(O_VH, [33, 8, 64])
    O_SQ2 = O_VH + 8 * T
    assert O_SQ2 + 512 <= ARENA
    sq1 = av(O_SQ2, [512])
    KTB = [Buf("KT%d" % i) for i in range(5)]
    VhB = [Buf("Vh%d" % i) for i in range(33)]
    hTt = [av(0, [8, 512]), av(4096, [8, 512])]
    hTtB = [Buf("hTt0"), Buf("hTt1")]
    _off = [8192]

    def take(n):
        o = _off[0]
        _off[0] += n
        return o
    QTi = [av(take(512), [512]) for _ in range(2)]
    szi = [av(take(512), [512]) for _ in range(2)]
    yg = [av(take(512), [512]) for _ in range(4)]
    Pb = [av(take(512), [512]) for _ in range(3)]
    Bt = [av(take(192), [192]) for _ in range(2)]
    cKT = [av(take(512), [512]), av(O_SQ2, [512])]
    cVh = [av(take(512), [8, 64]), av(O_SQ2 + 512, [8, 64])]
    assert O_SQ2 + 1024 <= ARENA
    wB = [dict(wq=av(take(1024), [8, 128]), wz=av(take(1024), [8, 128])) for _ in range(2)]
    wo_v = [av(take(1024), [1024]) for _ in range(4)]
    sqq = av(take(512), [512])
    sq2 = av(take(1024), [2, 512])
    assert _off[0] <= O_KT, _off[0]
    QTiB = [Buf("QTi0"), Buf("QTi1")]
    sziB = [Buf("szi0"), Buf("szi1")]
    ygB = [Buf("yg%d" % i) for i in range(4)]
    PbB = [Buf("Pb%d" % i) for i in range(3)]
    BtB = [Buf("Bt0"), Buf("Bt1")]
    cKTB = [Buf("cKT0"), Buf("cKT1")]
    cVhB = [Buf("cVh0"), Buf("cVh1")]
    wBB = [Buf("wB0"), Buf("wB1")]
    sqqB = Buf("sqq")
    cvec = small[:, 0:2]
    cvecB = [Buf("cvec0"), Buf("cvec1")]

    P.dma("sp", "cst", lambda e: e.dma_start(out=ident[:], in_=ident_d[:, :]), writes=[constB])
    P.dma("sp", "cst", lambda e: e.dma_start(out=cbf[:], in_=cbf_d[:, :]), writes=[constB])
    P.dma("sp", "cst", lambda e: e.dma_start(out=vecs[:], in_=vecs_d[:, :]), writes=[constB])
    P.dma("sp", "cst", lambda e: e.dma_start(out=gkrow[:], in_=gkrow_d[:, :]), writes=[constB])
    for tt in range(5):
        pass
    P.op("dve", lambda e: e.memset(hT[:, :, T:TP], 0.0), writes=[hTB[4]])
    for s in range(2):
        P.op("dve", lambda e, s=s: e.memset(slots[s]["hist"], 0.0), writes=[slotB[s]])
    P.op("dve", lambda e: e.tensor_scalar(out=vecs[:, V_GQS:V_GQS + 2], in0=vecs[:, V_GQ:V_GQ + 2],
                                          scalar1=0.125, scalar2=None, op0=ALU.mult),
         reads=[constB], writes=[smallB])

    stop_at('p0a')
    def load_group(l, g, after=()):
        s = (l * 4 + g) % 2
        sv = slots[s]
        wi = w_in_a[l].rearrange("(k p) n -> p k n", p=128)
        P.dma("pool", "w%d" % s, lambda e: e.dma_start(out=sv["wu"], in_=wi[:, :, g * 512:(g + 1) * 512]),
              writes=[slotB[s]], after_tokens=[b_.w for b_ in after])
        P.dma("pool", "w%d" % s,
              lambda e: e.dma_start(out=sv["wz"], in_=wi[:, :, 2048 + g * 512:2048 + (g + 1) * 512]),
              writes=[slotB[s]])
        wgv = w_grp_a[l, g].rearrange("(k p) n -> p k n", p=128)
        P.dma("pool", "w%d" % s, lambda e: e.dma_start(out=sv["wg"], in_=wgv), writes=[slotB[s]])
        wov = w_out_a[l, g * 512:(g + 1) * 512, :].rearrange("(k p) n -> p k n", p=128)
        P.dma("pool", "w%d" % s, lambda e: e.dma_start(out=sv["wo"], in_=wov), writes=[slotB[s]])
        P.dma("pool", "w%d" % s,
              lambda e: e.dma_start(out=sv["hist"][0:15, :], in_=hist[l, :, g * 512:(g + 1) * 512]),
              writes=[slotB[s]])

    load_group(0, 0)
    stop_at('p0b')

    xst = [stage[0], stage[1]]
    xstB = [stageB[0], stageB[1]]
    def p0_subtile(st):
        rows = 128 if st < 16 else 64
        slot = st % 2
        tt = st // 4
        src = xp[st * 128:(st + 1) * 128, :] if st < 16 else xs[:, :]
        P.dma("sp", "xld%d" % slot, lambda e, slot=slot, rows=rows, src=src:
              e.dma_start(out=xst[slot][0:rows, :], in_=src), writes=[xstB[slot]])
        for half in range(2):
            b = next_bank()

            def fn(e, b=b, half=half, slot=slot, rows=rows):
                last = None
                for c4 in range(4):
                    c = half * 4 + c4
                    last = e.transpose(out=banks[b][:, c4 * 128:(c4 + 1) * 128],
                                       in_=xst[slot][:, c * 128:(c + 1) * 128],
                                       identity=ident[:, :])
                return last
            P.op("pe", fn, reads=[xstB[slot], constB], writes=[bankB[b]])
            o_ap = xT[:, half * 4:(half + 1) * 4, st * 128:st * 128 + rows]
            i_ap = banks[b][:, :].rearrange("p (c t) -> p c t", t=128)[:, :, 0:rows]
            if half == 0:
                P.op("act", lambda e, o_ap=o_ap, i_ap=i_ap: e.copy(out=o_ap, in_=i_ap),
                     reads=[bankB[b]], writes=xTB[tt][half * 4:(half + 1) * 4])
            else:
                P.op("dve", lambda e, o_ap=o_ap, i_ap=i_ap: e.tensor_copy(out=o_ap, in_=i_ap),
                     reads=[bankB[b]], writes=xTB[tt][half * 4:(half + 1) * 4])

    def phase0_tile(tt):
        for st in (range(4 * tt, 4 * tt + 4) if tt < 4 else [16]):
            p0_subtile(st)

    stop_at('p0')
    def norm_tile(tt, gcol, dst, dstB, sqv, sqvB, nch=4):
        c0, w = TILES[tt]
        b = next_bank(exclude=norm_excl[0])
        npass = 8 // nch
        for ps in range(npass):
            P.op("act", lambda e, ps=ps: e.activation(out=sqv[:, :, 0:w],
                                                      in_=xT[:, ps * nch:(ps + 1) * nch, c0:c0 + w],
                                                      func=AF.Square),
                 reads=xTB[tt][ps * nch:(ps + 1) * nch], writes=[sqvB])

            def fn(e, ps=ps):
                last = None
                for c4 in range(nch):
                    last = e.matmul(banks[b][:, 0:w], lhsT=ones128, rhs=sqv[:, c4, 0:w],
                                    start=(ps == 0 and c4 == 0), stop=(ps == npass - 1 and c4 == nch - 1))
                return last
            P.op("pe", fn, reads=[sqvB, constB], writes=[bankB[b]])
        P.op("act", lambda e: e.activation(out=lnv[:, 0:w], in_=banks[b][:, 0:w], func=AF.Ln,
                                           bias=EPS, scale=1.0 / 1024.0),
             reads=[bankB[b]], writes=[lnvB])
        P.op("act", lambda e: e.activation(out=rstd[:, 0:w], in_=lnv[:, 0:w], func=AF.Exp, scale=-0.5),
             reads=[lnvB], writes=[rstdB])
        for c in range(8):
            P.op("dve", lambda e, c=c: e.scalar_tensor_tensor(
                out=dst(c), in0=xT[:, c, c0:c0 + w], scalar=vecs[:, gcol + c:gcol + c + 1],
                in1=rstd[:, 0:w], op0=ALU.mult, op1=ALU.mult),
                reads=[xTB[tt][c], rstdB, constB], writes=[dstB])

    def normA(tt, gcol):
        c0, w = TILES[tt]
        norm_tile(tt, gcol, lambda c: hT[:, c, c0:c0 + w], hTB[tt], sq, sqB)

    def stage_UZ(l, g, tt):
        s = (l * 4 + g) % 2
        sv = slots[s]
        c0, w = TILES[tt]
        nsub = 4 if tt < 4 else 1
        for si in range(nsub):
            b = next_bank()
            ring = (4 * tt + si) % 8
            tc0 = c0 + si * 128

            def fn(e, b=b, tc0=tc0):
                last = None
                for k in range(8):
                    last = e.matmul(banks[b][:, :], lhsT=hT[:, k, tc0:tc0 + 128], rhs=sv["wu"][:, k, :],
                                    start=(k == 0), stop=(k == 7))
                return last
            P.op("pe", fn, reads=[hTB[tt], slotB[s]], writes=[bankB[b]])
            P.op("act", lambda e, b=b, ring=ring: e.copy(out=utok[:, ring, :], in_=banks[b][:, :]),
                 reads=[bankB[b]], writes=[utokB[ring]])
            if (tt == 3 and si == 3) or tt == 4:
                p0, pn = (64, 64) if tt == 3 else (32, 32)
                r0 = 113 if tt == 3 else 49
                dst = npp if tt == 3 else nps
                P.op("dve", lambda e, b=b, p0=p0, pn=pn: e.tensor_copy(out=stage[0][p0:p0 + pn, 0:512],
                                                                      in_=banks[b][p0:p0 + pn, :]),
                     reads=[bankB[b], utokB[ring]], writes=[stageB[0]])
                P.dma("sp", "npo", lambda e, r0=r0, dst=dst: e.dma_start(
                    out=dst[l, :, g * 512:(g + 1) * 512], in_=stage[0][r0:r0 + 15, 0:512]),
                    reads=[stageB[0]])
        for c in range(4):
            b = next_bank()

            def fn(e, b=b, c=c):
                last = None
                for k in range(8):
                    last = e.matmul(banks[b][:, 0:w], lhsT=sv["wz"][:, k, c * 128:(c + 1) * 128],
                                    rhs=hT[:, k, c0:c0 + w], start=(k == 0), stop=(k == 7))
                return last
            P.op("pe", fn, reads=[hTB[tt], slotB[s]], writes=[bankB[b]])
            P.op("act", lambda e, b=b, c=c: e.activation(out=szA[:, c, 0:w], in_=banks[b][:, 0:w],
                                                         func=AF.Silu),
                 reads=[bankB[b]], writes=[szB])

    def stage_POOL(l, g, tt):
        s = (l * 4 + g) % 2
        sv = slots[s]
        c0, w = TILES[tt]
        for c in range(4):
            b = next_bank()
            srcs = []
            if tt == 4:
                srcs.append((sv["hist"][:, c * 128:(c + 1) * 128], mmat(g, 2, 0, 64), 0, 64, [slotB[s]]))
                ring = (4 * tt) % 8
                srcs.append((utok[:, ring, c * 128:(c + 1) * 128], mmat(g, 0, 0, 64), 0, 64, [utokB[ring]]))
            else:
                if tt > 0:
                    ring = (4 * tt - 1) % 8
                    srcs.append((utok[:, ring, c * 128:(c + 1) * 128], mmat(g, 0, 128, 143), 0, 15,
                                 [utokB[ring]]))
                for si in range(4):
                    ring = (4 * tt + si) % 8
                    var = 1 if (tt == 0 and si == 0) else 0
                    n = 143 if si < 3 else 128
                    srcs.append((utok[:, ring, c * 128:(c + 1) * 128], mmat(g, var, 0, n), si * 128, n,
                                 [utokB[ring]]))
            rd = [constB]
            for sr in srcs:
                rd += sr[4]

            def fn(e, b=b, srcs=srcs):
                last = None
                for i, (l_ap, r_ap, oc, n, _) in enumerate(srcs):
                    last = e.matmul(banks[b][:, oc:oc + n], lhsT=l_ap, rhs=r_ap, start=(i == 0),
                                    stop=(i == len(srcs) - 1), skip_group_check=True)
                return last
            P.op("pe", fn, reads=rd, writes=[bankB[b]])
            if tt == 0:
                P.op("dve", lambda e, b=b, c=c: e.tensor_tensor(
                    out=ptA[:, c, 0:16], in0=banks[b][:, 0:16], in1=vecs[:, V_IC + g * 16:V_IC + g * 16 + 16],
                    op=ALU.mult), reads=[bankB[b], constB], writes=[ptB])
                P.op("dve", lambda e, b=b, c=c: e.tensor_copy(out=ptA[:, c, 16:w], in_=banks[b][:, 16:w]),
                     reads=[bankB[b]], writes=[ptB])
            else:
                P.op("dve", lambda e, b=b, c=c: e.tensor_copy(out=ptA[:, c, 0:w], in_=banks[b][:, 0:w]),
                     reads=[bankB[b]], writes=[ptB])

    def stage_GRP(l, g, tt):
        s = (l * 4 + g) % 2
        sv = slots[s]
        c0, w = TILES[tt]
        for co in range(4):
            b = next_bank()

            def fn(e, b=b, co=co):
                last = None
                for ci in range(4):
                    last = e.matmul(banks[b][:, 0:w], lhsT=sv["wg"][:, ci, co * 128:(co + 1) * 128],
                                    rhs=ptA[:, ci, 0:w], start=(ci == 0), stop=(ci == 3))
                return last
            P.op("pe", fn, reads=[ptB, slotB[s]], writes=[bankB[b]])
            col = V_SC + l * 16 + g * 4 + co
            P.op("dve", lambda e, b=b, co=co, col=col: e.scalar_tensor_tensor(
                out=ytA[:, co, 0:w], in0=banks[b][:, 0:w], scalar=vecs[:, col:col + 1],
                in1=szA[:, co, 0:w], op0=ALU.mult, op1=ALU.mult),
                reads=[bankB[b], szB, constB], writes=[ytB])

    def stage_OUT(l, g, tt):
        s = (l * 4 + g) % 2
        sv = slots[s]
        c0, w = TILES[tt]
        for oc in range(8):
            b = next_bank()

            def fn(e, b=b, oc=oc):
                last = None
                for ci in range(4):
                    last = e.matmul(banks[b][:, 0:w], lhsT=sv["wo"][:, ci, oc * 128:(oc + 1) * 128],
                                    rhs=ytA[:, ci, 0:w], start=(ci == 0), stop=(ci == 3))
                return last
            P.op("pe", fn, reads=[ytB, slotB[s]], writes=[bankB[b]])
            P.op("dve", lambda e, b=b, oc=oc: e.tensor_tensor(
                out=xT[:, oc, c0:c0 + w], in0=banks[b][:, 0:w], in1=xT[:, oc, c0:c0 + w], op=ALU.add),
                reads=[bankB[b], xTB[tt][oc]], writes=[xTB[tt][oc]])

    def load_wkv(part):
        wv_ = w_kv.rearrange("(k p) n -> p k n", p=128)
        P.dma("pool", "w0", lambda e: e.dma_start(out=wkv, in_=wv_[:, :, part * 1024:(part + 1) * 1024]),
              writes=[slotB[0]])

    for l in range(2):
        next_gcol = V_NA + 8 if l == 0 else V_NKV
        stop_at('A0')
        prev = None
        for g in range(4):
            for tt in range(5):
                if l == 0 and g == 0:
                    phase0_tile(tt)
                    normA(tt, V_NA)
                    if tt == 4:
                        load_group(0, 1, after=xTB[4])
                stage_UZ(l, g, tt)
                stop_at('A1')
                stage_POOL(l, g, tt)
                stop_at('A2')
                if prev is not None:
                    stage_OUT(*prev)
                    if prev[1] == 3:
                        normA(prev[2], next_gcol)
                    stop_at('A4')
                if tt == 0 and not (l == 0 and g == 0):
                    if l == 1 and g == 3:
                        load_wkv(0)
                    else:
                        ng = (l * 4 + g + 1)
                        load_group(ng // 4, ng % 4)
                stage_GRP(l, g, tt)
                stop_at('A3')
                stop_at('I%d' % (l * 20 + g * 5 + tt))
                prev = (l, g, tt)
        stage_OUT(*prev)
        normA(prev[2], next_gcol)

    stop_at('A')
    P.barrier()
    stage_rr = [0]

    def head_norm_T(b, w, gcol, dst_ap, dstBuf, sqbuf, sqbufB):
        P.op("act", lambda e: e.activation(out=sqbuf[:, 0:w], in_=banks[b][:, 0:w], func=AF.Square),
             reads=[bankB[b]], writes=[sqbufB])
        b2 = next_bank(exclude=(b,))
        P.op("pe", lambda e: e.matmul(banks[b2][:, 0:w], lhsT=onesbd, rhs=sqbuf[:, 0:w], start=True, stop=True),
             reads=[sqbufB, constB], writes=[bankB[b2]])
        P.op("act", lambda e: e.activation(out=lnv[:, 0:w], in_=banks[b2][:, 0:w], func=AF.Ln,
                                           bias=EPS, scale=1.0 / 64.0),
             reads=[bankB[b2]], writes=[lnvB])
        P.op("act", lambda e: e.activation(out=rstd[:, 0:w], in_=lnv[:, 0:w], func=AF.Exp, scale=-0.5),
             reads=[lnvB], writes=[rstdB])
        P.op("dve", lambda e: e.scalar_tensor_tensor(
            out=dst_ap, in0=banks[b][:, 0:w], scalar=vecs[:, gcol:gcol + 1], in1=rstd[:, 0:w],
            op0=ALU.mult, op1=ALU.mult),
            reads=[bankB[b], rstdB, constB, smallB], writes=[dstBuf])

    kheld = set()
    kjobs = [(tt, i) for tt in range(5) for i in range(8)]
    kbank = {}

    def k_mm(idx):
        tt, i = kjobs[idx]
        c0, w = TILES[tt]
        b = next_bank(exclude=kheld)
        kheld.add(b)
        kbank[idx] = b

        def fn(e):
            last = None
            for k in range(8):
                last = e.matmul(banks[b][:, 0:w], lhsT=wkv[:, k, i * 128:(i + 1) * 128],
                                rhs=hT[:, k, c0:c0 + w], start=(k == 0), stop=(k == 7))
            return last
        P.op("pe", fn, reads=[hTB[tt], slotB[0]], writes=[bankB[b]])
        sqb = sqk[idx % 2]
        P.op("act", lambda e: e.activation(out=sqb[:, 0:w], in_=banks[b][:, 0:w], func=AF.Square),
             reads=[bankB[b]], writes=[sqkB[idx % 2]])

    def k_norm(idx):
        tt, i = kjobs[idx]
        c0, w = TILES[tt]
        b = kbank[idx]
        sqb = sqk[idx % 2]
        b2 = next_bank(exclude=kheld)
        P.op("pe", lambda e: e.matmul(banks[b2][:, 0:w], lhsT=onesbd, rhs=sqb[:, 0:w], start=True, stop=True),
             reads=[sqkB[idx % 2], constB], writes=[bankB[b2]])
        lb = lnk[idx % 2]
        P.op("act", lambda e: e.activation(out=lb[:, 0:w], in_=banks[b2][:, 0:w], func=AF.Ln,
                                           bias=EPS, scale=1.0 / 64.0),
             reads=[bankB[b2]], writes=[lnkB[idx % 2]])
        P.op("act", lambda e: e.activation(out=lb[:, 0:w], in_=lb[:, 0:w], func=AF.Exp, scale=-0.5),
             reads=[lnkB[idx % 2]], writes=[lnkB[idx % 2]])
        P.op("dve", lambda e: e.scalar_tensor_tensor(
            out=KT[:, i, c0:c0 + w], in0=banks[b][:, 0:w], scalar=vecs[:, V_GK:V_GK + 1], in1=lb[:, 0:w],
            op0=ALU.mult, op1=ALU.mult),
            reads=[bankB[b], lnkB[idx % 2], constB], writes=[KTB[tt]])
        kheld.discard(b)

    sqk = [sq1, av(O_SQ2 + 512, [512])]
    assert O_SQ2 + 1024 <= ARENA
    sqkB = [Buf("sqk0"), Buf("sqk1")]
    lnk = [lnv, rstd]
    lnkB = [lnvB, rstdB]
    k_mm(0)
    for idx in range(len(kjobs)):
        if idx + 1 < len(kjobs):
            k_mm(idx + 1)
        k_norm(idx)

    def k_tokmajor(st):
        tt = st // 4
        tc0 = st * 128
        rows = 128 if st < 16 else 64
        sl = stage_rr[0] % 2
        stage_rr[0] += 1
        bb = []
        for half in range(2):
            b = next_bank(exclude=tuple(bb))
            bb.append(b)

            def fn(e, b=b, half=half):
                last = None
                for k in range(8):
                    last = e.matmul(banks[b][:, :], lhsT=hT[:, k, tc0:tc0 + 128],
                                    rhs=wkv[:, k, half * 512:(half + 1) * 512], start=(k == 0), stop=(k == 7))
                return last
            P.op("pe", fn, reads=[hTB[tt], slotB[0]], writes=[bankB[b]])
            P.op("act", lambda e, b=b, half=half: e.activation(
                out=stage[sl][:, half * 512:(half + 1) * 512], in_=banks[b][:, :], func=AF.Square),
                reads=[bankB[b]], writes=[stageB[sl]])
        P.op("dve", lambda e: e.tensor_reduce(out=small[:, 16:32],
                                              in_=stage[sl][:, :].rearrange("p (h d) -> p h d", d=64),
                                              axis=AX.X, op=ALU.add),
             reads=[stageB[sl]], writes=[smallB])
        P.op("act", lambda e: e.activation(out=small[:, 32:48], in_=small[:, 16:32], func=AF.Ln,
                                           bias=EPS, scale=1.0 / 64.0), reads=[smallB], writes=[smallB])
        P.op("act", lambda e: e.activation(out=small[:, 48:64], in_=small[:, 32:48], func=AF.Exp, scale=-0.5),
             reads=[smallB], writes=[smallB])
        for half in range(2):
            b = bb[half]
            o_ap = stage[sl][:, half * 512:(half + 1) * 512].rearrange("p (h d) -> p h d", d=64)
            P.op("dve", lambda e, b=b, half=half, o_ap=o_ap: e.tensor_tensor(
                out=o_ap, in0=banks[b][:, :].rearrange("p (h d) -> p h d", d=64),
                in1=small[:, 48 + half * 8:48 + half * 8 + 8].unsqueeze(2).broadcast_to([128, 8, 64]),
                op=ALU.mult), reads=[bankB[b], smallB], writes=[stageB[sl]])
            P.op("dve", lambda e, o_ap=o_ap: e.tensor_tensor(
                out=o_ap, in0=o_ap, in1=gkrow[:, :].unsqueeze(1).broadcast_to([128, 8, 64]), op=ALU.mult),
                reads=[constB, stageB[sl]], writes=[stageB[sl]])
        dst = nkp[(st - 12) * 128:(st - 11) * 128, :] if st < 16 else nks[:, :]
        P.dma("sp", "ost%d" % sl, lambda e: e.dma_start(out=dst, in_=stage[sl][0:rows, :]),
              reads=[stageB[sl]])

    for st in (12, 13, 14, 15, 16):
        k_tokmajor(st)

    load_wkv(1)
    for st in range(17):
        tt = st // 4
        tc0 = st * 128
        rows = 128 if st < 16 else 64
        want_out = st >= 12
        if want_out:
            sl = stage_rr[0] % 2
            stage_rr[0] += 1
        for half in range(2):
            b = next_bank()

            def fn(e, b=b, half=half, tc0=tc0):
                last = None
                for k in range(8):
                    last = e.matmul(banks[b][:, :], lhsT=hT[:, k, tc0:tc0 + 128],
                                    rhs=wkv[:, k, half * 512:(half + 1) * 512], start=(k == 0), stop=(k == 7))
                return last
            P.op("pe", fn, reads=[hTB[tt], slotB[0]], writes=[bankB[b]])
            bv = banks[b][:, :].rearrange("p (i h d) -> p i h d", h=2, d=64)
            ncp = 2 if st < 16 else 1
            for cp in range(ncp):
                for hh in range(2):
                    ch = 2 * st + cp
                    o_ap = Vh[hh * 64:(hh + 1) * 64, ch, half * 4:(half + 1) * 4, :]
                    i_ap = bv[cp * 64:(cp + 1) * 64, :, hh, :]
                    if half == 0:
                        P.op("act", lambda e, o_ap=o_ap, i_ap=i_ap: e.copy(out=o_ap, in_=i_ap),
                             reads=[bankB[b]], writes=[VhB[ch]])
                    else:
                        P.op("dve", lambda e, o_ap=o_ap, i_ap=i_ap: e.tensor_copy(out=o_ap, in_=i_ap),
                             reads=[bankB[b]], writes=[VhB[ch]])
            if want_out and half == 0:
                P.op("act", lambda e, b=b, half=half, sl=sl: e.copy(
                    out=stage[sl][:, half * 512:(half + 1) * 512], in_=banks[b][:, :]),
                    reads=[bankB[b]], writes=[stageB[sl]])
            elif want_out:
                P.op("dve", lambda e, b=b, half=half, sl=sl: e.tensor_copy(
                    out=stage[sl][:, half * 512:(half + 1) * 512], in_=banks[b][:, :]),
                    reads=[bankB[b]], writes=[stageB[sl]])
        if want_out:
            dst = nvp[(st - 12) * 128:(st - 11) * 128, :] if st < 16 else nvs[:, :]
            P.dma("sp", "ost%d" % sl, lambda e, dst=dst, sl=sl, rows=rows: e.dma_start(
                out=dst, in_=stage[sl][0:rows, :]), reads=[stageB[sl]])

    stop_at('KV')
    P.barrier()
    rbT = stage[0][0:16, 0:384]
    rbX = stage[0][0:16, 512:896]
    rbIn = stage[1]

    def prep_bias(j):
        P.dma("sp", "rb", lambda e: e.dma_start(out=rbIn[:, 0:16], in_=rel_bias[j, 0:128, :]), writes=[stageB[1]])
        P.dma("sp", "rb", lambda e: e.dma_start(out=rbIn[:, 128:144], in_=rel_bias[j, 128:256, :]),
              writes=[stageB[1]])
        P.dma("sp", "rb", lambda e: e.dma_start(out=rbIn[0:1, 256:272], in_=rel_bias[j, 256:257, :]),
              writes=[stageB[1]])
        b = next_bank()

        def fn(e):
            e.transpose(out=banks[b][:, 0:128], in_=rbIn[:, 0:128], identity=ident[:, :])
            e.transpose(out=banks[b][:, 128:256], in_=rbIn[:, 128:256], identity=ident[:, :])
            return e.transpose(out=banks[b][:, 256:384], in_=rbIn[:, 256:384], identity=ident[:, :])
        P.op("pe", fn, reads=[stageB[1], constB], writes=[bankB[b]])
        P.op("dve", lambda e: e.tensor_copy(out=rbT[:, 0:257], in_=banks[b][0:16, 0:257]),
             reads=[bankB[b]], writes=[stageB[0]])
        P.op("dve", lambda e: e.tensor_copy(out=rbT[:, 257:384],
                                            in_=rbT[:, 256:257].to_broadcast([16, 127])),
             reads=[stageB[0]], writes=[stageB[0]])
        P.op("dve", lambda e: e.tensor_scalar(out=rbX, in0=rbT, scalar1=rbT[:, 256:257], scalar2=None,
                                              op0=ALU.subtract),
             reads=[stageB[0]], writes=[stageB[0]])
        P.dma("sp", "rbo", lambda e: e.dma_start(out=rbx_d[j], in_=rbX), reads=[stageB[0]], writes=[rbxB])
        P.dma("sp", "rbo", lambda e: e.dma_start(out=cst_d[j].rearrange("(h o) -> h o", o=1),
                                                 in_=rbT[:, 256:257]), reads=[stageB[0]], writes=[rbxB])

    rbxB = Buf("rbx")
    prep_bias(0)
    prep_bias(1)

    def key_list(tt):
        ents = []
        if tt < 4:
            for kc in range(max(0, 8 * tt - 8), 8 * tt + 8):
                qa = max(kc, 8 * tt) - 8 * tt
                qb = min(kc + 8, 8 * tt + 7) - 8 * tt + 1
                ents.append(("n", kc, qa, qb, 8 * tt + qa - kc))
        else:
            for jc in range(8):
                ents.append(("c", jc, 0, 1, 8 - jc))
            ents.append(("n", 32, 0, 1, 0))
        return ents

    wqzB = [Buf("wqz0"), Buf("wqz1")]
    woB = [Buf("wo%d" % i) for i in range(4)]
    held = set()
    norm_excl[0] = held
    steps = [(j, tt, i) for j in range(2) for tt in range(5) for i in range(8)]
    NSTEP = len(steps) if _STOP[0] is None or not str(_STOP[0]).startswith('N') else int(_STOP[0][1:])
    hb_of = {}
    _hb = 0
    for j in range(2):
        for tt in range(5):
            hb_of[(j, tt)] = _hb
            _hb ^= 1
    st8 = {}
    ez = stage[1][:, 512:1024]
    ezB = Buf("ez")

    def b_loads(n):
        j, tt, i = steps[n]
        s = n % 2
        s3 = n % 4
        P.dma("pool", "wb%d" % s, lambda e: e.dma_start(
            out=wB[s]["wq"], in_=wqz[j, i].rearrange("p (k n) -> p k n", n=128)), writes=[wqzB[s]])
        P.dma("pool", "wb%d" % s, lambda e: e.dma_start(
            out=wB[s]["wz"], in_=wqz[j, 8 + i].rearrange("p (k n) -> p k n", n=128)), writes=[wqzB[s]])
        P.dma("pool", "wo%d" % s3, lambda e: e.dma_start(
            out=wo_v[s3], in_=w_out_b[j, i * 128:(i + 1) * 128, :]), writes=[woB[s3]])
        for hh in range(2):
            src = bass.AP(tensor=rbx_d.tensor, offset=(j * 16 + 2 * i + hh) * 384 + 65, ap=[[1, 64], [1, 192]])
            P.dma("pool", "bt%d" % s, lambda e, src=src, hh=hh: e.dma_start(
                out=Bt[s][hh * 64:(hh + 1) * 64, :], in_=src), reads=[rbxB], writes=[BtB[s]])
            csrc = bass.AP(tensor=cst_d.tensor, offset=j * 16 + 2 * i + hh, ap=[[0, 64], [1, 1]])
            P.dma("sp", "cv%d" % s, lambda e, csrc=csrc, hh=hh: e.dma_start(
                out=small[hh * 64:(hh + 1) * 64, s:s + 1], in_=csrc), reads=[rbxB], writes=[cvecB[s]])
        return

    def b_cache(n):
        j, tt, i = steps[n]
        s = n % 2
        if tt == 4:
            P.dma("sp", "ck", lambda e: e.dma_start(
                out=stage[1][:, 0:512].rearrange("p (t c) -> p t c", c=128),
                in_=ck[:, i * 128:(i + 1) * 128].rearrange("(t p) c -> p t c", p=128)), writes=[stageB[1]])
            bt = next_bank(exclude=held)

            def fnT(e):
                last = None
                for t4 in range(4):
                    last = e.transpose(out=banks[bt][:, t4 * 128:(t4 + 1) * 128],
                                       in_=stage[1][:, t4 * 128:(t4 + 1) * 128], identity=ident[:, :])
                return last
            P.op("pe", fnT, reads=[stageB[1], constB], writes=[bankB[bt]])
            P.op("dve", lambda e: e.tensor_copy(out=cKT[s], in_=banks[bt][:, :]), reads=[bankB[bt]],
                 writes=[cKTB[s]])
            for hh in range(2):
                h = 2 * i + hh
                P.dma("pool", "cvh%d" % s, lambda e, hh=hh, h=h: e.dma_start(
                    out=cVh[s][hh * 64:(hh + 1) * 64, :, :],
                    in_=cv[:, h * 64:(h + 1) * 64].rearrange("(jc k) d -> k jc d", k=64)), writes=[cVhB[s]])

    qc = stage[0][:, 0:512]
    ssc = stage[0][:, 512:1024]
    qcB, sscB = Buf("qc"), Buf("ssc")

    def b_proj(n):
        j, tt, i = steps[n]
        c0, w = TILES[tt]
        s = n % 2
        hbuf = hb_of[(j, tt)]
        bq = next_bank(exclude=held)

        def fnq(e):
            last = None
            for k in range(8):
                last = e.matmul(banks[bq][:, 0:w], lhsT=wB[s]["wq"][:, k, :], rhs=hTt[hbuf][:, k, 0:w],
                                start=(k == 0), stop=(k == 7))
            return last
        P.op("pe", fnq, reads=[hTtB[hbuf], wqzB[s]], writes=[bankB[bq]])
        P.op("dve", lambda e: e.tensor_copy(out=qc[:, 0:w], in_=banks[bq][:, 0:w]),
             reads=[bankB[bq]], writes=[qcB, stageB[0]])
        P.op("dve", lambda e: e.tensor_tensor(out=sqq[:, 0:w], in0=banks[bq][:, 0:w], in1=qc[:, 0:w], op=ALU.mult),
             reads=[bankB[bq], qcB], writes=[sqqB])
        bz = next_bank(exclude=held)

        def fnz(e):
            last = None
            for k in range(8):
                last = e.matmul(banks[bz][:, 0:w], lhsT=wB[s]["wz"][:, k, :], rhs=hTt[hbuf][:, k, 0:w],
                                start=(k == 0), stop=(k == 7))
            return last
        P.op("pe", fnz, reads=[hTtB[hbuf], wqzB[s]], writes=[bankB[bz]])
        P.op("dve", lambda e: e.tensor_copy(out=szi[s][:, 0:w], in_=banks[bz][:, 0:w]),
             reads=[bankB[bz]], writes=[sziB[s]])

    def b_ss(n):
        j, tt, i = steps[n]
        c0, w = TILES[tt]
        b2 = next_bank(exclude=held)
        P.op("pe", lambda e: e.matmul(banks[b2][:, 0:w], lhsT=onesbd, rhs=sqq[:, 0:w], start=True, stop=True),
             reads=[sqqB, constB], writes=[bankB[b2]])
        P.op("dve", lambda e: e.tensor_copy(out=ssc[:, 0:w], in_=banks[b2][:, 0:w]),
             reads=[bankB[b2]], writes=[sscB, stageB[0]])

    def b_qnorm_act(n):
        j, tt, i = steps[n]
        c0, w = TILES[tt]
        P.op("act", lambda e: e.activation(out=lnv[:, 0:w], in_=ssc[:, 0:w], func=AF.Ln,
                                           bias=EPS, scale=1.0 / 64.0),
             reads=[sscB], writes=[lnvB])
        P.op("act", lambda e: e.activation(out=rstd[:, 0:w], in_=lnv[:, 0:w], func=AF.Exp, scale=-0.5),
             reads=[lnvB], writes=[rstdB])

    def b_qnorm_dve(n):
        j, tt, i = steps[n]
        c0, w = TILES[tt]
        s = n % 2
        gcol = V_GQS + j
        P.op("dve", lambda e: e.scalar_tensor_tensor(
            out=QTi[s][:, 0:w], in0=qc[:, 0:w], scalar=vecs[:, gcol:gcol + 1], in1=rstd[:, 0:w],
            op0=ALU.mult, op1=ALU.mult),
            reads=[qcB, rstdB, constB, smallB], writes=[QTiB[s]])

    def b_ez(n):
        j, tt, i = steps[n]
        c0, w = TILES[tt]
        s = n % 2
        P.op("act", lambda e: e.activation(out=ez[:, 0:w], in_=szi[s][:, 0:w], func=AF.Exp, scale=-1.0),
             reads=[sziB[s]], writes=[ezB, stageB[1]])

    def pack_groups(ents):
        rem = sorted(range(len(ents)), key=lambda e: -(ents[e][3] - ents[e][2]))
        groups = []
        while rem:
            g = [rem.pop(0)]
            tot = (ents[g[0]][3] - ents[g[0]][2]) * 64
            k = 0
            while k < len(rem):
                nn = (ents[rem[k]][3] - ents[rem[k]][2]) * 64
                if tot + nn <= 512:
                    g.append(rem.pop(k))
                    tot += nn
                else:
                    k += 1
            groups.append(g)
        return groups

    def b_sloop(n):
        j, tt, i = steps[n]
        c0, w = TILES[tt]
        s = n % 2
        ents = key_list(tt)
        groups = pack_groups(ents)
        info = {}

        def alloc():
            bo = next_bank(exclude=held)
            held.add(bo)
            bd = next_bank(exclude=held)
            held.add(bd)
            st8[n] = dict(bo=bo, bd=bd)

        def emit_S(gi):
            bs = next_bank(exclude=held)
            held.add(bs)
            offs = []
            off = 0
            rd = [QTiB[s], BtB[s], constB]
            mm = []
            for e_ in groups[gi]:
                kind, idx, qa, qb, d0 = ents[e_]
                q0, q1 = qa * 64, qb * 64
                nq = q1 - q0
                nb = max(0, min(3 - d0, qb - qa)) * 64
                if kind == "n":
                    kcol = idx * 64
                    k_lo, k_hi = KT[0:64, i, kcol:kcol + 64], KT[64:128, i, kcol:kcol + 64]
                    rd.append(KTB[4] if idx == 32 else KTB[idx // 8])
                else:
                    k_lo, k_hi = cKT[s][0:64, idx * 64:(idx + 1) * 64], cKT[s][64:128, idx * 64:(idx + 1) * 64]
                    rd.append(cKTB[s])
                mm.append((off, q0, q1, nb, d0, k_lo, k_hi))
                offs.append(off)
                off += nq
            info[gi] = (bs, offs, off)

            def fns(e):
                last = None
                for (o_, q0, q1, nb, d0, k_lo, k_hi) in mm:
                    nq = q1 - q0
                    e.matmul(banks[bs][0:64, o_:o_ + nq], lhsT=k_lo, rhs=QTi[s][0:64, q0:q1], start=True,
                             stop=(nb == 0), skip_group_check=True)
                    last = e.matmul(banks[bs][64:128, o_:o_ + nq], lhsT=k_hi, rhs=QTi[s][64:128, q0:q1],
                                    start=True, stop=(nb == 0), skip_group_check=True)
                    if nb > 0:
                        e.matmul(banks[bs][0:64, o_:o_ + nb], lhsT=jbd[0:64, 0:64],
                                 rhs=Bt[s][0:64, d0 * 64:d0 * 64 + nb], start=False, stop=True,
                                 skip_group_check=True)
                        last = e.matmul(banks[bs][64:128, o_:o_ + nb], lhsT=jbd[64:128, 64:128],
                                        rhs=Bt[s][64:128, d0 * 64:d0 * 64 + nb], start=False, stop=True,
                                        skip_group_check=True)
                return last
            P.op("pe", fns, reads=rd, writes=[bankB[bs]])

        def emit_PV(gi):
            bs, offs, tot = info[gi]
            bo, bd = st8[n]["bo"], st8[n]["bd"]
            pb = gi % 3
            P.op("act", lambda e: e.activation(
                out=Pb[pb][:, 0:tot], in_=banks[bs][:, 0:tot], func=AF.Exp, bias=small[:, s:s + 1], scale=1.0),
                reads=[bankB[bs], cvecB[s]], writes=[PbB[pb]])
            held.discard(bs)
            rd = [PbB[pb], constB]
            mm = []
            for e_, o_ in zip(groups[gi], offs):
                kind, idx, qa, qb, d0 = ents[e_]
                q0, q1 = qa * 64, qb * 64
                if kind == "n":
                    v_lo, v_hi = Vh[0:64, idx, i, :], Vh[64:128, idx, i, :]
                    rd.append(VhB[idx])
                else:
                    v_lo, v_hi = cVh[s][0:64, idx, :], cVh[s][64:128, idx, :]
                    rd.append(cVhB[s])
                mm.append((o_, q0, q1, v_lo, v_hi))

            def fno(e):
                last = None
                for k_, (o_, q0, q1, v_lo, v_hi) in enumerate(mm):
                    nq = q1 - q0
                    st_ = (gi == 0 and k_ == 0)
                    sp_ = (gi == len(groups) - 1 and k_ == len(mm) - 1)
                    e.matmul(banks[bo][0:64, q0:q1], lhsT=v_lo, rhs=Pb[pb][0:64, o_:o_ + nq], start=st_, stop=sp_,
                             skip_group_check=True)
                    e.matmul(banks[bo][64:128, q0:q1], lhsT=v_hi, rhs=Pb[pb][64:128, o_:o_ + nq], start=st_,
                             stop=sp_, skip_group_check=True)
                    e.matmul(banks[bd][0:64, q0:q1], lhsT=onesbd[0:64, 0:64], rhs=Pb[pb][0:64, o_:o_ + nq],
                             start=st_, stop=sp_, skip_group_check=True)
                    last = e.matmul(banks[bd][64:128, q0:q1], lhsT=onesbd[64:128, 64:128],
                                    rhs=Pb[pb][64:128, o_:o_ + nq], start=st_, stop=sp_, skip_group_check=True)
                return last
            P.op("pe", fno, reads=rd, writes=[bankB[bo], bankB[bd]])

        AHEAD = 2
        th = [alloc]
        for gi in range(min(AHEAD, len(groups))):
            th.append(lambda gi=gi: emit_S(gi))
        for gi in range(len(groups)):
            if gi + AHEAD < len(groups):
                th.append(lambda gi=gi: emit_S(gi + AHEAD))
            th.append(lambda gi=gi: emit_PV(gi))
        return th

    def b_gate(n):
        j, tt, i = steps[n]
        c0, w = TILES[tt]
        s = n % 2
        P.op("act", lambda e: e.activation(out=ez[:, 0:w], in_=szi[s][:, 0:w], func=AF.Exp, scale=-1.0),
             reads=[sziB[s]], writes=[ezB, stageB[1]])

    def b_tail_D(n):
        j, tt, i = steps[n]
        c0, w = TILES[tt]
        bd = st8[n]["bd"]
        P.op("dve", lambda e: e.scalar_tensor_tensor(
            out=rden[:, 0:w], in0=ez[:, 0:w], scalar=1.0, in1=banks[bd][:, 0:w], op0=ALU.add, op1=ALU.mult),
            reads=[ezB, bankB[bd]], writes=[rdenB])

    def b_tail_t0(n):
        j, tt, i = steps[n]
        c0, w = TILES[tt]
        s = n % 2
        bo = st8[n]["bo"]
        P.op("dve", lambda e: e.tensor_tensor(out=ez[:, 0:w], in0=banks[bo][:, 0:w], in1=szi[s][:, 0:w],
                                              op=ALU.mult),
             reads=[bankB[bo], sziB[s]], writes=[ezB, stageB[1]])

    def b_tail_rd(n):
        j, tt, i = steps[n]
        c0, w = TILES[tt]
        P.op("act", lambda e: e.activation(out=rden[:, 0:w], in_=rden[:, 0:w], func=AF.Ln),
             reads=[rdenB], writes=[rdenB])
        P.op("act", lambda e: e.activation(out=rden[:, 0:w], in_=rden[:, 0:w], func=AF.Exp, scale=-1.0),
             reads=[rdenB], writes=[rdenB])

    def b_tail_y(n):
        j, tt, i = steps[n]
        c0, w = TILES[tt]
        y4 = n % 4
        P.op("dve", lambda e: e.tensor_tensor(out=yg[y4][:, 0:w], in0=ez[:, 0:w], in1=rden[:, 0:w],
                                              op=ALU.mult),
             reads=[rdenB, ezB], writes=[ygB[y4]])

    def b_outproj_group(g):
        prs = [p for p in (2 * g, 2 * g + 1) if p < NSTEP]
        j, tt, _ = steps[prs[0]]
        c0, w = TILES[tt]
        for oc in range(8):
            b = next_bank(exclude=held)

            def fn(e, b=b, oc=oc):
                last = None
                for ki, p in enumerate(prs):
                    last = e.matmul(banks[b][:, 0:w], lhsT=wo_v[p % 4][:, oc * 128:(oc + 1) * 128],
                                    rhs=yg[p % 4][:, 0:w], start=(ki == 0), stop=(ki == len(prs) - 1))
                return last
            P.op("pe", fn, reads=[ygB[p % 4] for p in prs] + [woB[p % 4] for p in prs], writes=[bankB[b]])
            P.op("dve", lambda e, b=b, oc=oc: e.tensor_tensor(
                out=xT[:, oc, c0:c0 + w], in0=banks[b][:, 0:w], in1=xT[:, oc, c0:c0 + w], op=ALU.add),
                reads=[bankB[b], xTB[tt][oc]], writes=[xTB[tt][oc]])

    def b_norm(j, tt):
        c0, w = TILES[tt]
        hbuf = hb_of[(j, tt)]
        norm_tile(tt, V_NB + j * 8, lambda c: hTt[hbuf][:, c, 0:w], hTtB[hbuf], sq2, sqB, nch=2)

    b_norm(0, 0)
    b_loads(0)
    b_proj(0)
    b_ss(0)
    b_qnorm_act(0)
    b_qnorm_dve(0)
    ngroups = (NSTEP + 1) // 2
    next_g = 0
    late_release = []
    for n in range(NSTEP):
        j, tt, i = steps[n]
        while next_g < ngroups and 2 * next_g + 3 <= n:
            b_outproj_group(next_g)
            next_g += 1
        if n >= 1:
            b_tail_y(n - 1)
        if n + 1 < NSTEP:
            b_loads(n + 1)
            b_cache(n + 1)
            b_proj(n + 1)
        for b_ in late_release:
            held.discard(b_)
        late_release = []
        if i == 5 and n + 3 < NSTEP:
            jn, ttn, _ = steps[n + 3]
            b_norm(jn, ttn)
        th = b_sloop(n)
        cut = min(len(th), 1 + 2 + 2 * 2)
        for t_ in th[:cut]:
            t_()
        b_gate(n)
        for t_ in th[cut:]:
            t_()
        b_tail_D(n)
        if n + 1 < NSTEP:
            b_ss(n + 1)
        b_tail_t0(n)
        if n + 1 < NSTEP:
            b_qnorm_act(n + 1)
            b_qnorm_dve(n + 1)
        b_tail_rd(n)
        late_release = [st8[n]["bo"], st8[n]["bd"]]
    b_tail_y(NSTEP - 1)
    for b_ in late_release:
        held.discard(b_)
    while next_g < ngroups:
        b_outproj_group(next_g)
        next_g += 1
    stop_at('B')
    P.barrier()
    fst = [stage[0], stage[1]] + [arena[:, 2048 * k:2048 * (k + 1)].bitcast(F32) for k in range(6)]
    fstB = [stageB[0], stageB[1]] + [Buf("fst%d" % k) for k in range(6)]
    for st in range(17):
        rows = 128 if st < 16 else 64
        tt = st // 4
        sl = st % 8
        for half in range(2):
            b = next_bank()

            def fn(e, b=b, half=half, st=st, rows=rows):
                last = None
                for c4 in range(4):
                    c = half * 4 + c4
                    last = e.transpose(out=banks[b][:, c4 * 128:(c4 + 1) * 128],
                                       in_=xT[:, c, st * 128:(st + 1) * 128], identity=ident[:, :])
                return last
            P.op("pe", fn, reads=xTB[tt][half * 4:(half + 1) * 4] + [constB], writes=[bankB[b]])
            if half == 0:
                P.op("act", lambda e, b=b, sl=sl, rows=rows: e.copy(out=fst[sl][0:rows, 0:512],
                                                                   in_=banks[b][0:rows, :]),
                     reads=[bankB[b]], writes=[fstB[sl]])
            else:
                P.op("dve", lambda e, b=b, sl=sl, rows=rows: e.tensor_copy(out=fst[sl][0:rows, 512:1024],
                                                                          in_=banks[b][0:rows, :]),
                     reads=[bankB[b]], writes=[fstB[sl]])
        dst = yp[st * 128:(st + 1) * 128, :] if st < 16 else ys[:, :]
        P.dma("sp", "fst%d" % sl, lambda e, dst=dst, sl=sl, rows=rows: e.dma_start(
            out=dst, in_=fst[sl][0:rows, :]), reads=[fstB[sl]])
    P.barrier()

    emit()
    es.close()


def _consts():
    bf = ml_dtypes.bfloat16
    cb = np.zeros((128, NCB), np.float32)
    cb[:, C_ONES:C_ONES + 128] = 1.0
    cb[0:64, C_OBD:C_OBD + 64] = 1.0
    cb[64:128, C_OBD + 64:C_OBD + 128] = 1.0
    for r in range(64):
        cb[r, C_JBD + 63 - r] = 1.0
        cb[64 + r, C_JBD + 64 + 63 - r] = 1.0
    s = np.arange(128)[:, None]
    t = np.arange(144)[None, :]
    dt = t - s
    for g, wdw in enumerate(POOL_W):
        band = ((dt >= 0) & (dt <= wdw - 1)).astype(np.float32)
        dlt = (dt == 0).astype(np.float32)
        o = C_M + (g * 3) * 144
        cb[:, o:o + 144] = band / wdw - dlt
        cnt = np.minimum(t + 1, wdw).astype(np.float32)
        first = band / wdw - dlt
        first[:, :16] = (band - cnt * dlt)[:, :16]
        cb[:, o + 144:o + 288] = first
        dth = t + 15 - s
        bh = ((dth >= 1) & (dth <= wdw - 1) & (s < 15)).astype(np.float32)
        cb[:, o + 288:o + 432] = bh / wdw
    return cb.astype(bf), np.eye(128, dtype=np.float32)


_NC_CACHE = {}


def kernel(x_prompt, x_sample, state_pool, cache_k, cache_v, norm_a, w_in_a, w_grp_a, scale_a, w_out_a,
           norm_kv, w_kv, g_k, norm_b, w_in_b, g_q, rel_bias_b, w_out_b):
    f = lambda a: np.ascontiguousarray(np.asarray(a, dtype=np.float32))
    x_prompt, x_sample, state_pool, cache_k, cache_v = map(f, (x_prompt, x_sample, state_pool, cache_k, cache_v))
    norm_a, w_in_a, w_grp_a, scale_a, w_out_a = map(f, (norm_a, w_in_a, w_grp_a, scale_a, w_out_a))
    norm_kv, w_kv, g_k, norm_b, w_in_b, g_q, rel_bias_b, w_out_b = map(
        f, (norm_kv, w_kv, g_k, norm_b, w_in_b, g_q, rel_bias_b, w_out_b))
    cbf_np, ident_np = _consts()
    vecs = np.zeros((128, NV), np.float32)
    for l in range(2):
        vecs[:, V_NA + l * 8:V_NA + l * 8 + 8] = norm_a[l].reshape(8, 128).T
        vecs[:, V_SC + l * 16:V_SC + l * 16 + 16] = scale_a[l].reshape(16, 128).T
        vecs[:, V_NB + l * 8:V_NB + l * 8 + 8] = norm_b[l].reshape(8, 128).T
        vecs[:, V_GQ + l] = np.concatenate([g_q[l], g_q[l]])
    vecs[:, V_NKV:V_NKV + 8] = norm_kv.reshape(8, 128).T
    vecs[:, V_GK] = np.concatenate([g_k, g_k])
    for g, wdw in enumerate(POOL_W):
        vecs[:, V_IC + g * 16:V_IC + g * 16 + 16] = (1.0 / np.minimum(np.arange(16) + 1, wdw)).astype(np.float32)[None, :]
    gkrow = np.ascontiguousarray(np.broadcast_to(g_k[None, :], (128, 64)))
    wqz = np.ascontiguousarray(w_in_b.reshape(2, 8, 128, 16, 128).transpose(0, 3, 2, 1, 4).reshape(2, 16, 128, 1024))
    if "nc" not in _NC_CACHE:
        _NC_CACHE["nc"] = build_program()
    nc = _NC_CACHE["nc"]
    in_maps = []
    for b in range(8):
        in_maps.append({
            "xp": x_prompt[b], "xs": x_sample[b], "hist": np.ascontiguousarray(state_pool[:, b]),
            "ck": cache_k[b].reshape(512, 1024), "cv": cache_v[b].reshape(512, 1024),
            "w_in_a": w_in_a, "w_grp_a": w_grp_a, "w_out_a": w_out_a, "w_kv": w_kv, "wqz": wqz,
            "w_out_b": w_out_b, "rel_bias": rel_bias_b, "vecs": vecs, "gkrow": gkrow, "ident": ident_np,
            "cbf": cbf_np,
        })
    res = run_bass_kernel_spmd(nc, in_maps, core_ids=list(range(8)))
    r = res.results
    y_prompt = np.stack([r[b]["yp"] for b in range(8)]).astype(np.float32)
    y_sample = np.stack([r[b]["ys"] for b in range(8)]).astype(np.float32)
    new_pool_p = np.stack([r[b]["npp"] for b in range(8)], axis=1).astype(np.float32)
    new_pool_s = np.stack([r[b]["nps"] for b in range(8)], axis=1).astype(np.float32)
    new_k_p = np.stack([r[b]["nkp"] for b in range(8)]).reshape(8, 512, 16, 64).astype(np.float32)
    new_v_p = np.stack([r[b]["nvp"] for b in range(8)]).reshape(8, 512, 16, 64).astype(np.float32)
    new_k_s = np.stack([r[b]["nks"] for b in range(8)]).reshape(8, 64, 16, 64).astype(np.float32)
    new_v_s = np.stack([r[b]["nvs"] for b in range(8)]).reshape(8, 64, 16, 64).astype(np.float32)
    return (y_prompt, y_sample, new_pool_p, new_pool_s, new_k_p, new_v_p, new_k_s, new_v_s)
```

```python
import numpy as np
import ml_dtypes
from contextlib import ExitStack
import concourse.bass as bass
import concourse.mybir as mybir
from concourse.bass_utils import run_bass_kernel_spmd

F32 = mybir.dt.float32
BF16 = mybir.dt.bfloat16
AF = mybir.ActivationFunctionType
ALU = mybir.AluOpType
AX = mybir.AxisListType
EPS = 1e-6
T = 2112
TP = 2176
TILES = [(0, 512), (512, 512), (1024, 512), (1536, 512), (2048, 64)]
POOL_W = (2, 4, 8, 16)

C_ONES, C_OBD, C_JBD, C_M = 0, 128, 256, 384
NCB = 384 + 12 * 144
V_NA, V_SC, V_NKV, V_NB, V_GK, V_GQ, V_GQS, V_IC, NV = 0, 16, 48, 56, 72, 73, 76, 80, 144

ENGS = ("pe", "act", "dve", "pool", "sp")
SAME_ENGINE_WAR_WAW = True


class Buf:
    __slots__ = ("name", "w", "r", "last")

    def __init__(self, name):
        self.name = name
        self.w = None
        self.r = {}
        self.last = 0


class Prog:
    def __init__(self):
        self.q = {e: [] for e in ENGS}
        self.cnt = {e: 0 for e in ENGS}
        self.seen = {e: {} for e in ENGS}
        self.stream_tot = {}
        self.seq = 0

    def _deps(self, eng, reads, writes, stream=None):
        need = {}

        def add(tok, raw):
            if tok is None:
                return
            k, v = tok
            if k == stream:
                return
            if k == eng and (eng in ("pe", "sp") or not (raw or SAME_ENGINE_WAR_WAW)):
                return
            if need.get(k, 0) < v:
                need[k] = v

        for b in reads:
            add(b.w, True)
        for b in writes:
            add(b.w, False)
            for k, v in b.r.items():
                add((k, v), False)
        waits = []
        for k, v in need.items():
            if self.seen[eng].get(k, 0) < v:
                self.seen[eng][k] = v
                waits.append((k, v))
        return waits

    def _mark(self, tok, reads, writes):
        k, v = tok
        self.seq += 1
        for b in reads:
            b.last = self.seq
        for b in writes:
            b.last = self.seq
        for b in reads:
            if b.r.get(k, 0) < v:
                b.r[k] = v
        for b in writes:
            b.w = tok
            b.r = {}

    def op(self, eng, fn, reads=(), writes=()):
        waits = self._deps(eng, reads, writes)
        self.cnt[eng] += 1
        tok = (eng, self.cnt[eng])
        self.q[eng].append((waits, fn, eng, 1))
        self._mark(tok, reads, writes)
        return tok

    def dma(self, eng, stream, fn, reads=(), writes=(), after_tokens=()):
        waits = self._deps(eng, reads, writes, stream=stream)
        for tok in after_tokens:
            if tok is None:
                continue
            k, v = tok
            if k != eng and self.seen[eng].get(k, 0) < v:
                self.seen[eng][k] = v
                waits.append((k, v))
        self.stream_tot[stream] = self.stream_tot.get(stream, 0) + 16
        tok = (stream, self.stream_tot[stream])
        self.q[eng].append((waits, fn, stream, 16))
        self._mark(tok, reads, writes)
        return tok

    def barrier(self):
        for e in ENGS:
            waits = []
            for k in list(ENGS) + list(self.stream_tot.keys()):
                v = self.cnt[k] if k in self.cnt else self.stream_tot[k]
                if k == e or v == 0:
                    continue
                if self.seen[e].get(k, 0) < v:
                    self.seen[e][k] = v
                    waits.append((k, v))
            if waits:
                self.q[e].append((waits, None, None, 0))


class _StopBuild(Exception):
    pass


_STOP = [None]
_DBG = [None]


def build_program():
    nc = bass.Bass("TRN2", target_bir_lowering=False)
    try:
        _build_body(nc)
    except _StopBuild:
        pass
    return nc


def _build_body(nc):

    def din(name, shape, dt=F32):
        return nc.dram_tensor(name, list(shape), dt, kind="ExternalInput").ap()

    def dout(name, shape):
        return nc.dram_tensor(name, list(shape), F32, kind="ExternalOutput").ap()

    xp = din("xp", [2048, 1024])
    xs = din("xs", [64, 1024])
    hist = din("hist", [2, 15, 2048])
    ck = din("ck", [512, 1024])
    cv = din("cv", [512, 1024])
    w_in_a = din("w_in_a", [2, 1024, 4096])
    w_grp_a = din("w_grp_a", [2, 4, 512, 512])
    w_out_a = din("w_out_a", [2, 2048, 1024])
    w_kv = din("w_kv", [1024, 2048])
    wqz = din("wqz", [2, 16, 128, 1024])
    w_out_b = din("w_out_b", [2, 1024, 1024])
    rel_bias = din("rel_bias", [2, 257, 16])
    vecs_d = din("vecs", [128, NV])
    gkrow_d = din("gkrow", [128, 64])
    ident_d = din("ident", [128, 128])
    cbf_d = din("cbf", [128, NCB], BF16)

    yp = dout("yp", [2048, 1024])
    ys = dout("ys", [64, 1024])
    npp = dout("npp", [2, 15, 2048])
    nps = dout("nps", [2, 15, 2048])
    nkp = dout("nkp", [512, 1024])
    nvp = dout("nvp", [512, 1024])
    nks = dout("nks", [64, 1024])
    nvs = dout("nvs", [64, 1024])
    rbx_d = nc.dram_tensor("rbx_scr", [2, 16, 384], F32, kind="Internal").ap()
    cst_d = nc.dram_tensor("cst_scr", [2, 16], F32, kind="Internal").ap()

    P = Prog()
    es = ExitStack()

    def sb(name, shape, dt):
        return es.enter_context(nc.sbuf_tensor(name, list(shape), dt))

    def emit():
        keys = list(ENGS) + list(P.stream_tot.keys())
        sems = {k: es.enter_context(nc.semaphore("s_" + k)) for k in keys}
        block = es.enter_context(nc.Block())
        for ename, attr in (("pe", "tensor"), ("act", "scalar"), ("dve", "vector"), ("pool", "gpsimd"), ("sp", "sync")):
            ops = P.q[ename]

            def body(e, ops=ops):
                for waits, fn, inc_key, inc in ops:
                    for k, v in waits:
                        e.wait_ge(sems[k], v)
                    if fn is None:
                        continue
                    inst = fn(e)
                    inst.then_inc(sems[inc_key], inc)
            getattr(block, attr)(body)

    def stop_at(tag):
        if _STOP[0] == tag:
            P.barrier()
            emit()
            es.close()
            raise _StopBuild()

    xT = sb("xT", [128, 8, TP], F32)
    ident = sb("ident_sb", [128, 128], F32)
    cbf = sb("cbf_sb", [128, NCB], BF16)
    vecs = sb("vecs_sb", [128, NV], F32)
    gkrow = sb("gkrow_sb", [128, 64], F32)
    lnv = sb("lnv", [128, 512], F32)
    rstd = sb("rstd", [128, 512], F32)
    rden = sb("rden", [128, 512], F32)
    stage = [sb("stage0", [128, 1024], F32), sb("stage1", [128, 1024], F32)]
    small = sb("small", [128, 64], F32)
    ARENA = 61000
    arena = sb("arena", [128, ARENA], BF16)
    banks = [es.enter_context(nc.psum_tensor("bank%d" % i, [128, 512], F32)) for i in range(8)]

    ones128 = cbf[:, C_ONES:C_ONES + 128]
    onesbd = cbf[:, C_OBD:C_OBD + 128]
    jbd = cbf[:, C_JBD:C_JBD + 128]

    def mmat(g, v, c0, c1):
        o = C_M + (g * 3 + v) * 144
        return cbf[:, o + c0:o + c1]

    xTB = [[Buf("xT%d_%d" % (i, c)) for c in range(8)] for i in range(5)]
    bankB = [Buf("bank%d" % i) for i in range(8)]
    constB = Buf("const")
    stageB = [Buf("stage0"), Buf("stage1")]
    lnvB, rstdB, rdenB, smallB = Buf("lnv"), Buf("rstd"), Buf("rden"), Buf("small")
    bank_rr = [0]
    norm_excl = [()]

    def next_bank(exclude=()):
        best = None
        for b in range(8):
            if b in exclude:
                continue
            key = (bankB[b].last, b)
            if best is None or key < best[0]:
                best = (key, b)
        b = best[1]
        P.seq += 1
        bankB[b].last = P.seq
        return b

    def av(off, shape):
        n = 1
        for s in shape:
            n *= s
        v = arena[:, off:off + n]
        if len(shape) == 1:
            return v
        if len(shape) == 2:
            return v.rearrange("p (a b) -> p a b", b=shape[1])
        return v.rearrange("p (a b c) -> p a b c", b=shape[1], c=shape[2])

    O_HT = 0
    hT = av(O_HT, [8, TP])
    O_S0 = 17408
    SLOT = 14848
    O_UT = O_S0 + 2 * SLOT
    utok = av(O_UT, [8, 512])
    O_SZ = O_UT + 4096
    szA = av(O_SZ, [4, 512])
    ptA = av(O_SZ + 2048, [4, 512])
    ytA = av(O_SZ + 4096, [4, 512])
    O_SQ = O_SZ + 6144
    sq = av(O_SQ, [4, 512])
    assert O_SQ + 2048 <= ARENA

    def slot_views(s):
        o = O_S0 + s * SLOT
        return dict(wu=av(o, [8, 512]), wz=av(o + 4096, [8, 512]), wg=av(o + 8192, [4, 512]),
                    wo=av(o + 10240, [4, 1024]), hist=av(o + 14336, [512]))

    slots = [slot_views(0), slot_views(1)]
    slotB = [Buf("slot0"), Buf("slot1")]
    hTB = [Buf("hT%d" % i) for i in range(5)]
    utokB = [Buf("utok%d" % i) for i in range(8)]
    szB, ptB, ytB, sqB = Buf("sz"), Buf("pt"), Buf("yt"), Buf("sq")
    stagePoolB = Buf("stagepool")

    O_WKV = O_S0
    wkv = av(O_WKV, [8, 1024])
    O_KT = 25600
    KT = av(O_KT, [8, T])
    O_VH = O_KT + 8 * T
    Vh = av(O_VH, [33, 8, 64])
    O_SQ2 = O_VH + 8 * T
    assert O_SQ2 + 512 <= ARENA
    sq1 = av(O_SQ2, [512])
    KTB = [Buf("KT%d" % i) for i in range(5)]
    VhB = [Buf("Vh%d" % i) for i in range(33)]
    hTt = [av(0, [8, 512]), av(4096, [8, 512])]
    hTtB = [Buf("hTt0"), Buf("hTt1")]
    _off = [8192]

    def take(n):
        o = _off[0]
        _off[0] += n
        return o
    QTi = [av(take(512), [512]) for _ in range(2)]
    szi = [av(take(512), [512]) for _ in range(2)]
    yg = [av(take(512), [512]) for _ in range(4)]
    Pb = [av(take(512), [512]) for _ in range(3)]
    Bt = [av(take(192), [192]) for _ in range(2)]
    cKT = [av(take(512), [512]), av(O_SQ2, [512])]
    cVh = [av(take(512), [8, 64]), av(O_SQ2 + 512, [8, 64])]
    assert O_SQ2 + 1024 <= ARENA
    wB = [dict(wq=av(take(1024), [8, 128]), wz=av(take(1024), [8, 128])) for _ in range(2)]
    wo_v = [av(take(1024), [1024]) for _ in range(4)]
    sqq = av(take(512), [512])
    sq2 = av(take(1024), [2, 512])
    assert _off[0] <= O_KT, _off[0]
    QTiB = [Buf("QTi0"), Buf("QTi1")]
    sziB = [Buf("szi0"), Buf("szi1")]
    ygB = [Buf("yg%d" % i) for i in range(4)]
    PbB = [Buf("Pb%d" % i) for i in range(3)]
    BtB = [Buf("Bt0"), Buf("Bt1")]
    cKTB = [Buf("cKT0"), Buf("cKT1")]
    cVhB = [Buf("cVh0"), Buf("cVh1")]
    wBB = [Buf("wB0"), Buf("wB1")]
    sqqB = Buf("sqq")
    cvec = small[:, 0:2]
    cvecB = [Buf("cvec0"), Buf("cvec1")]

    P.dma("sp", "cst", lambda e: e.dma_start(out=ident[:], in_=ident_d[:, :]), writes=[constB])
    P.dma("sp", "cst", lambda e: e.dma_start(out=cbf[:], in_=cbf_d[:, :]), writes=[constB])
    P.dma("sp", "cst", lambda e: e.dma_start(out=vecs[:], in_=vecs_d[:, :]), writes=[constB])
    P.dma("sp", "cst", lambda e: e.dma_start(out=gkrow[:], in_=gkrow_d[:, :]), writes=[constB])
    for tt in range(5):
        pass
    P.op("dve", lambda e: e.memset(hT[:, :, T:TP], 0.0), writes=[hTB[4]])
    for s in range(2):
        P.op("dve", lambda e, s=s: e.memset(slots[s]["hist"], 0.0), writes=[slotB[s]])
    P.op("dve", lambda e: e.tensor_scalar(out=vecs[:, V_GQS:V_GQS + 2], in0=vecs[:, V_GQ:V_GQ + 2],
                                          scalar1=0.125, scalar2=None, op0=ALU.mult),
         reads=[constB], writes=[smallB])

    stop_at('p0a')
    def load_group(l, g, after=()):
        s = (l * 4 + g) % 2
        sv = slots[s]
        wi = w_in_a[l].rearrange("(k p) n -> p k n", p=128)
        P.dma("pool", "w%d" % s, lambda e: e.dma_start(out=sv["wu"], in_=wi[:, :, g * 512:(g + 1) * 512]),
              writes=[slotB[s]], after_tokens=[b_.w for b_ in after])
        P.dma("pool", "w%d" % s,
              lambda e: e.dma_start(out=sv["wz"], in_=wi[:, :, 2048 + g * 512:2048 + (g + 1) * 512]),
              writes=[slotB[s]])
        wgv = w_grp_a[l, g].rearrange("(k p) n -> p k n", p=128)
        P.dma("pool", "w%d" % s, lambda e: e.dma_start(out=sv["wg"], in_=wgv), writes=[slotB[s]])
        wov = w_out_a[l, g * 512:(g + 1) * 512, :].rearrange("(k p) n -> p k n", p=128)
        P.dma("pool", "w%d" % s, lambda e: e.dma_start(out=sv["wo"], in_=wov), writes=[slotB[s]])
        P.dma("pool", "w%d" % s,
              lambda e: e.dma_start(out=sv["hist"][0:15, :], in_=hist[l, :, g * 512:(g + 1) * 512]),
              writes=[slotB[s]])

    load_group(0, 0)
    stop_at('p0b')

    xst = [stage[0], stage[1]] + [arena[:, O_UT + 2048 * k:O_UT + 2048 * (k + 1)].bitcast(F32) for k in range(4)]
    xstB = [stageB[0], stageB[1]] + [Buf("xst%d" % k) for k in range(4)]
    for st in range(17):
        rows = 128 if st < 16 else 64
        slot = st % 6
        tt = st // 4
        src = xp[st * 128:(st + 1) * 128, :] if st < 16 else xs[:, :]
        P.dma("sp", "xld%d" % slot, lambda e, slot=slot, rows=rows, src=src:
              e.dma_start(out=xst[slot][0:rows, :], in_=src), writes=[xstB[slot]])
        for half in range(2):
            b = next_bank()

            def fn(e, b=b, half=half, slot=slot, rows=rows):
                last = None
                for c4 in range(4):
                    c = half * 4 + c4
                    last = e.transpose(out=banks[b][:, c4 * 128:(c4 + 1) * 128],
                                       in_=xst[slot][:, c * 128:(c + 1) * 128],
                                       identity=ident[:, :])
                return last
            P.op("pe", fn, reads=[xstB[slot], constB], writes=[bankB[b]])
            o_ap = xT[:, half * 4:(half + 1) * 4, st * 128:st * 128 + rows]
            i_ap = banks[b][:, :].rearrange("p (c t) -> p c t", t=128)[:, :, 0:rows]
            if half == 0:
                P.op("act", lambda e, o_ap=o_ap, i_ap=i_ap: e.copy(out=o_ap, in_=i_ap),
                     reads=[bankB[b]], writes=xTB[tt][half * 4:(half + 1) * 4])
            else:
                P.op("dve", lambda e, o_ap=o_ap, i_ap=i_ap: e.tensor_copy(out=o_ap, in_=i_ap),
                     reads=[bankB[b]], writes=xTB[tt][half * 4:(half + 1) * 4])

    load_group(0, 1, after=xTB[4])
    stop_at('p0')
    def norm_tile(tt, gcol, dst, dstB, sqv, sqvB, nch=4, split=False, nbuf=None, nbufB=None):
        c0, w = TILES[tt]
        b = next_bank(exclude=norm_excl[0])
        npass = 8 // nch
        lbuf, lB = (lnv, lnvB) if nbuf is None else (nbuf, nbufB)
        rbuf, rB = (rstd, rstdB) if nbuf is None else (nbuf, nbufB)
        for ps in range(npass):
            P.op("act", lambda e, ps=ps: e.activation(out=sqv[:, :, 0:w],
                                                      in_=xT[:, ps * nch:(ps + 1) * nch, c0:c0 + w],
                                                      func=AF.Square),
                 reads=xTB[tt][ps * nch:(ps + 1) * nch], writes=[sqvB])

            def fn(e, ps=ps):
                last = None
                for c4 in range(nch):
                    last = e.matmul(banks[b][:, 0:w], lhsT=ones128, rhs=sqv[:, c4, 0:w],
                                    start=(ps == 0 and c4 == 0), stop=(ps == npass - 1 and c4 == nch - 1))
                return last
            P.op("pe", fn, reads=[sqvB, constB], writes=[bankB[b]])
        P.op("act", lambda e: e.activation(out=lbuf[:, 0:w], in_=banks[b][:, 0:w], func=AF.Ln,
                                           bias=EPS, scale=1.0 / 1024.0),
             reads=[bankB[b]], writes=[lB])

        def second():
            P.op("act", lambda e: e.activation(out=rbuf[:, 0:w], in_=lbuf[:, 0:w], func=AF.Exp, scale=-0.5),
                 reads=[lB], writes=[rB])
            for c in range(8):
                P.op("dve", lambda e, c=c: e.scalar_tensor_tensor(
                    out=dst(c), in0=xT[:, c, c0:c0 + w], scalar=vecs[:, gcol + c:gcol + c + 1],
                    in1=rbuf[:, 0:w], op0=ALU.mult, op1=ALU.mult),
                    reads=[xTB[tt][c], rB, constB], writes=[dstB])
        if split:
            return second
        second()
        return None

    def normA(tt, gcol):
        c0, w = TILES[tt]
        norm_tile(tt, gcol, lambda c: hT[:, c, c0:c0 + w], hTB[tt], sq, sqB)

    def stage_UZ(l, g, tt):
        s = (l * 4 + g) % 2
        sv = slots[s]
        c0, w = TILES[tt]
        nsub = 4 if tt < 4 else 1
        for si in range(nsub):
            b = next_bank()
            ring = (4 * tt + si) % 8
            tc0 = c0 + si * 128

            def fn(e, b=b, tc0=tc0):
                last = None
                for k in range(8):
                    last = e.matmul(banks[b][:, :], lhsT=hT[:, k, tc0:tc0 + 128], rhs=sv["wu"][:, k, :],
                                    start=(k == 0), stop=(k == 7))
                return last
            P.op("pe", fn, reads=[hTB[tt], slotB[s]], writes=[bankB[b]])
            P.op("act", lambda e, b=b, ring=ring: e.copy(out=utok[:, ring, :], in_=banks[b][:, :]),
                 reads=[bankB[b]], writes=[utokB[ring]])
            if (tt == 3 and si == 3) or tt == 4:
                p0, pn = (64, 64) if tt == 3 else (32, 32)
                r0 = 113 if tt == 3 else 49
                dst = npp if tt == 3 else nps
                P.op("dve", lambda e, b=b, p0=p0, pn=pn: e.tensor_copy(out=stage[0][p0:p0 + pn, 0:512],
                                                                      in_=banks[b][p0:p0 + pn, :]),
                     reads=[bankB[b], utokB[ring]], writes=[stageB[0]])
                P.dma("sp", "npo", lambda e, r0=r0, dst=dst: e.dma_start(
                    out=dst[l, :, g * 512:(g + 1) * 512], in_=stage[0][r0:r0 + 15, 0:512]),
                    reads=[stageB[0]])
        for c in range(4):
            b = next_bank()

            def fn(e, b=b, c=c):
                last = None
                for k in range(8):
                    last = e.matmul(banks[b][:, 0:w], lhsT=sv["wz"][:, k, c * 128:(c + 1) * 128],
                                    rhs=hT[:, k, c0:c0 + w], start=(k == 0), stop=(k == 7))
                return last
            P.op("pe", fn, reads=[hTB[tt], slotB[s]], writes=[bankB[b]])
            P.op("act", lambda e, b=b, c=c: e.activation(out=szA[:, c, 0:w], in_=banks[b][:, 0:w],
                                                         func=AF.Silu),
                 reads=[bankB[b]], writes=[szB])

    def stage_POOL(l, g, tt):
        s = (l * 4 + g) % 2
        sv = slots[s]
        c0, w = TILES[tt]
        for c in range(4):
            b = next_bank()
            srcs = []
            if tt == 4:
                srcs.append((sv["hist"][:, c * 128:(c + 1) * 128], mmat(g, 2, 0, 64), 0, 64, [slotB[s]]))
                ring = (4 * tt) % 8
                srcs.append((utok[:, ring, c * 128:(c + 1) * 128], mmat(g, 0, 0, 64), 0, 64, [utokB[ring]]))
            else:
                if tt > 0:
                    ring = (4 * tt - 1) % 8
                    srcs.append((utok[:, ring, c * 128:(c + 1) * 128], mmat(g, 0, 128, 143), 0, 15,
                                 [utokB[ring]]))
                for si in range(4):
                    ring = (4 * tt + si) % 8
                    var = 1 if (tt == 0 and si == 0) else 0
                    n = 143 if si < 3 else 128
                    srcs.append((utok[:, ring, c * 128:(c + 1) * 128], mmat(g, var, 0, n), si * 128, n,
                                 [utokB[ring]]))
            rd = [constB]
            for sr in srcs:
                rd += sr[4]

            def fn(e, b=b, srcs=srcs):
                last = None
                for i, (l_ap, r_ap, oc, n, _) in enumerate(srcs):
                    last = e.matmul(banks[b][:, oc:oc + n], lhsT=l_ap, rhs=r_ap, start=(i == 0),
                                    stop=(i == len(srcs) - 1), skip_group_check=True)
                return last
            P.op("pe", fn, reads=rd, writes=[bankB[b]])
            if tt == 0:
                P.op("dve", lambda e, b=b, c=c: e.tensor_tensor(
                    out=ptA[:, c, 0:16], in0=banks[b][:, 0:16], in1=vecs[:, V_IC + g * 16:V_IC + g * 16 + 16],
                    op=ALU.mult), reads=[bankB[b], constB], writes=[ptB])
                P.op("dve", lambda e, b=b, c=c: e.tensor_copy(out=ptA[:, c, 16:w], in_=banks[b][:, 16:w]),
                     reads=[bankB[b]], writes=[ptB])
            else:
                P.op("dve", lambda e, b=b, c=c: e.tensor_copy(out=ptA[:, c, 0:w], in_=banks[b][:, 0:w]),
                     reads=[bankB[b]], writes=[ptB])

    def stage_GRP(l, g, tt):
        s = (l * 4 + g) % 2
        sv = slots[s]
        c0, w = TILES[tt]
        for co in range(4):
            b = next_bank()

            def fn(e, b=b, co=co):
                last = None
                for ci in range(4):
                    last = e.matmul(banks[b][:, 0:w], lhsT=sv["wg"][:, ci, co * 128:(co + 1) * 128],
                                    rhs=ptA[:, ci, 0:w], start=(ci == 0), stop=(ci == 3))
                return last
            P.op("pe", fn, reads=[ptB, slotB[s]], writes=[bankB[b]])
            col = V_SC + l * 16 + g * 4 + co
            P.op("dve", lambda e, b=b, co=co, col=col: e.scalar_tensor_tensor(
                out=ytA[:, co, 0:w], in0=banks[b][:, 0:w], scalar=vecs[:, col:col + 1],
                in1=szA[:, co, 0:w], op0=ALU.mult, op1=ALU.mult),
                reads=[bankB[b], szB, constB], writes=[ytB])

    def stage_OUT(l, g, tt):
        s = (l * 4 + g) % 2
        sv = slots[s]
        c0, w = TILES[tt]
        for oc in range(8):
            b = next_bank()

            def fn(e, b=b, oc=oc):
                last = None
                for ci in range(4):
                    last = e.matmul(banks[b][:, 0:w], lhsT=sv["wo"][:, ci, oc * 128:(oc + 1) * 128],
                                    rhs=ytA[:, ci, 0:w], start=(ci == 0), stop=(ci == 3))
                return last
            P.op("pe", fn, reads=[ytB, slotB[s]], writes=[bankB[b]])
            P.op("dve", lambda e, b=b, oc=oc: e.tensor_tensor(
                out=xT[:, oc, c0:c0 + w], in0=banks[b][:, 0:w], in1=xT[:, oc, c0:c0 + w], op=ALU.add),
                reads=[bankB[b], xTB[tt][oc]], writes=[xTB[tt][oc]])

    def load_wkv(part):
        wv_ = w_kv.rearrange("(k p) n -> p k n", p=128)
        P.dma("pool", "w0", lambda e: e.dma_start(out=wkv, in_=wv_[:, :, part * 1024:(part + 1) * 1024]),
              writes=[slotB[0]])

    for tt in range(5):
        normA(tt, V_NA)
    for l in range(2):
        next_gcol = V_NA + 8 if l == 0 else V_NKV
        stop_at('A0')
        prev = None
        for g in range(4):
            for tt in range(5):
                stage_UZ(l, g, tt)
                stop_at('A1')
                stage_POOL(l, g, tt)
                stop_at('A2')
                if prev is not None:
                    stage_OUT(*prev)
                    if prev[1] == 3:
                        normA(prev[2], next_gcol)
                    stop_at('A4')
                if tt == 0 and not (l == 0 and g == 0):
                    if l == 1 and g == 3:
                        load_wkv(0)
                    else:
                        ng = (l * 4 + g + 1)
                        load_group(ng // 4, ng % 4)
                stage_GRP(l, g, tt)
                stop_at('A3')
                stop_at('I%d' % (l * 20 + g * 5 + tt))
                prev = (l, g, tt)
        stage_OUT(*prev)
        normA(prev[2], next_gcol)

    stop_at('A')
    P.barrier()
    stage_rr = [0]

    def head_norm_T(b, w, gcol, dst_ap, dstBuf, sqbuf, sqbufB):
        P.op("act", lambda e: e.activation(out=sqbuf[:, 0:w], in_=banks[b][:, 0:w], func=AF.Square),
             reads=[bankB[b]], writes=[sqbufB])
        b2 = next_bank(exclude=(b,))
        P.op("pe", lambda e: e.matmul(banks[b2][:, 0:w], lhsT=onesbd, rhs=sqbuf[:, 0:w], start=True, stop=True),
             reads=[sqbufB, constB], writes=[bankB[b2]])
        P.op("act", lambda e: e.activation(out=lnv[:, 0:w], in_=banks[b2][:, 0:w], func=AF.Ln,
                                           bias=EPS, scale=1.0 / 64.0),
             reads=[bankB[b2]], writes=[lnvB])
        P.op("act", lambda e: e.activation(out=rstd[:, 0:w], in_=lnv[:, 0:w], func=AF.Exp, scale=-0.5),
             reads=[lnvB], writes=[rstdB])
        P.op("dve", lambda e: e.scalar_tensor_tensor(
            out=dst_ap, in0=banks[b][:, 0:w], scalar=vecs[:, gcol:gcol + 1], in1=rstd[:, 0:w],
            op0=ALU.mult, op1=ALU.mult),
            reads=[bankB[b], rstdB, constB, smallB], writes=[dstBuf])

    kheld = set()
    kjobs = [(tt, i) for tt in range(5) for i in range(8)]
    kbank = {}

    def k_mm(idx):
        tt, i = kjobs[idx]
        c0, w = TILES[tt]
        b = next_bank(exclude=kheld)
        kheld.add(b)
        kbank[idx] = b

        def fn(e):
            last = None
            for k in range(8):
                last = e.matmul(banks[b][:, 0:w], lhsT=wkv[:, k, i * 128:(i + 1) * 128],
                                rhs=hT[:, k, c0:c0 + w], start=(k == 0), stop=(k == 7))
            return last
        P.op("pe", fn, reads=[hTB[tt], slotB[0]], writes=[bankB[b]])
        sqb = sqk[idx % 2]
        P.op("act", lambda e: e.activation(out=sqb[:, 0:w], in_=banks[b][:, 0:w], func=AF.Square),
             reads=[bankB[b]], writes=[sqkB[idx % 2]])

    def k_norm(idx):
        tt, i = kjobs[idx]
        c0, w = TILES[tt]
        b = kbank[idx]
        sqb = sqk[idx % 2]
        b2 = next_bank(exclude=kheld)
        P.op("pe", lambda e: e.matmul(banks[b2][:, 0:w], lhsT=onesbd, rhs=sqb[:, 0:w], start=True, stop=True),
             reads=[sqkB[idx % 2], constB], writes=[bankB[b2]])
        lb = lnk[idx % 2]
        P.op("act", lambda e: e.activation(out=lb[:, 0:w], in_=banks[b2][:, 0:w], func=AF.Ln,
                                           bias=EPS, scale=1.0 / 64.0),
             reads=[bankB[b2]], writes=[lnkB[idx % 2]])
        P.op("act", lambda e: e.activation(out=lb[:, 0:w], in_=lb[:, 0:w], func=AF.Exp, scale=-0.5),
             reads=[lnkB[idx % 2]], writes=[lnkB[idx % 2]])
        P.op("dve", lambda e: e.scalar_tensor_tensor(
            out=KT[:, i, c0:c0 + w], in0=banks[b][:, 0:w], scalar=vecs[:, V_GK:V_GK + 1], in1=lb[:, 0:w],
            op0=ALU.mult, op1=ALU.mult),
            reads=[bankB[b], lnkB[idx % 2], constB], writes=[KTB[tt]])
        kheld.discard(b)

    sqk = [sq1, av(O_SQ2 + 512, [512])]
    assert O_SQ2 + 1024 <= ARENA
    sqkB = [Buf("sqk0"), Buf("sqk1")]
    lnk = [lnv, rstd]
    lnkB = [lnvB, rstdB]
    k_mm(0)
    for idx in range(len(kjobs)):
        if idx + 1 < len(kjobs):
            k_mm(idx + 1)
        k_norm(idx)

    def k_tokmajor(st):
        tt = st // 4
        tc0 = st * 128
        rows = 128 if st < 16 else 64
        sl = stage_rr[0] % 2
        stage_rr[0] += 1
        bb = []
        for half in range(2):
            b = next_bank(exclude=tuple(bb))
            bb.append(b)

            def fn(e, b=b, half=half):
                last = None
                for k in range(8):
                    last = e.matmul(banks[b][:, :], lhsT=hT[:, k, tc0:tc0 + 128],
                                    rhs=wkv[:, k, half * 512:(half + 1) * 512], start=(k == 0), stop=(k == 7))
                return last
            P.op("pe", fn, reads=[hTB[tt], slotB[0]], writes=[bankB[b]])
            P.op("act", lambda e, b=b, half=half: e.activation(
                out=stage[sl][:, half * 512:(half + 1) * 512], in_=banks[b][:, :], func=AF.Square),
                reads=[bankB[b]], writes=[stageB[sl]])
        P.op("dve", lambda e: e.tensor_reduce(out=small[:, 16:32],
                                              in_=stage[sl][:, :].rearrange("p (h d) -> p h d", d=64),
                                              axis=AX.X, op=ALU.add),
             reads=[stageB[sl]], writes=[smallB])
        P.op("act", lambda e: e.activation(out=small[:, 32:48], in_=small[:, 16:32], func=AF.Ln,
                                           bias=EPS, scale=1.0 / 64.0), reads=[smallB], writes=[smallB])
        P.op("act", lambda e: e.activation(out=small[:, 48:64], in_=small[:, 32:48], func=AF.Exp, scale=-0.5),
             reads=[smallB], writes=[smallB])
        for half in range(2):
            b = bb[half]
            o_ap = stage[sl][:, half * 512:(half + 1) * 512].rearrange("p (h d) -> p h d", d=64)
            P.op("dve", lambda e, b=b, half=half, o_ap=o_ap: e.tensor_tensor(
                out=o_ap, in0=banks[b][:, :].rearrange("p (h d) -> p h d", d=64),
                in1=small[:, 48 + half * 8:48 + half * 8 + 8].unsqueeze(2).broadcast_to([128, 8, 64]),
                op=ALU.mult), reads=[bankB[b], smallB], writes=[stageB[sl]])
            P.op("dve", lambda e, o_ap=o_ap: e.tensor_tensor(
                out=o_ap, in0=o_ap, in1=gkrow[:, :].unsqueeze(1).broadcast_to([128, 8, 64]), op=ALU.mult),
                reads=[constB, stageB[sl]], writes=[stageB[sl]])
        dst = nkp[(st - 12) * 128:(st - 11) * 128, :] if st < 16 else nks[:, :]
        P.dma("sp", "ost%d" % sl, lambda e: e.dma_start(out=dst, in_=stage[sl][0:rows, :]),
              reads=[stageB[sl]])

    for st in (12, 13, 14, 15, 16):
        k_tokmajor(st)

    load_wkv(1)
    for st in range(17):
        tt = st // 4
        tc0 = st * 128
        rows = 128 if st < 16 else 64
        want_out = st >= 12
        if want_out:
            sl = stage_rr[0] % 2
            stage_rr[0] += 1
        for half in range(2):
            b = next_bank()

            def fn(e, b=b, half=half, tc0=tc0):
                last = None
                for k in range(8):
                    last = e.matmul(banks[b][:, :], lhsT=hT[:, k, tc0:tc0 + 128],
                                    rhs=wkv[:, k, half * 512:(half + 1) * 512], start=(k == 0), stop=(k == 7))
                return last
            P.op("pe", fn, reads=[hTB[tt], slotB[0]], writes=[bankB[b]])
            bv = banks[b][:, :].rearrange("p (i h d) -> p i h d", h=2, d=64)
            ncp = 2 if st < 16 else 1
            for cp in range(ncp):
                for hh in range(2):
                    ch = 2 * st + cp
                    o_ap = Vh[hh * 64:(hh + 1) * 64, ch, half * 4:(half + 1) * 4, :]
                    i_ap = bv[cp * 64:(cp + 1) * 64, :, hh, :]
                    if half == 0:
                        P.op("act", lambda e, o_ap=o_ap, i_ap=i_ap: e.copy(out=o_ap, in_=i_ap),
                             reads=[bankB[b]], writes=[VhB[ch]])
                    else:
                        P.op("dve", lambda e, o_ap=o_ap, i_ap=i_ap: e.tensor_copy(out=o_ap, in_=i_ap),
                             reads=[bankB[b]], writes=[VhB[ch]])
            if want_out and half == 0:
                P.op("act", lambda e, b=b, half=half, sl=sl: e.copy(
                    out=stage[sl][:, half * 512:(half + 1) * 512], in_=banks[b][:, :]),
                    reads=[bankB[b]], writes=[stageB[sl]])
            elif want_out:
                P.op("dve", lambda e, b=b, half=half, sl=sl: e.tensor_copy(
                    out=stage[sl][:, half * 512:(half + 1) * 512], in_=banks[b][:, :]),
                    reads=[bankB[b]], writes=[stageB[sl]])
        if want_out:
            dst = nvp[(st - 12) * 128:(st - 11) * 128, :] if st < 16 else nvs[:, :]
            P.dma("sp", "ost%d" % sl, lambda e, dst=dst, sl=sl, rows=rows: e.dma_start(
                out=dst, in_=stage[sl][0:rows, :]), reads=[stageB[sl]])

    stop_at('KV')
    P.barrier()
    rbT = stage[0][0:16, 0:384]
    rbX = stage[0][0:16, 512:896]
    rbIn = stage[1]

    def prep_bias(j):
        P.dma("sp", "rb", lambda e: e.dma_start(out=rbIn[:, 0:16], in_=rel_bias[j, 0:128, :]), writes=[stageB[1]])
        P.dma("sp", "rb", lambda e: e.dma_start(out=rbIn[:, 128:144], in_=rel_bias[j, 128:256, :]),
              writes=[stageB[1]])
        P.dma("sp", "rb", lambda e: e.dma_start(out=rbIn[0:1, 256:272], in_=rel_bias[j, 256:257, :]),
              writes=[stageB[1]])
        b = next_bank()

        def fn(e):
            e.transpose(out=banks[b][:, 0:128], in_=rbIn[:, 0:128], identity=ident[:, :])
            e.transpose(out=banks[b][:, 128:256], in_=rbIn[:, 128:256], identity=ident[:, :])
            return e.transpose(out=banks[b][:, 256:384], in_=rbIn[:, 256:384], identity=ident[:, :])
        P.op("pe", fn, reads=[stageB[1], constB], writes=[bankB[b]])
        P.op("dve", lambda e: e.tensor_copy(out=rbT[:, 0:257], in_=banks[b][0:16, 0:257]),
             reads=[bankB[b]], writes=[stageB[0]])
        P.op("dve", lambda e: e.tensor_copy(out=rbT[:, 257:384],
                                            in_=rbT[:, 256:257].to_broadcast([16, 127])),
             reads=[stageB[0]], writes=[stageB[0]])
        P.op("dve", lambda e: e.tensor_scalar(out=rbX, in0=rbT, scalar1=rbT[:, 256:257], scalar2=None,
                                              op0=ALU.subtract),
             reads=[stageB[0]], writes=[stageB[0]])
        P.dma("sp", "rbo", lambda e: e.dma_start(out=rbx_d[j], in_=rbX), reads=[stageB[0]], writes=[rbxB])
        P.dma("sp", "rbo", lambda e: e.dma_start(out=cst_d[j].rearrange("(h o) -> h o", o=1),
                                                 in_=rbT[:, 256:257]), reads=[stageB[0]], writes=[rbxB])

    rbxB = Buf("rbx")
    prep_bias(0)
    prep_bias(1)
    P.barrier()

    def key_list(tt):
        ents = []
        if tt < 4:
            for kc in range(max(0, 8 * tt - 8), 8 * tt + 8):
                qa = max(kc, 8 * tt) - 8 * tt
                qb = min(kc + 8, 8 * tt + 7) - 8 * tt + 1
                ents.append(("n", kc, qa, qb, 8 * tt + qa - kc))
        else:
            for jc in range(8):
                ents.append(("c", jc, 0, 1, 8 - jc))
            ents.append(("n", 32, 0, 1, 0))
        return ents

    wqzB = [Buf("wqz0"), Buf("wqz1")]
    woB = [Buf("wo%d" % i) for i in range(4)]
    held = set()
    norm_excl[0] = held
    steps = [(j, tt, i) for j in range(2) for tt in range(5) for i in range(8)]
    NSTEP = len(steps) if _STOP[0] is None or not str(_STOP[0]).startswith('N') else int(_STOP[0][1:])
    hb_of = {}
    _hb = 0
    for j in range(2):
        for tt in range(5):
            hb_of[(j, tt)] = _hb
            _hb ^= 1
    st8 = {}
    ez = stage[1][:, 512:1024]
    nrmB = Buf("stage1_lo")
    ezB = Buf("ez")

    def b_loads(n):
        j, tt, i = steps[n]
        s = n % 2
        s3 = n % 4
        P.dma("pool", "wb%d" % s, lambda e: e.dma_start(
            out=wB[s]["wq"], in_=wqz[j, i].rearrange("p (k n) -> p k n", n=128)), writes=[wqzB[s]])
        P.dma("pool", "wb%d" % s, lambda e: e.dma_start(
            out=wB[s]["wz"], in_=wqz[j, 8 + i].rearrange("p (k n) -> p k n", n=128)), writes=[wqzB[s]])
        P.dma("pool", "wo%d" % s3, lambda e: e.dma_start(
            out=wo_v[s3], in_=w_out_b[j, i * 128:(i + 1) * 128, :]), writes=[woB[s3]])
        for hh in range(2):
            src = bass.AP(tensor=rbx_d.tensor, offset=(j * 16 + 2 * i + hh) * 384 + 65, ap=[[1, 64], [1, 192]])
            P.dma("pool", "bt%d" % s, lambda e, src=src, hh=hh: e.dma_start(
                out=Bt[s][hh * 64:(hh + 1) * 64, :], in_=src), reads=[rbxB], writes=[BtB[s]])
            csrc = bass.AP(tensor=cst_d.tensor, offset=j * 16 + 2 * i + hh, ap=[[0, 64], [1, 1]])
            P.dma("sp", "cv%d" % s, lambda e, csrc=csrc, hh=hh: e.dma_start(
                out=small[hh * 64:(hh + 1) * 64, s:s + 1], in_=csrc), reads=[rbxB], writes=[cvecB[s]])
        return

    def b_cache(n):
        j, tt, i = steps[n]
        s = n % 2
        if tt == 4:
            P.dma("sp", "ck", lambda e: e.dma_start(
                out=stage[1][:, 0:512].rearrange("p (t c) -> p t c", c=128),
                in_=ck[:, i * 128:(i + 1) * 128].rearrange("(t p) c -> p t c", p=128)), writes=[nrmB])
            bt = next_bank(exclude=held)

            def fnT(e):
                last = None
                for t4 in range(4):
                    last = e.transpose(out=banks[bt][:, t4 * 128:(t4 + 1) * 128],
                                       in_=stage[1][:, t4 * 128:(t4 + 1) * 128], identity=ident[:, :])
                return last
            P.op("pe", fnT, reads=[nrmB, constB], writes=[bankB[bt]])
            P.op("dve", lambda e: e.tensor_copy(out=cKT[s], in_=banks[bt][:, :]), reads=[bankB[bt]],
                 writes=[cKTB[s]])
            for hh in range(2):
                h = 2 * i + hh
                P.dma("pool", "cvh%d" % s, lambda e, hh=hh, h=h: e.dma_start(
                    out=cVh[s][hh * 64:(hh + 1) * 64, :, :],
                    in_=cv[:, h * 64:(h + 1) * 64].rearrange("(jc k) d -> k jc d", k=64)), writes=[cVhB[s]])

    qc = stage[0][:, 0:512]
    ssc = stage[0][:, 512:1024]
    qcB, sscB = Buf("qc"), Buf("ssc")

    def b_proj(n):
        j, tt, i = steps[n]
        c0, w = TILES[tt]
        s = n % 2
        hbuf = hb_of[(j, tt)]
        bq = next_bank(exclude=held)

        def fnq(e):
            last = None
            for k in range(8):
                last = e.matmul(banks[bq][:, 0:w], lhsT=wB[s]["wq"][:, k, :], rhs=hTt[hbuf][:, k, 0:w],
                                start=(k == 0), stop=(k == 7))
            return last
        P.op("pe", fnq, reads=[hTtB[hbuf], wqzB[s]], writes=[bankB[bq]])
        P.op("dve", lambda e: e.tensor_copy(out=qc[:, 0:w], in_=banks[bq][:, 0:w]),
             reads=[bankB[bq]], writes=[qcB, stageB[0]])
        P.op("dve", lambda e: e.tensor_tensor(out=sqq[:, 0:w], in0=banks[bq][:, 0:w], in1=qc[:, 0:w], op=ALU.mult),
             reads=[bankB[bq], qcB], writes=[sqqB])
        bz = next_bank(exclude=held)

        def fnz(e):
            last = None
            for k in range(8):
                last = e.matmul(banks[bz][:, 0:w], lhsT=wB[s]["wz"][:, k, :], rhs=hTt[hbuf][:, k, 0:w],
                                start=(k == 0), stop=(k == 7))
            return last
        P.op("pe", fnz, reads=[hTtB[hbuf], wqzB[s]], writes=[bankB[bz]])
        P.op("dve", lambda e: e.tensor_copy(out=szi[s][:, 0:w], in_=banks[bz][:, 0:w]),
             reads=[bankB[bz]], writes=[sziB[s]])

    def b_ss(n):
        j, tt, i = steps[n]
        c0, w = TILES[tt]
        b2 = next_bank(exclude=held)
        P.op("pe", lambda e: e.matmul(banks[b2][:, 0:w], lhsT=onesbd, rhs=sqq[:, 0:w], start=True, stop=True),
             reads=[sqqB, constB], writes=[bankB[b2]])
        P.op("dve", lambda e: e.tensor_copy(out=ssc[:, 0:w], in_=banks[b2][:, 0:w]),
             reads=[bankB[b2]], writes=[sscB, stageB[0]])

    def b_qnorm_act(n):
        j, tt, i = steps[n]
        c0, w = TILES[tt]
        P.op("act", lambda e: e.activation(out=lnv[:, 0:w], in_=ssc[:, 0:w], func=AF.Ln,
                                           bias=EPS, scale=1.0 / 64.0),
             reads=[sscB], writes=[lnvB])
        P.op("act", lambda e: e.activation(out=rstd[:, 0:w], in_=lnv[:, 0:w], func=AF.Exp, scale=-0.5),
             reads=[lnvB], writes=[rstdB])

    def b_qnorm_dve(n):
        j, tt, i = steps[n]
        c0, w = TILES[tt]
        s = n % 2
        gcol = V_GQS + j
        P.op("dve", lambda e: e.scalar_tensor_tensor(
            out=QTi[s][:, 0:w], in0=qc[:, 0:w], scalar=vecs[:, gcol:gcol + 1], in1=rstd[:, 0:w],
            op0=ALU.mult, op1=ALU.mult),
            reads=[qcB, rstdB, constB, smallB], writes=[QTiB[s]])

    def b_ez(n):
        j, tt, i = steps[n]
        c0, w = TILES[tt]
        s = n % 2
        P.op("act", lambda e: e.activation(out=ez[:, 0:w], in_=szi[s][:, 0:w], func=AF.Exp, scale=-1.0),
             reads=[sziB[s]], writes=[ezB])

    def pack_groups(ents):
        rem = sorted(range(len(ents)), key=lambda e: -(ents[e][3] - ents[e][2]))
        groups = []
        while rem:
            g = [rem.pop(0)]
            tot = (ents[g[0]][3] - ents[g[0]][2]) * 64
            k = 0
            while k < len(rem):
                nn = (ents[rem[k]][3] - ents[rem[k]][2]) * 64
                if tot + nn <= 512:
                    g.append(rem.pop(k))
                    tot += nn
                else:
                    k += 1
            groups.append(g)
        return groups

    def b_sloop(n):
        j, tt, i = steps[n]
        c0, w = TILES[tt]
        s = n % 2
        ents = key_list(tt)
        groups = pack_groups(ents)
        info = {}

        def alloc():
            bo = next_bank(exclude=held)
            held.add(bo)
            bd = next_bank(exclude=held)
            held.add(bd)
            st8[n] = dict(bo=bo, bd=bd)

        def emit_S(gi):
            bs = next_bank(exclude=held)
            held.add(bs)
            offs = []
            off = 0
            rd = [QTiB[s], BtB[s], constB]
            mm = []
            for e_ in groups[gi]:
                kind, idx, qa, qb, d0 = ents[e_]
                q0, q1 = qa * 64, qb * 64
                nq = q1 - q0
                nb = max(0, min(3 - d0, qb - qa)) * 64
                if kind == "n":
                    kcol = idx * 64
                    k_lo, k_hi = KT[0:64, i, kcol:kcol + 64], KT[64:128, i, kcol:kcol + 64]
                    rd.append(KTB[4] if idx == 32 else KTB[idx // 8])
                else:
                    k_lo, k_hi = cKT[s][0:64, idx * 64:(idx + 1) * 64], cKT[s][64:128, idx * 64:(idx + 1) * 64]
                    rd.append(cKTB[s])
                mm.append((off, q0, q1, nb, d0, k_lo, k_hi))
                offs.append(off)
                off += nq
            info[gi] = (bs, offs, off)

            def fns(e):
                last = None
                for (o_, q0, q1, nb, d0, k_lo, k_hi) in mm:
                    nq = q1 - q0
                    e.matmul(banks[bs][0:64, o_:o_ + nq], lhsT=k_lo, rhs=QTi[s][0:64, q0:q1], start=True,
                             stop=(nb == 0), skip_group_check=True)
                    last = e.matmul(banks[bs][64:128, o_:o_ + nq], lhsT=k_hi, rhs=QTi[s][64:128, q0:q1],
                                    start=True, stop=(nb == 0), skip_group_check=True)
                    if nb > 0:
                        e.matmul(banks[bs][0:64, o_:o_ + nb], lhsT=jbd[0:64, 0:64],
                                 rhs=Bt[s][0:64, d0 * 64:d0 * 64 + nb], start=False, stop=True,
                                 skip_group_check=True)
                        last = e.matmul(banks[bs][64:128, o_:o_ + nb], lhsT=jbd[64:128, 64:128],
                                        rhs=Bt[s][64:128, d0 * 64:d0 * 64 + nb], start=False, stop=True,
                                        skip_group_check=True)
                return last
            P.op("pe", fns, reads=rd, writes=[bankB[bs]])

        def emit_PV(gi):
            bs, offs, tot = info[gi]
            bo, bd = st8[n]["bo"], st8[n]["bd"]
            pb = gi % 3
            P.op("act", lambda e: e.activation(
                out=Pb[pb][:, 0:tot], in_=banks[bs][:, 0:tot], func=AF.Exp, bias=small[:, s:s + 1], scale=1.0),
                reads=[bankB[bs], cvecB[s]], writes=[PbB[pb]])
            held.discard(bs)
            rd = [PbB[pb], constB]
            mm = []
            for e_, o_ in zip(groups[gi], offs):
                kind, idx, qa, qb, d0 = ents[e_]
                q0, q1 = qa * 64, qb * 64
                if kind == "n":
                    v_lo, v_hi = Vh[0:64, idx, i, :], Vh[64:128, idx, i, :]
                    rd.append(VhB[idx])
                else:
                    v_lo, v_hi = cVh[s][0:64, idx, :], cVh[s][64:128, idx, :]
                    rd.append(cVhB[s])
                mm.append((o_, q0, q1, v_lo, v_hi))

            def fno(e):
                last = None
                for k_, (o_, q0, q1, v_lo, v_hi) in enumerate(mm):
                    nq = q1 - q0
                    st_ = (gi == 0 and k_ == 0)
                    sp_ = (gi == len(groups) - 1 and k_ == len(mm) - 1)
                    e.matmul(banks[bo][0:64, q0:q1], lhsT=v_lo, rhs=Pb[pb][0:64, o_:o_ + nq], start=st_, stop=sp_,
                             skip_group_check=True)
                    e.matmul(banks[bo][64:128, q0:q1], lhsT=v_hi, rhs=Pb[pb][64:128, o_:o_ + nq], start=st_,
                             stop=sp_, skip_group_check=True)
                    e.matmul(banks[bd][0:64, q0:q1], lhsT=onesbd[0:64, 0:64], rhs=Pb[pb][0:64, o_:o_ + nq],
                             start=st_, stop=sp_, skip_group_check=True)
                    last = e.matmul(banks[bd][64:128, q0:q1], lhsT=onesbd[64:128, 64:128],
                                    rhs=Pb[pb][64:128, o_:o_ + nq], start=st_, stop=sp_, skip_group_check=True)
                return last
            P.op("pe", fno, reads=rd, writes=[bankB[bo], bankB[bd]])

        AHEAD = 2
        th = [alloc]
        for gi in range(min(AHEAD, len(groups))):
            th.append(lambda gi=gi: emit_S(gi))
        for gi in range(len(groups)):
            if gi + AHEAD < len(groups):
                th.append(lambda gi=gi: emit_S(gi + AHEAD))
            th.append(lambda gi=gi: emit_PV(gi))
        return th

    def b_gate(n):
        j, tt, i = steps[n]
        c0, w = TILES[tt]
        s = n % 2
        P.op("act", lambda e: e.activation(out=ez[:, 0:w], in_=szi[s][:, 0:w], func=AF.Exp, scale=-1.0),
             reads=[sziB[s]], writes=[ezB])

    def b_tail_D(n):
        j, tt, i = steps[n]
        c0, w = TILES[tt]
        bd = st8[n]["bd"]
        P.op("dve", lambda e: e.scalar_tensor_tensor(
            out=rden[:, 0:w], in0=ez[:, 0:w], scalar=1.0, in1=banks[bd][:, 0:w], op0=ALU.add, op1=ALU.mult),
            reads=[ezB, bankB[bd]], writes=[rdenB])

    def b_tail_t0(n):
        j, tt, i = steps[n]
        c0, w = TILES[tt]
        s = n % 2
        bo = st8[n]["bo"]
        P.op("dve", lambda e: e.tensor_tensor(out=ez[:, 0:w], in0=banks[bo][:, 0:w], in1=szi[s][:, 0:w],
                                              op=ALU.mult),
             reads=[bankB[bo], sziB[s]], writes=[ezB])

    def b_tail_rd(n):
        j, tt, i = steps[n]
        c0, w = TILES[tt]
        P.op("act", lambda e: e.activation(out=rden[:, 0:w], in_=rden[:, 0:w], func=AF.Ln),
             reads=[rdenB], writes=[rdenB])
        P.op("act", lambda e: e.activation(out=rden[:, 0:w], in_=rden[:, 0:w], func=AF.Exp, scale=-1.0),
             reads=[rdenB], writes=[rdenB])

    def b_tail_y(n):
        j, tt, i = steps[n]
        c0, w = TILES[tt]
        y4 = n % 4
        P.op("dve", lambda e: e.tensor_tensor(out=yg[y4][:, 0:w], in0=ez[:, 0:w], in1=rden[:, 0:w],
                                              op=ALU.mult),
             reads=[rdenB, ezB], writes=[ygB[y4]])

    def b_outproj_group(g):
        prs = [p for p in (2 * g, 2 * g + 1) if p < NSTEP]
        j, tt, _ = steps[prs[0]]
        c0, w = TILES[tt]
        for oc in range(8):
            b = next_bank(exclude=held)

            def fn(e, b=b, oc=oc):
                last = None
                for ki, p in enumerate(prs):
                    last = e.matmul(banks[b][:, 0:w], lhsT=wo_v[p % 4][:, oc * 128:(oc + 1) * 128],
                                    rhs=yg[p % 4][:, 0:w], start=(ki == 0), stop=(ki == len(prs) - 1))
                return last
            P.op("pe", fn, reads=[ygB[p % 4] for p in prs] + [woB[p % 4] for p in prs], writes=[bankB[b]])
            P.op("dve", lambda e, b=b, oc=oc: e.tensor_tensor(
                out=xT[:, oc, c0:c0 + w], in0=banks[b][:, 0:w], in1=xT[:, oc, c0:c0 + w], op=ALU.add),
                reads=[bankB[b], xTB[tt][oc]], writes=[xTB[tt][oc]])

    def b_norm(j, tt, split=False):
        c0, w = TILES[tt]
        hbuf = hb_of[(j, tt)]
        return norm_tile(tt, V_NB + j * 8, lambda c: hTt[hbuf][:, c, 0:w], hTtB[hbuf], sq2, sqB, nch=2,
                         split=split, nbuf=stage[1][:, 0:512], nbufB=nrmB)

    b_norm(0, 0)
    b_loads(0)
    b_proj(0)
    b_ss(0)
    b_qnorm_act(0)
    b_qnorm_dve(0)
    ngroups = (NSTEP + 1) // 2
    next_g = 0
    late_release = []
    for n in range(NSTEP):
        j, tt, i = steps[n]
        while next_g < ngroups and 2 * next_g + 3 <= n:
            b_outproj_group(next_g)
            next_g += 1
        if n >= 1:
            b_tail_y(n - 1)
        if n + 1 < NSTEP:
            b_loads(n + 1)
            b_cache(n + 1)
            b_proj(n + 1)
        for b_ in late_release:
            held.discard(b_)
        late_release = []
        norm2 = None
        if i == 5 and n + 3 < NSTEP:
            jn, ttn, _ = steps[n + 3]
            norm2 = b_norm(jn, ttn, split=True)
        th = b_sloop(n)
        cut = min(len(th), 1 + 2 + 2 * 2)
        for t_ in th[:cut]:
            t_()
        b_gate(n)
        if norm2 is not None:
            norm2()
        for t_ in th[cut:]:
            t_()
        b_tail_D(n)
        if n + 1 < NSTEP:
            b_ss(n + 1)
        b_tail_t0(n)
        if n + 1 < NSTEP:
            b_qnorm_act(n + 1)
            b_qnorm_dve(n + 1)
        b_tail_rd(n)
        late_release = [st8[n]["bo"], st8[n]["bd"]]
    b_tail_y(NSTEP - 1)
    for b_ in late_release:
        held.discard(b_)
    while next_g < ngroups:
        b_outproj_group(next_g)
        next_g += 1
    stop_at('B')
    P.barrier()
    fst = [stage[0], stage[1]] + [arena[:, 2048 * k:2048 * (k + 1)].bitcast(F32) for k in range(6)]
    fstB = [stageB[0], stageB[1]] + [Buf("fst%d" % k) for k in range(6)]
    for st in range(17):
        rows = 128 if st < 16 else 64
        tt = st // 4
        sl = st % 8
        for half in range(2):
            b = next_bank()

            def fn(e, b=b, half=half, st=st, rows=rows):
                last = None
                for c4 in range(4):
                    c = half * 4 + c4
                    last = e.transpose(out=banks[b][:, c4 * 128:(c4 + 1) * 128],
                                       in_=xT[:, c, st * 128:(st + 1) * 128], identity=ident[:, :])
                return last
            P.op("pe", fn, reads=xTB[tt][half * 4:(half + 1) * 4] + [constB], writes=[bankB[b]])
            if half == 0:
                P.op("act", lambda e, b=b, sl=sl, rows=rows: e.copy(out=fst[sl][0:rows, 0:512],
                                                                   in_=banks[b][0:rows, :]),
                     reads=[bankB[b]], writes=[fstB[sl]])
            else:
                P.op("dve", lambda e, b=b, sl=sl, rows=rows: e.tensor_copy(out=fst[sl][0:rows, 512:1024],
                                                                          in_=banks[b][0:rows, :]),
                     reads=[bankB[b]], writes=[fstB[sl]])
        dst = yp[st * 128:(st + 1) * 128, :] if st < 16 else ys[:, :]
        P.dma("sp", "fst%d" % sl, lambda e, dst=dst, sl=sl, rows=rows: e.dma_start(
            out=dst, in_=fst[sl][0:rows, :]), reads=[fstB[sl]])
    P.barrier()

    emit()
    es.close()


def _consts():
    bf = ml_dtypes.bfloat16
    cb = np.zeros((128, NCB), np.float32)
    cb[:, C_ONES:C_ONES + 128] = 1.0
    cb[0:64, C_OBD:C_OBD + 64] = 1.0
    cb[64:128, C_OBD + 64:C_OBD + 128] = 1.0
    for r in range(64):
        cb[r, C_JBD + 63 - r] = 1.0
        cb[64 + r, C_JBD + 64 + 63 - r] = 1.0
    s = np.arange(128)[:, None]
    t = np.arange(144)[None, :]
    dt = t - s
    for g, wdw in enumerate(POOL_W):
        band = ((dt >= 0) & (dt <= wdw - 1)).astype(np.float32)
        dlt = (dt == 0).astype(np.float32)
        o = C_M + (g * 3) * 144
        cb[:, o:o + 144] = band / wdw - dlt
        cnt = np.minimum(t + 1, wdw).astype(np.float32)
        first = band / wdw - dlt
        first[:, :16] = (band - cnt * dlt)[:, :16]
        cb[:, o + 144:o + 288] = first
        dth = t + 15 - s
        bh = ((dth >= 1) & (dth <= wdw - 1) & (s < 15)).astype(np.float32)
        cb[:, o + 288:o + 432] = bh / wdw
    return cb.astype(bf), np.eye(128, dtype=np.float32)


_NC_CACHE = {}


def kernel(x_prompt, x_sample, state_pool, cache_k, cache_v, norm_a, w_in_a, w_grp_a, scale_a, w_out_a,
           norm_kv, w_kv, g_k, norm_b, w_in_b, g_q, rel_bias_b, w_out_b):
    f = lambda a: np.ascontiguousarray(np.asarray(a, dtype=np.float32))
    x_prompt, x_sample, state_pool, cache_k, cache_v = map(f, (x_prompt, x_sample, state_pool, cache_k, cache_v))
    norm_a, w_in_a, w_grp_a, scale_a, w_out_a = map(f, (norm_a, w_in_a, w_grp_a, scale_a, w_out_a))
    norm_kv, w_kv, g_k, norm_b, w_in_b, g_q, rel_bias_b, w_out_b = map(
        f, (norm_kv, w_kv, g_k, norm_b, w_in_b, g_q, rel_bias_b, w_out_b))
    cbf_np, ident_np = _consts()
    vecs = np.zeros((128, NV), np.float32)
    for l in range(2):
        vecs[:, V_NA + l * 8:V_NA + l * 8 + 8] = norm_a[l].reshape(8, 128).T
        vecs[:, V_SC + l * 16:V_SC + l * 16 + 16] = scale_a[l].reshape(16, 128).T
        vecs[:, V_NB + l * 8:V_NB + l * 8 + 8] = norm_b[l].reshape(8, 128).T
        vecs[:, V_GQ + l] = np.concatenate([g_q[l], g_q[l]])
    vecs[:, V_NKV:V_NKV + 8] = norm_kv.reshape(8, 128).T
    vecs[:, V_GK] = np.concatenate([g_k, g_k])
    for g, wdw in enumerate(POOL_W):
        vecs[:, V_IC + g * 16:V_IC + g * 16 + 16] = (1.0 / np.minimum(np.arange(16) + 1, wdw)).astype(np.float32)[None, :]
    gkrow = np.ascontiguousarray(np.broadcast_to(g_k[None, :], (128, 64)))
    wqz = np.ascontiguousarray(w_in_b.reshape(2, 8, 128, 16, 128).transpose(0, 3, 2, 1, 4).reshape(2, 16, 128, 1024))
    if "nc" not in _NC_CACHE:
        _NC_CACHE["nc"] = build_program()
    nc = _NC_CACHE["nc"]
    in_maps = []
    for b in range(8):
        in_maps.append({
            "xp": x_prompt[b], "xs": x_sample[b], "hist": np.ascontiguousarray(state_pool[:, b]),
            "ck": cache_k[b].reshape(512, 1024), "cv": cache_v[b].reshape(512, 1024),
            "w_in_a": w_in_a, "w_grp_a": w_grp_a, "w_out_a": w_out_a, "w_kv": w_kv, "wqz": wqz,
            "w_out_b": w_out_b, "rel_bias": rel_bias_b, "vecs": vecs, "gkrow": gkrow, "ident": ident_np,
            "cbf": cbf_np,
        })
    res = run_bass_kernel_spmd(nc, in_maps, core_ids=list(range(8)))
    r = res.results
    y_prompt = np.stack([r[b]["yp"] for b in range(8)]).astype(np.float32)
    y_sample = np.stack([r[b]["ys"] for b in range(8)]).astype(np.float32)
    new_pool_p = np.stack([r[b]["npp"] for b in range(8)], axis=1).astype(np.float32)
    new_pool_s = np.stack([r[b]["nps"] for b in range(8)], axis=1).astype(np.float32)
    new_k_p = np.stack([r[b]["nkp"] for b in range(8)]).reshape(8, 512, 16, 64).astype(np.float32)
    new_v_p = np.stack([r[b]["nvp"] for b in range(8)]).reshape(8, 512, 16, 64).astype(np.float32)
    new_k_s = np.stack([r[b]["nks"] for b in range(8)]).reshape(8, 64, 16, 64).astype(np.float32)
    new_v_s = np.stack([r[b]["nvs"] for b in range(8)]).reshape(8, 64, 16, 64).astype(np.float32)
    return (y_prompt, y_sample, new_pool_p, new_pool_s, new_k_p, new_v_p, new_k_s, new_v_s)
```

```python
import numpy as np
import ml_dtypes
from contextlib import ExitStack
import concourse.bass as bass
import concourse.mybir as mybir
from concourse.bass_utils import run_bass_kernel_spmd

F32 = mybir.dt.float32
BF16 = mybir.dt.bfloat16
AF = mybir.ActivationFunctionType
ALU = mybir.AluOpType
AX = mybir.AxisListType
EPS = 1e-6
T = 2112
TP = 2176
TILES = [(0, 512), (512, 512), (1024, 512), (1536, 512), (2048, 64)]
POOL_W = (2, 4, 8, 16)

C_ONES, C_OBD, C_JBD, C_M = 0, 128, 256, 384
NCB = 384 + 12 * 144
V_NA, V_SC, V_NKV, V_NB, V_GK, V_GQ, V_GQS, V_IC, NV = 0, 16, 48, 56, 72, 73, 76, 80, 144

ENGS = ("pe", "act", "dve", "pool", "sp")
SAME_ENGINE_WAR_WAW = True


class Buf:
    __slots__ = ("name", "w", "r", "last")

    def __init__(self, name):
        self.name = name
        self.w = None
        self.r = {}
        self.last = 0


class Prog:
    def __init__(self):
        self.q = {e: [] for e in ENGS}
        self.cnt = {e: 0 for e in ENGS}
        self.seen = {e: {} for e in ENGS}
        self.stream_tot = {}
        self.seq = 0

    def _deps(self, eng, reads, writes, stream=None):
        need = {}

        def add(tok, raw):
            if tok is None:
                return
            k, v = tok
            if k == stream:
                return
            if k == eng and (eng in ("pe", "sp") or not (raw or SAME_ENGINE_WAR_WAW)):
                return
            if need.get(k, 0) < v:
                need[k] = v

        for b in reads:
            add(b.w, True)
        for b in writes:
            add(b.w, False)
            for k, v in b.r.items():
                add((k, v), False)
        waits = []
        for k, v in need.items():
            if self.seen[eng].get(k, 0) < v:
                self.seen[eng][k] = v
                waits.append((k, v))
        return waits

    def _mark(self, tok, reads, writes):
        k, v = tok
        self.seq += 1
        for b in reads:
            b.last = self.seq
        for b in writes:
            b.last = self.seq
        for b in reads:
            if b.r.get(k, 0) < v:
                b.r[k] = v
        for b in writes:
            b.w = tok
            b.r = {}

    def op(self, eng, fn, reads=(), writes=()):
        waits = self._deps(eng, reads, writes)
        self.cnt[eng] += 1
        tok = (eng, self.cnt[eng])
        self.q[eng].append((waits, fn, eng, 1))
        self._mark(tok, reads, writes)
        return tok

    def dma(self, eng, stream, fn, reads=(), writes=(), after_tokens=()):
        waits = self._deps(eng, reads, writes, stream=stream)
        for tok in after_tokens:
            if tok is None:
                continue
            k, v = tok
            if k != eng and self.seen[eng].get(k, 0) < v:
                self.seen[eng][k] = v
                waits.append((k, v))
        self.stream_tot[stream] = self.stream_tot.get(stream, 0) + 16
        tok = (stream, self.stream_tot[stream])
        self.q[eng].append((waits, fn, stream, 16))
        self._mark(tok, reads, writes)
        return tok

    def barrier(self):
        for e in ENGS:
            waits = []
            for k in list(ENGS) + list(self.stream_tot.keys()):
                v = self.cnt[k] if k in self.cnt else self.stream_tot[k]
                if k == e or v == 0:
                    continue
                if self.seen[e].get(k, 0) < v:
                    self.seen[e][k] = v
                    waits.append((k, v))
            if waits:
                self.q[e].append((waits, None, None, 0))


class _StopBuild(Exception):
    pass


_STOP = [None]
_DBG = [None]


def build_program():
    nc = bass.Bass("TRN2", target_bir_lowering=False)
    try:
        _build_body(nc)
    except _StopBuild:
        pass
    return nc


def _build_body(nc):

    def din(name, shape, dt=F32):
        return nc.dram_tensor(name, list(shape), dt, kind="ExternalInput").ap()

    def dout(name, shape):
        return nc.dram_tensor(name, list(shape), F32, kind="ExternalOutput").ap()

    xp = din("xp", [2048, 1024])
    xs = din("xs", [64, 1024])
    hist = din("hist", [2, 15, 2048])
    ck = din("ck", [512, 1024])
    cv = din("cv", [512, 1024])
    w_in_a = din("w_in_a", [2, 1024, 4096])
    w_grp_a = din("w_grp_a", [2, 4, 512, 512])
    w_out_a = din("w_out_a", [2, 2048, 1024])
    w_kv = din("w_kv", [1024, 2048])
    wqz = din("wqz", [2, 16, 128, 1024])
    w_out_b = din("w_out_b", [2, 1024, 1024])
    rel_bias = din("rel_bias", [2, 257, 16])
    vecs_d = din("vecs", [128, NV])
    gkrow_d = din("gkrow", [128, 64])
    ident_d = din("ident", [128, 128])
    cbf_d = din("cbf", [128, NCB], BF16)

    yp = dout("yp", [2048, 1024])
    ys = dout("ys", [64, 1024])
    npp = dout("npp", [2, 15, 2048])
    nps = dout("nps", [2, 15, 2048])
    nkp = dout("nkp", [512, 1024])
    nvp = dout("nvp", [512, 1024])
    nks = dout("nks", [64, 1024])
    nvs = dout("nvs", [64, 1024])
    rbx_d = nc.dram_tensor("rbx_scr", [2, 16, 384], F32, kind="Internal").ap()
    cst_d = nc.dram_tensor("cst_scr", [2, 16], F32, kind="Internal").ap()

    P = Prog()
    es = ExitStack()

    def sb(name, shape, dt):
        return es.enter_context(nc.sbuf_tensor(name, list(shape), dt))

    def emit():
        keys = list(ENGS) + list(P.stream_tot.keys())
        sems = {k: es.enter_context(nc.semaphore("s_" + k)) for k in keys}
        block = es.enter_context(nc.Block())
        for ename, attr in (("pe", "tensor"), ("act", "scalar"), ("dve", "vector"), ("pool", "gpsimd"), ("sp", "sync")):
            ops = P.q[ename]

            def body(e, ops=ops):
                for waits, fn, inc_key, inc in ops:
                    for k, v in waits:
                        e.wait_ge(sems[k], v)
                    if fn is None:
                        continue
                    inst = fn(e)
                    inst.then_inc(sems[inc_key], inc)
            getattr(block, attr)(body)

    def stop_at(tag):
        if _STOP[0] == tag:
            P.barrier()
            emit()
            es.close()
            raise _StopBuild()

    xT = sb("xT", [128, 8, TP], F32)
    ident = sb("ident_sb", [128, 128], F32)
    cbf = sb("cbf_sb", [128, NCB], BF16)
    vecs = sb("vecs_sb", [128, NV], F32)
    gkrow = sb("gkrow_sb", [128, 64], F32)
    lnv = sb("lnv", [128, 512], F32)
    rstd = sb("rstd", [128, 512], F32)
    rden = sb("rden", [128, 512], F32)
    stage = [sb("stage0", [128, 1024], F32), sb("stage1", [128, 1024], F32)]
    small = sb("small", [128, 64], F32)
    ARENA = 61000
    arena = sb("arena", [128, ARENA], BF16)
    banks = [es.enter_context(nc.psum_tensor("bank%d" % i, [128, 512], F32)) for i in range(8)]

    ones128 = cbf[:, C_ONES:C_ONES + 128]
    onesbd = cbf[:, C_OBD:C_OBD + 128]
    jbd = cbf[:, C_JBD:C_JBD + 128]

    def mmat(g, v, c0, c1):
        o = C_M + (g * 3 + v) * 144
        return cbf[:, o + c0:o + c1]

    xTB = [[Buf("xT%d_%d" % (i, c)) for c in range(8)] for i in range(5)]
    bankB = [Buf("bank%d" % i) for i in range(8)]
    constB = Buf("const")
    stageB = [Buf("stage0"), Buf("stage1")]
    lnvB, rstdB, rdenB, smallB = Buf("lnv"), Buf("rstd"), Buf("rden"), Buf("small")
    bank_rr = [0]
    norm_excl = [()]

    def next_bank(exclude=()):
        best = None
        for b in range(8):
            if b in exclude:
                continue
            key = (bankB[b].last, b)
            if best is None or key < best[0]:
                best = (key, b)
        b = best[1]
        P.seq += 1
        bankB[b].last = P.seq
        return b

    def av(off, shape):
        n = 1
        for s in shape:
            n *= s
        v = arena[:, off:off + n]
        if len(shape) == 1:
            return v
        if len(shape) == 2:
            return v.rearrange("p (a b) -> p a b", b=shape[1])
        return v.rearrange("p (a b c) -> p a b c", b=shape[1], c=shape[2])

    O_HT = 0
    hT = av(O_HT, [8, TP])
    O_S0 = 17408
    SLOT = 14848
    O_UT = O_S0 + 2 * SLOT
    utok = av(O_UT, [8, 512])
    O_SZ = O_UT + 4096
    szA = av(O_SZ, [4, 512])
    ptA = av(O_SZ + 2048, [4, 512])
    ytA = av(O_SZ + 4096, [4, 512])
    O_SQ = O_SZ + 6144
    sq = av(O_SQ, [4, 512])
    assert O_SQ + 2048 <= ARENA

    def slot_views(s):
        o = O_S0 + s * SLOT
        return dict(wu=av(o, [8, 512]), wz=av(o + 4096, [8, 512]), wg=av(o + 8192, [4, 512]),
                    wo=av(o + 10240, [4, 1024]), hist=av(o + 14336, [512]))

    slots = [slot_views(0), slot_views(1)]
    slotB = [Buf("slot0"), Buf("slot1")]
    hTB = [Buf("hT%d" % i) for i in range(5)]
    utokB = [Buf("utok%d" % i) for i in range(8)]
    szB = [Buf("sz%d" % c) for c in range(4)]
    ptB = [Buf("pt%d" % c) for c in range(4)]
    ytB = [Buf("yt%d" % c) for c in range(4)]
    sqB = Buf("sq")
    stagePoolB = Buf("stagepool")

    O_WKV = O_S0
    wkv = av(O_WKV, [8, 1024])
    O_KT = 25600
    KT = av(O_KT, [8, T])
    O_VH = O_KT + 8 * T
    Vh = av(O_VH, [33, 8, 64])
    O_SQ2 = O_VH + 8 * T
    assert O_SQ2 + 512 <= ARENA
    sq1 = av(O_SQ2, [512])
    KTB = [Buf("KT%d" % i) for i in range(5)]
    VhB = [Buf("Vh%d" % i) for i in range(33)]
    hTt = [av(0, [8, 512]), av(4096, [8, 512])]
    hTtB = [Buf("hTt0"), Buf("hTt1")]
    _off = [8192]

    def take(n):
        o = _off[0]
        _off[0] += n
        return o
    QTi = [av(take(512), [512]) for _ in range(2)]
    szi = [av(take(512), [512]) for _ in range(2)]
    yg = [av(take(512), [512]) for _ in range(4)]
    Pb = [av(take(512), [512]) for _ in range(3)]
    Bt = [av(take(192), [192]) for _ in range(2)]
    cKT = [av(take(512), [512]), av(O_SQ2, [512])]
    cVh = [av(take(512), [8, 64]), av(O_SQ2 + 512, [8, 64])]
    assert O_SQ2 + 1024 <= ARENA
    wB = [dict(wq=av(take(1024), [8, 128]), wz=av(take(1024), [8, 128])) for _ in range(2)]
    wo_v = [av(take(1024), [1024]) for _ in range(4)]
    sqq = av(take(512), [512])
    sq2 = av(take(1024), [2, 512])
    assert _off[0] <= O_KT, _off[0]
    QTiB = [Buf("QTi0"), Buf("QTi1")]
    sziB = [Buf("szi0"), Buf("szi1")]
    ygB = [Buf("yg%d" % i) for i in range(4)]
    PbB = [Buf("Pb%d" % i) for i in range(3)]
    BtB = [Buf("Bt0"), Buf("Bt1")]
    cKTB = [Buf("cKT0"), Buf("cKT1")]
    cVhB = [Buf("cVh0"), Buf("cVh1")]
    wBB = [Buf("wB0"), Buf("wB1")]
    sqqB = Buf("sqq")
    cvec = small[:, 0:2]
    cvecB = [Buf("cvec0"), Buf("cvec1")]

    P.dma("sp", "cst", lambda e: e.dma_start(out=ident[:], in_=ident_d[:, :]), writes=[constB])
    P.dma("sp", "cst", lambda e: e.dma_start(out=cbf[:], in_=cbf_d[:, :]), writes=[constB])
    P.dma("sp", "cst", lambda e: e.dma_start(out=vecs[:], in_=vecs_d[:, :]), writes=[constB])
    P.dma("sp", "cst", lambda e: e.dma_start(out=gkrow[:], in_=gkrow_d[:, :]), writes=[constB])
    for tt in range(5):
        pass
    P.op("dve", lambda e: e.memset(hT[:, :, T:TP], 0.0), writes=[hTB[4]])
    for s in range(2):
        P.op("dve", lambda e, s=s: e.memset(slots[s]["hist"], 0.0), writes=[slotB[s]])
    P.op("dve", lambda e: e.tensor_scalar(out=vecs[:, V_GQS:V_GQS + 2], in0=vecs[:, V_GQ:V_GQ + 2],
                                          scalar1=0.125, scalar2=None, op0=ALU.mult),
         reads=[constB], writes=[smallB])

    stop_at('p0a')
    def load_group(l, g, after=()):
        s = (l * 4 + g) % 2
        sv = slots[s]
        wi = w_in_a[l].rearrange("(k p) n -> p k n", p=128)
        P.dma("pool", "w%d" % s, lambda e: e.dma_start(out=sv["wu"], in_=wi[:, :, g * 512:(g + 1) * 512]),
              writes=[slotB[s]], after_tokens=[b_.w for b_ in after])
        P.dma("pool", "w%d" % s,
              lambda e: e.dma_start(out=sv["wz"], in_=wi[:, :, 2048 + g * 512:2048 + (g + 1) * 512]),
              writes=[slotB[s]])
        wgv = w_grp_a[l, g].rearrange("(k p) n -> p k n", p=128)
        P.dma("pool", "w%d" % s, lambda e: e.dma_start(out=sv["wg"], in_=wgv), writes=[slotB[s]])
        wov = w_out_a[l, g * 512:(g + 1) * 512, :].rearrange("(k p) n -> p k n", p=128)
        P.dma("pool", "w%d" % s, lambda e: e.dma_start(out=sv["wo"], in_=wov), writes=[slotB[s]])
        P.dma("pool", "w%d" % s,
              lambda e: e.dma_start(out=sv["hist"][0:15, :], in_=hist[l, :, g * 512:(g + 1) * 512]),
              writes=[slotB[s]])

    load_group(0, 0)
    stop_at('p0b')

    xst = [stage[0], stage[1]] + [arena[:, O_UT + 2048 * k:O_UT + 2048 * (k + 1)].bitcast(F32) for k in range(4)]
    xstB = [stageB[0], stageB[1]] + [Buf("xst%d" % k) for k in range(4)]
    for st in range(17):
        rows = 128 if st < 16 else 64
        slot = st % 6
        tt = st // 4
        src = xp[st * 128:(st + 1) * 128, :] if st < 16 else xs[:, :]
        P.dma("sp", "xld%d" % slot, lambda e, slot=slot, rows=rows, src=src:
              e.dma_start(out=xst[slot][0:rows, :], in_=src), writes=[xstB[slot]])
        for half in range(2):
            b = next_bank()

            def fn(e, b=b, half=half, slot=slot, rows=rows):
                last = None
                for c4 in range(4):
                    c = half * 4 + c4
                    last = e.transpose(out=banks[b][:, c4 * 128:(c4 + 1) * 128],
                                       in_=xst[slot][:, c * 128:(c + 1) * 128],
                                       identity=ident[:, :])
                return last
            P.op("pe", fn, reads=[xstB[slot], constB], writes=[bankB[b]])
            o_ap = xT[:, half * 4:(half + 1) * 4, st * 128:st * 128 + rows]
            i_ap = banks[b][:, :].rearrange("p (c t) -> p c t", t=128)[:, :, 0:rows]
            if half == 0:
                P.op("act", lambda e, o_ap=o_ap, i_ap=i_ap: e.copy(out=o_ap, in_=i_ap),
                     reads=[bankB[b]], writes=xTB[tt][half * 4:(half + 1) * 4])
            else:
                P.op("dve", lambda e, o_ap=o_ap, i_ap=i_ap: e.tensor_copy(out=o_ap, in_=i_ap),
                     reads=[bankB[b]], writes=xTB[tt][half * 4:(half + 1) * 4])

    load_group(0, 1, after=xTB[4])
    stop_at('p0')
    def norm_tile(tt, gcol, dst, dstB, sqv, sqvB, nch=4):
        c0, w = TILES[tt]
        b = next_bank(exclude=norm_excl[0])
        npass = 8 // nch
        for ps in range(npass):
            P.op("act", lambda e, ps=ps: e.activation(out=sqv[:, :, 0:w],
                                                      in_=xT[:, ps * nch:(ps + 1) * nch, c0:c0 + w],
                                                      func=AF.Square),
                 reads=xTB[tt][ps * nch:(ps + 1) * nch], writes=[sqvB])

            def fn(e, ps=ps):
                last = None
                for c4 in range(nch):
                    last = e.matmul(banks[b][:, 0:w], lhsT=ones128, rhs=sqv[:, c4, 0:w],
                                    start=(ps == 0 and c4 == 0), stop=(ps == npass - 1 and c4 == nch - 1))
                return last
            P.op("pe", fn, reads=[sqvB, constB], writes=[bankB[b]])
        P.op("act", lambda e: e.activation(out=lnv[:, 0:w], in_=banks[b][:, 0:w], func=AF.Ln,
                                           bias=EPS, scale=1.0 / 1024.0),
             reads=[bankB[b]], writes=[lnvB])
        P.op("act", lambda e: e.activation(out=rstd[:, 0:w], in_=lnv[:, 0:w], func=AF.Exp, scale=-0.5),
             reads=[lnvB], writes=[rstdB])
        for c in range(8):
            P.op("dve", lambda e, c=c: e.scalar_tensor_tensor(
                out=dst(c), in0=xT[:, c, c0:c0 + w], scalar=vecs[:, gcol + c:gcol + c + 1],
                in1=rstd[:, 0:w], op0=ALU.mult, op1=ALU.mult),
                reads=[xTB[tt][c], rstdB, constB], writes=[dstB])

    def normA(tt, gcol):
        c0, w = TILES[tt]
        norm_tile(tt, gcol, lambda c: hT[:, c, c0:c0 + w], hTB[tt], sq, sqB)

    def stage_UZ(l, g, tt):
        s = (l * 4 + g) % 2
        sv = slots[s]
        c0, w = TILES[tt]
        nsub = 4 if tt < 4 else 1
        for si in range(nsub):
            b = next_bank()
            ring = (4 * tt + si) % 8
            tc0 = c0 + si * 128

            def fn(e, b=b, tc0=tc0):
                last = None
                for k in range(8):
                    last = e.matmul(banks[b][:, :], lhsT=hT[:, k, tc0:tc0 + 128], rhs=sv["wu"][:, k, :],
                                    start=(k == 0), stop=(k == 7))
                return last
            P.op("pe", fn, reads=[hTB[tt], slotB[s]], writes=[bankB[b]])
            P.op("act", lambda e, b=b, ring=ring: e.copy(out=utok[:, ring, :], in_=banks[b][:, :]),
                 reads=[bankB[b]], writes=[utokB[ring]])
            if (tt == 3 and si == 3) or tt == 4:
                p0, pn = (64, 64) if tt == 3 else (32, 32)
                r0 = 113 if tt == 3 else 49
                dst = npp if tt == 3 else nps
                P.op("dve", lambda e, b=b, p0=p0, pn=pn: e.tensor_copy(out=stage[0][p0:p0 + pn, 0:512],
                                                                      in_=banks[b][p0:p0 + pn, :]),
                     reads=[bankB[b], utokB[ring]], writes=[stageB[0]])
                P.dma("sp", "npo", lambda e, r0=r0, dst=dst: e.dma_start(
                    out=dst[l, :, g * 512:(g + 1) * 512], in_=stage[0][r0:r0 + 15, 0:512]),
                    reads=[stageB[0]])
        for c in range(4):
            b = next_bank()

            def fn(e, b=b, c=c):
                last = None
                for k in range(8):
                    last = e.matmul(banks[b][:, 0:w], lhsT=sv["wz"][:, k, c * 128:(c + 1) * 128],
                                    rhs=hT[:, k, c0:c0 + w], start=(k == 0), stop=(k == 7))
                return last
            P.op("pe", fn, reads=[hTB[tt], slotB[s]], writes=[bankB[b]])
            P.op("act", lambda e, b=b, c=c: e.activation(out=szA[:, c, 0:w], in_=banks[b][:, 0:w],
                                                         func=AF.Silu),
                 reads=[bankB[b]], writes=[szB[c]])

    def stage_POOL(l, g, tt):
        s = (l * 4 + g) % 2
        sv = slots[s]
        c0, w = TILES[tt]
        for c in range(4):
            b = next_bank()
            srcs = []
            if tt == 4:
                srcs.append((sv["hist"][:, c * 128:(c + 1) * 128], mmat(g, 2, 0, 64), 0, 64, [slotB[s]]))
                ring = (4 * tt) % 8
                srcs.append((utok[:, ring, c * 128:(c + 1) * 128], mmat(g, 0, 0, 64), 0, 64, [utokB[ring]]))
            else:
                if tt > 0:
                    ring = (4 * tt - 1) % 8
                    srcs.append((utok[:, ring, c * 128:(c + 1) * 128], mmat(g, 0, 128, 143), 0, 15,
                                 [utokB[ring]]))
                for si in range(4):
                    ring = (4 * tt + si) % 8
                    var = 1 if (tt == 0 and si == 0) else 0
                    n = 143 if si < 3 else 128
                    srcs.append((utok[:, ring, c * 128:(c + 1) * 128], mmat(g, var, 0, n), si * 128, n,
                                 [utokB[ring]]))
            rd = [constB]
            for sr in srcs:
                rd += sr[4]

            def fn(e, b=b, srcs=srcs):
                last = None
                for i, (l_ap, r_ap, oc, n, _) in enumerate(srcs):
                    last = e.matmul(banks[b][:, oc:oc + n], lhsT=l_ap, rhs=r_ap, start=(i == 0),
                                    stop=(i == len(srcs) - 1), skip_group_check=True)
                return last
            P.op("pe", fn, reads=rd, writes=[bankB[b]])
            if tt == 0:
                P.op("dve", lambda e, b=b, c=c: e.tensor_tensor(
                    out=ptA[:, c, 0:16], in0=banks[b][:, 0:16], in1=vecs[:, V_IC + g * 16:V_IC + g * 16 + 16],
                    op=ALU.mult), reads=[bankB[b], constB], writes=[ptB[c]])
                P.op("dve", lambda e, b=b, c=c: e.tensor_copy(out=ptA[:, c, 16:w], in_=banks[b][:, 16:w]),
                     reads=[bankB[b]], writes=[ptB[c]])
            else:
                P.op("dve", lambda e, b=b, c=c: e.tensor_copy(out=ptA[:, c, 0:w], in_=banks[b][:, 0:w]),
                     reads=[bankB[b]], writes=[ptB[c]])

    def stage_GRP(l, g, tt):
        s = (l * 4 + g) % 2
        sv = slots[s]
        c0, w = TILES[tt]
        for co in range(4):
            b = next_bank()

            def fn(e, b=b, co=co):
                last = None
                for ci in range(4):
                    last = e.matmul(banks[b][:, 0:w], lhsT=sv["wg"][:, ci, co * 128:(co + 1) * 128],
                                    rhs=ptA[:, ci, 0:w], start=(ci == 0), stop=(ci == 3))
                return last
            P.op("pe", fn, reads=ptB + [slotB[s]], writes=[bankB[b]])
            col = V_SC + l * 16 + g * 4 + co
            P.op("dve", lambda e, b=b, co=co, col=col: e.scalar_tensor_tensor(
                out=ytA[:, co, 0:w], in0=banks[b][:, 0:w], scalar=vecs[:, col:col + 1],
                in1=szA[:, co, 0:w], op0=ALU.mult, op1=ALU.mult),
                reads=[bankB[b], szB[co], constB], writes=[ytB[co]])

    def stage_OUT(l, g, tt):
        s = (l * 4 + g) % 2
        sv = slots[s]
        c0, w = TILES[tt]
        for oc in range(8):
            b = next_bank()

            def fn(e, b=b, oc=oc):
                last = None
                for ci in range(4):
                    last = e.matmul(banks[b][:, 0:w], lhsT=sv["wo"][:, ci, oc * 128:(oc + 1) * 128],
                                    rhs=ytA[:, ci, 0:w], start=(ci == 0), stop=(ci == 3))
                return last
            P.op("pe", fn, reads=ytB + [slotB[s]], writes=[bankB[b]])
            P.op("dve", lambda e, b=b, oc=oc: e.tensor_tensor(
                out=xT[:, oc, c0:c0 + w], in0=banks[b][:, 0:w], in1=xT[:, oc, c0:c0 + w], op=ALU.add),
                reads=[bankB[b], xTB[tt][oc]], writes=[xTB[tt][oc]])

    def load_wkv(part):
        wv_ = w_kv.rearrange("(k p) n -> p k n", p=128)
        P.dma("pool", "w0", lambda e: e.dma_start(out=wkv, in_=wv_[:, :, part * 1024:(part + 1) * 1024]),
              writes=[slotB[0]])

    for tt in range(5):
        normA(tt, V_NA)
    for l in range(2):
        next_gcol = V_NA + 8 if l == 0 else V_NKV
        stop_at('A0')
        prev = None
        for g in range(4):
            for tt in range(5):
                stage_UZ(l, g, tt)
                stop_at('A1')
                stage_POOL(l, g, tt)
                stop_at('A2')
                if prev is not None:
                    stage_OUT(*prev)
                    if prev[1] == 3:
                        normA(prev[2], next_gcol)
                    stop_at('A4')
                if tt == 0 and not (l == 0 and g == 0):
                    if l == 1 and g == 3:
                        load_wkv(0)
                    else:
                        ng = (l * 4 + g + 1)
                        load_group(ng // 4, ng % 4)
                stage_GRP(l, g, tt)
                stop_at('A3')
                stop_at('I%d' % (l * 20 + g * 5 + tt))
                prev = (l, g, tt)
        stage_OUT(*prev)
        normA(prev[2], next_gcol)

    stop_at('A')
    P.barrier()
    stage_rr = [0]

    def head_norm_T(b, w, gcol, dst_ap, dstBuf, sqbuf, sqbufB):
        P.op("act", lambda e: e.activation(out=sqbuf[:, 0:w], in_=banks[b][:, 0:w], func=AF.Square),
             reads=[bankB[b]], writes=[sqbufB])
        b2 = next_bank(exclude=(b,))
        P.op("pe", lambda e: e.matmul(banks[b2][:, 0:w], lhsT=onesbd, rhs=sqbuf[:, 0:w], start=True, stop=True),
             reads=[sqbufB, constB], writes=[bankB[b2]])
        P.op("act", lambda e: e.activation(out=lnv[:, 0:w], in_=banks[b2][:, 0:w], func=AF.Ln,
                                           bias=EPS, scale=1.0 / 64.0),
             reads=[bankB[b2]], writes=[lnvB])
        P.op("act", lambda e: e.activation(out=rstd[:, 0:w], in_=lnv[:, 0:w], func=AF.Exp, scale=-0.5),
             reads=[lnvB], writes=[rstdB])
        P.op("dve", lambda e: e.scalar_tensor_tensor(
            out=dst_ap, in0=banks[b][:, 0:w], scalar=vecs[:, gcol:gcol + 1], in1=rstd[:, 0:w],
            op0=ALU.mult, op1=ALU.mult),
            reads=[bankB[b], rstdB, constB, smallB], writes=[dstBuf])

    kheld = set()
    kjobs = [(tt, i) for tt in range(5) for i in range(8)]
    kbank = {}

    def k_mm(idx):
        tt, i = kjobs[idx]
        c0, w = TILES[tt]
        b = next_bank(exclude=kheld)
        kheld.add(b)
        kbank[idx] = b

        def fn(e):
            last = None
            for k in range(8):
                last = e.matmul(banks[b][:, 0:w], lhsT=wkv[:, k, i * 128:(i + 1) * 128],
                                rhs=hT[:, k, c0:c0 + w], start=(k == 0), stop=(k == 7))
            return last
        P.op("pe", fn, reads=[hTB[tt], slotB[0]], writes=[bankB[b]])
        sqb = sqk[idx % 2]
        P.op("act", lambda e: e.activation(out=sqb[:, 0:w], in_=banks[b][:, 0:w], func=AF.Square),
             reads=[bankB[b]], writes=[sqkB[idx % 2]])

    def k_norm(idx):
        tt, i = kjobs[idx]
        c0, w = TILES[tt]
        b = kbank[idx]
        sqb = sqk[idx % 2]
        b2 = next_bank(exclude=kheld)
        P.op("pe", lambda e: e.matmul(banks[b2][:, 0:w], lhsT=onesbd, rhs=sqb[:, 0:w], start=True, stop=True),
             reads=[sqkB[idx % 2], constB], writes=[bankB[b2]])
        lb = lnk[idx % 2]
        P.op("act", lambda e: e.activation(out=lb[:, 0:w], in_=banks[b2][:, 0:w], func=AF.Ln,
                                           bias=EPS, scale=1.0 / 64.0),
             reads=[bankB[b2]], writes=[lnkB[idx % 2]])
        P.op("act", lambda e: e.activation(out=lb[:, 0:w], in_=lb[:, 0:w], func=AF.Exp, scale=-0.5),
             reads=[lnkB[idx % 2]], writes=[lnkB[idx % 2]])
        P.op("dve", lambda e: e.scalar_tensor_tensor(
            out=KT[:, i, c0:c0 + w], in0=banks[b][:, 0:w], scalar=vecs[:, V_GK:V_GK + 1], in1=lb[:, 0:w],
            op0=ALU.mult, op1=ALU.mult),
            reads=[bankB[b], lnkB[idx % 2], constB], writes=[KTB[tt]])
        kheld.discard(b)

    sqk = [sq1, av(O_SQ2 + 512, [512])]
    assert O_SQ2 + 1024 <= ARENA
    sqkB = [Buf("sqk0"), Buf("sqk1")]
    lnk = [lnv, rstd]
    lnkB = [lnvB, rstdB]
    k_mm(0)
    for idx in range(len(kjobs)):
        if idx + 1 < len(kjobs):
            k_mm(idx + 1)
        k_norm(idx)

    def k_tokmajor(st):
        tt = st // 4
        tc0 = st * 128
        rows = 128 if st < 16 else 64
        sl = stage_rr[0] % 2
        stage_rr[0] += 1
        bb = []
        for half in range(2):
            b = next_bank(exclude=tuple(bb))
            bb.append(b)

            def fn(e, b=b, half=half):
                last = None
                for k in range(8):
                    last = e.matmul(banks[b][:, :], lhsT=hT[:, k, tc0:tc0 + 128],
                                    rhs=wkv[:, k, half * 512:(half + 1) * 512], start=(k == 0), stop=(k == 7))
                return last
            P.op("pe", fn, reads=[hTB[tt], slotB[0]], writes=[bankB[b]])
            P.op("act", lambda e, b=b, half=half: e.activation(
                out=stage[sl][:, half * 512:(half + 1) * 512], in_=banks[b][:, :], func=AF.Square),
                reads=[bankB[b]], writes=[stageB[sl]])
        P.op("dve", lambda e: e.tensor_reduce(out=small[:, 16:32],
                                              in_=stage[sl][:, :].rearrange("p (h d) -> p h d", d=64),
                                              axis=AX.X, op=ALU.add),
             reads=[stageB[sl]], writes=[smallB])
        P.op("act", lambda e: e.activation(out=small[:, 32:48], in_=small[:, 16:32], func=AF.Ln,
                                           bias=EPS, scale=1.0 / 64.0), reads=[smallB], writes=[smallB])
        P.op("act", lambda e: e.activation(out=small[:, 48:64], in_=small[:, 32:48], func=AF.Exp, scale=-0.5),
             reads=[smallB], writes=[smallB])
        for half in range(2):
            b = bb[half]
            o_ap = stage[sl][:, half * 512:(half + 1) * 512].rearrange("p (h d) -> p h d", d=64)
            P.op("dve", lambda e, b=b, half=half, o_ap=o_ap: e.tensor_tensor(
                out=o_ap, in0=banks[b][:, :].rearrange("p (h d) -> p h d", d=64),
                in1=small[:, 48 + half * 8:48 + half * 8 + 8].unsqueeze(2).broadcast_to([128, 8, 64]),
                op=ALU.mult), reads=[bankB[b], smallB], writes=[stageB[sl]])
            P.op("dve", lambda e, o_ap=o_ap: e.tensor_tensor(
                out=o_ap, in0=o_ap, in1=gkrow[:, :].unsqueeze(1).broadcast_to([128, 8, 64]), op=ALU.mult),
                reads=[constB, stageB[sl]], writes=[stageB[sl]])
        dst = nkp[(st - 12) * 128:(st - 11) * 128, :] if st < 16 else nks[:, :]
        P.dma("sp", "ost%d" % sl, lambda e: e.dma_start(out=dst, in_=stage[sl][0:rows, :]),
              reads=[stageB[sl]])

    for st in (12, 13, 14, 15, 16):
        k_tokmajor(st)

    load_wkv(1)
    for st in range(17):
        tt = st // 4
        tc0 = st * 128
        rows = 128 if st < 16 else 64
        want_out = st >= 12
        if want_out:
            sl = stage_rr[0] % 2
            stage_rr[0] += 1
        for half in range(2):
            b = next_bank()

            def fn(e, b=b, half=half, tc0=tc0):
                last = None
                for k in range(8):
                    last = e.matmul(banks[b][:, :], lhsT=hT[:, k, tc0:tc0 + 128],
                                    rhs=wkv[:, k, half * 512:(half + 1) * 512], start=(k == 0), stop=(k == 7))
                return last
            P.op("pe", fn, reads=[hTB[tt], slotB[0]], writes=[bankB[b]])
            bv = banks[b][:, :].rearrange("p (i h d) -> p i h d", h=2, d=64)
            ncp = 2 if st < 16 else 1
            for cp in range(ncp):
                for hh in range(2):
                    ch = 2 * st + cp
                    o_ap = Vh[hh * 64:(hh + 1) * 64, ch, half * 4:(half + 1) * 4, :]
                    i_ap = bv[cp * 64:(cp + 1) * 64, :, hh, :]
                    if half == 0:
                        P.op("act", lambda e, o_ap=o_ap, i_ap=i_ap: e.copy(out=o_ap, in_=i_ap),
                             reads=[bankB[b]], writes=[VhB[ch]])
                    else:
                        P.op("dve", lambda e, o_ap=o_ap, i_ap=i_ap: e.tensor_copy(out=o_ap, in_=i_ap),
                             reads=[bankB[b]], writes=[VhB[ch]])
            if want_out and half == 0:
                P.op("act", lambda e, b=b, half=half, sl=sl: e.copy(
                    out=stage[sl][:, half * 512:(half + 1) * 512], in_=banks[b][:, :]),
                    reads=[bankB[b]], writes=[stageB[sl]])
            elif want_out:
                P.op("dve", lambda e, b=b, half=half, sl=sl: e.tensor_copy(
                    out=stage[sl][:, half * 512:(half + 1) * 512], in_=banks[b][:, :]),
                    reads=[bankB[b]], writes=[stageB[sl]])
        if want_out:
            dst = nvp[(st - 12) * 128:(st - 11) * 128, :] if st < 16 else nvs[:, :]
            P.dma("sp", "ost%d" % sl, lambda e, dst=dst, sl=sl, rows=rows: e.dma_start(
                out=dst, in_=stage[sl][0:rows, :]), reads=[stageB[sl]])

    stop_at('KV')
    P.barrier()
    rbT = stage[0][0:16, 0:384]
    rbX = stage[0][0:16, 512:896]
    rbIn = stage[1]

    def prep_bias(j):
        P.dma("sp", "rb", lambda e: e.dma_start(out=rbIn[:, 0:16], in_=rel_bias[j, 0:128, :]), writes=[stageB[1]])
        P.dma("sp", "rb", lambda e: e.dma_start(out=rbIn[:, 128:144], in_=rel_bias[j, 128:256, :]),
              writes=[stageB[1]])
        P.dma("sp", "rb", lambda e: e.dma_start(out=rbIn[0:1, 256:272], in_=rel_bias[j, 256:257, :]),
              writes=[stageB[1]])
        b = next_bank()

        def fn(e):
            e.transpose(out=banks[b][:, 0:128], in_=rbIn[:, 0:128], identity=ident[:, :])
            e.transpose(out=banks[b][:, 128:256], in_=rbIn[:, 128:256], identity=ident[:, :])
            return e.transpose(out=banks[b][:, 256:384], in_=rbIn[:, 256:384], identity=ident[:, :])
        P.op("pe", fn, reads=[stageB[1], constB], writes=[bankB[b]])
        P.op("dve", lambda e: e.tensor_copy(out=rbT[:, 0:257], in_=banks[b][0:16, 0:257]),
             reads=[bankB[b]], writes=[stageB[0]])
        P.op("dve", lambda e: e.tensor_copy(out=rbT[:, 257:384],
                                            in_=rbT[:, 256:257].to_broadcast([16, 127])),
             reads=[stageB[0]], writes=[stageB[0]])
        P.op("dve", lambda e: e.tensor_scalar(out=rbX, in0=rbT, scalar1=rbT[:, 256:257], scalar2=None,
                                              op0=ALU.subtract),
             reads=[stageB[0]], writes=[stageB[0]])
        P.dma("sp", "rbo", lambda e: e.dma_start(out=rbx_d[j], in_=rbX), reads=[stageB[0]], writes=[rbxB])
        P.dma("sp", "rbo", lambda e: e.dma_start(out=cst_d[j].rearrange("(h o) -> h o", o=1),
                                                 in_=rbT[:, 256:257]), reads=[stageB[0]], writes=[rbxB])

    rbxB = Buf("rbx")
    prep_bias(0)
    prep_bias(1)

    def key_list(tt):
        ents = []
        if tt < 4:
            for kc in range(max(0, 8 * tt - 8), 8 * tt + 8):
                qa = max(kc, 8 * tt) - 8 * tt
                qb = min(kc + 8, 8 * tt + 7) - 8 * tt + 1
                ents.append(("n", kc, qa, qb, 8 * tt + qa - kc))
        else:
            for jc in range(8):
                ents.append(("c", jc, 0, 1, 8 - jc))
            ents.append(("n", 32, 0, 1, 0))
        return ents

    wqzB = [Buf("wqz0"), Buf("wqz1")]
    woB = [Buf("wo%d" % i) for i in range(4)]
    held = set()
    norm_excl[0] = held
    steps = [(j, tt, i) for j in range(2) for tt in range(5) for i in range(8)]
    NSTEP = len(steps) if _STOP[0] is None or not str(_STOP[0]).startswith('N') else int(_STOP[0][1:])
    hb_of = {}
    _hb = 0
    for j in range(2):
        for tt in range(5):
            hb_of[(j, tt)] = _hb
            _hb ^= 1
    st8 = {}
    ez = stage[1][:, 512:1024]
    ezB = Buf("ez")

    def b_loads(n):
        j, tt, i = steps[n]
        s = n % 2
        s3 = n % 4
        P.dma("pool", "wb%d" % s, lambda e: e.dma_start(
            out=wB[s]["wq"], in_=wqz[j, i].rearrange("p (k n) -> p k n", n=128)), writes=[wqzB[s]])
        P.dma("pool", "wb%d" % s, lambda e: e.dma_start(
            out=wB[s]["wz"], in_=wqz[j, 8 + i].rearrange("p (k n) -> p k n", n=128)), writes=[wqzB[s]])
        P.dma("pool", "wo%d" % s3, lambda e: e.dma_start(
            out=wo_v[s3], in_=w_out_b[j, i * 128:(i + 1) * 128, :]), writes=[woB[s3]])
        for hh in range(2):
            src = bass.AP(tensor=rbx_d.tensor, offset=(j * 16 + 2 * i + hh) * 384 + 65, ap=[[1, 64], [1, 192]])
            P.dma("pool", "bt%d" % s, lambda e, src=src, hh=hh: e.dma_start(
                out=Bt[s][hh * 64:(hh + 1) * 64, :], in_=src), reads=[rbxB], writes=[BtB[s]])
            csrc = bass.AP(tensor=cst_d.tensor, offset=j * 16 + 2 * i + hh, ap=[[0, 64], [1, 1]])
            P.dma("sp", "cv%d" % s, lambda e, csrc=csrc, hh=hh: e.dma_start(
                out=small[hh * 64:(hh + 1) * 64, s:s + 1], in_=csrc), reads=[rbxB], writes=[cvecB[s]])
        return

    def b_cache(n):
        j, tt, i = steps[n]
        s = n % 2
        if tt == 4:
            P.dma("sp", "ck", lambda e: e.dma_start(
                out=stage[1][:, 0:512].rearrange("p (t c) -> p t c", c=128),
                in_=ck[:, i * 128:(i + 1) * 128].rearrange("(t p) c -> p t c", p=128)), writes=[stageB[1]])
            bt = next_bank(exclude=held)

            def fnT(e):
                last = None
                for t4 in range(4):
                    last = e.transpose(out=banks[bt][:, t4 * 128:(t4 + 1) * 128],
                                       in_=stage[1][:, t4 * 128:(t4 + 1) * 128], identity=ident[:, :])
                return last
            P.op("pe", fnT, reads=[stageB[1], constB], writes=[bankB[bt]])
            P.op("dve", lambda e: e.tensor_copy(out=cKT[s], in_=banks[bt][:, :]), reads=[bankB[bt]],
                 writes=[cKTB[s]])
            for hh in range(2):
                h = 2 * i + hh
                P.dma("pool", "cvh%d" % s, lambda e, hh=hh, h=h: e.dma_start(
                    out=cVh[s][hh * 64:(hh + 1) * 64, :, :],
                    in_=cv[:, h * 64:(h + 1) * 64].rearrange("(jc k) d -> k jc d", k=64)), writes=[cVhB[s]])

    qc = stage[0][:, 0:512]
    ssc = stage[0][:, 512:1024]
    qcB, sscB = Buf("qc"), Buf("ssc")

    def b_proj(n):
        j, tt, i = steps[n]
        c0, w = TILES[tt]
        s = n % 2
        hbuf = hb_of[(j, tt)]
        bq = next_bank(exclude=held)

        def fnq(e):
            last = None
            for k in range(8):
                last = e.matmul(banks[bq][:, 0:w], lhsT=wB[s]["wq"][:, k, :], rhs=hTt[hbuf][:, k, 0:w],
                                start=(k == 0), stop=(k == 7))
            return last
        P.op("pe", fnq, reads=[hTtB[hbuf], wqzB[s]], writes=[bankB[bq]])
        P.op("dve", lambda e: e.tensor_copy(out=qc[:, 0:w], in_=banks[bq][:, 0:w]),
             reads=[bankB[bq]], writes=[qcB, stageB[0]])
        P.op("dve", lambda e: e.tensor_tensor(out=sqq[:, 0:w], in0=banks[bq][:, 0:w], in1=qc[:, 0:w], op=ALU.mult),
             reads=[bankB[bq], qcB], writes=[sqqB])
        bz = next_bank(exclude=held)

        def fnz(e):
            last = None
            for k in range(8):
                last = e.matmul(banks[bz][:, 0:w], lhsT=wB[s]["wz"][:, k, :], rhs=hTt[hbuf][:, k, 0:w],
                                start=(k == 0), stop=(k == 7))
            return last
        P.op("pe", fnz, reads=[hTtB[hbuf], wqzB[s]], writes=[bankB[bz]])
        P.op("dve", lambda e: e.tensor_copy(out=szi[s][:, 0:w], in_=banks[bz][:, 0:w]),
             reads=[bankB[bz]], writes=[sziB[s]])

    def b_ss(n):
        j, tt, i = steps[n]
        c0, w = TILES[tt]
        b2 = next_bank(exclude=held)
        P.op("pe", lambda e: e.matmul(banks[b2][:, 0:w], lhsT=onesbd, rhs=sqq[:, 0:w], start=True, stop=True),
             reads=[sqqB, constB], writes=[bankB[b2]])
        P.op("dve", lambda e: e.tensor_copy(out=ssc[:, 0:w], in_=banks[b2][:, 0:w]),
             reads=[bankB[b2]], writes=[sscB, stageB[0]])

    def b_qnorm_act(n):
        j, tt, i = steps[n]
        c0, w = TILES[tt]
        P.op("act", lambda e: e.activation(out=lnv[:, 0:w], in_=ssc[:, 0:w], func=AF.Ln,
                                           bias=EPS, scale=1.0 / 64.0),
             reads=[sscB], writes=[lnvB])
        P.op("act", lambda e: e.activation(out=rstd[:, 0:w], in_=lnv[:, 0:w], func=AF.Exp, scale=-0.5),
             reads=[lnvB], writes=[rstdB])

    def b_qnorm_dve(n):
        j, tt, i = steps[n]
        c0, w = TILES[tt]
        s = n % 2
        gcol = V_GQS + j
        P.op("dve", lambda e: e.scalar_tensor_tensor(
            out=QTi[s][:, 0:w], in0=qc[:, 0:w], scalar=vecs[:, gcol:gcol + 1], in1=rstd[:, 0:w],
            op0=ALU.mult, op1=ALU.mult),
            reads=[qcB, rstdB, constB, smallB], writes=[QTiB[s]])

    def b_ez(n):
        j, tt, i = steps[n]
        c0, w = TILES[tt]
        s = n % 2
        P.op("act", lambda e: e.activation(out=ez[:, 0:w], in_=szi[s][:, 0:w], func=AF.Exp, scale=-1.0),
             reads=[sziB[s]], writes=[ezB, stageB[1]])

    def pack_groups(ents):
        rem = sorted(range(len(ents)), key=lambda e: -(ents[e][3] - ents[e][2]))
        groups = []
        while rem:
            g = [rem.pop(0)]
            tot = (ents[g[0]][3] - ents[g[0]][2]) * 64
            k = 0
            while k < len(rem):
                nn = (ents[rem[k]][3] - ents[rem[k]][2]) * 64
                if tot + nn <= 512:
                    g.append(rem.pop(k))
                    tot += nn
                else:
                    k += 1
            groups.append(g)
        return groups

    def b_sloop(n):
        j, tt, i = steps[n]
        c0, w = TILES[tt]
        s = n % 2
        ents = key_list(tt)
        groups = pack_groups(ents)
        info = {}

        def alloc():
            bo = next_bank(exclude=held)
            held.add(bo)
            bd = next_bank(exclude=held)
            held.add(bd)
            st8[n] = dict(bo=bo, bd=bd)

        def emit_S(gi):
            bs = next_bank(exclude=held)
            held.add(bs)
            offs = []
            off = 0
            rd = [QTiB[s], BtB[s], constB]
            mm = []
            for e_ in groups[gi]:
                kind, idx, qa, qb, d0 = ents[e_]
                q0, q1 = qa * 64, qb * 64
                nq = q1 - q0
                nb = max(0, min(3 - d0, qb - qa)) * 64
                if kind == "n":
                    kcol = idx * 64
                    k_lo, k_hi = KT[0:64, i, kcol:kcol + 64], KT[64:128, i, kcol:kcol + 64]
                    rd.append(KTB[4] if idx == 32 else KTB[idx // 8])
                else:
                    k_lo, k_hi = cKT[s][0:64, idx * 64:(idx + 1) * 64], cKT[s][64:128, idx * 64:(idx + 1) * 64]
                    rd.append(cKTB[s])
                mm.append((off, q0, q1, nb, d0, k_lo, k_hi))
                offs.append(off)
                off += nq
            info[gi] = (bs, offs, off)

            def fns(e):
                last = None
                for (o_, q0, q1, nb, d0, k_lo, k_hi) in mm:
                    nq = q1 - q0
                    e.matmul(banks[bs][0:64, o_:o_ + nq], lhsT=k_lo, rhs=QTi[s][0:64, q0:q1], start=True,
                             stop=(nb == 0), skip_group_check=True)
                    last = e.matmul(banks[bs][64:128, o_:o_ + nq], lhsT=k_hi, rhs=QTi[s][64:128, q0:q1],
                                    start=True, stop=(nb == 0), skip_group_check=True)
                    if nb > 0:
                        e.matmul(banks[bs][0:64, o_:o_ + nb], lhsT=jbd[0:64, 0:64],
                                 rhs=Bt[s][0:64, d0 * 64:d0 * 64 + nb], start=False, stop=True,
                                 skip_group_check=True)
                        last = e.matmul(banks[bs][64:128, o_:o_ + nb], lhsT=jbd[64:128, 64:128],
                                        rhs=Bt[s][64:128, d0 * 64:d0 * 64 + nb], start=False, stop=True,
                                        skip_group_check=True)
                return last
            P.op("pe", fns, reads=rd, writes=[bankB[bs]])

        def emit_PV(gi):
            bs, offs, tot = info[gi]
            bo, bd = st8[n]["bo"], st8[n]["bd"]
            pb = gi % 3
            P.op("act", lambda e: e.activation(
                out=Pb[pb][:, 0:tot], in_=banks[bs][:, 0:tot], func=AF.Exp, bias=small[:, s:s + 1], scale=1.0),
                reads=[bankB[bs], cvecB[s]], writes=[PbB[pb]])
            held.discard(bs)
            rd = [PbB[pb], constB]
            mm = []
            for e_, o_ in zip(groups[gi], offs):
                kind, idx, qa, qb, d0 = ents[e_]
                q0, q1 = qa * 64, qb * 64
                if kind == "n":
                    v_lo, v_hi = Vh[0:64, idx, i, :], Vh[64:128, idx, i, :]
                    rd.append(VhB[idx])
                else:
                    v_lo, v_hi = cVh[s][0:64, idx, :], cVh[s][64:128, idx, :]
                    rd.append(cVhB[s])
                mm.append((o_, q0, q1, v_lo, v_hi))

            def fno(e):
                last = None
                for k_, (o_, q0, q1, v_lo, v_hi) in enumerate(mm):
                    nq = q1 - q0
                    st_ = (gi == 0 and k_ == 0)
                    sp_ = (gi == len(groups) - 1 and k_ == len(mm) - 1)
                    e.matmul(banks[bo][0:64, q0:q1], lhsT=v_lo, rhs=Pb[pb][0:64, o_:o_ + nq], start=st_, stop=sp_,
                             skip_group_check=True)
                    e.matmul(banks[bo][64:128, q0:q1], lhsT=v_hi, rhs=Pb[pb][64:128, o_:o_ + nq], start=st_,
                             stop=sp_, skip_group_check=True)
                    e.matmul(banks[bd][0:64, q0:q1], lhsT=onesbd[0:64, 0:64], rhs=Pb[pb][0:64, o_:o_ + nq],
                             start=st_, stop=sp_, skip_group_check=True)
                    last = e.matmul(banks[bd][64:128, q0:q1], lhsT=onesbd[64:128, 64:128],
                                    rhs=Pb[pb][64:128, o_:o_ + nq], start=st_, stop=sp_, skip_group_check=True)
                return last
            P.op("pe", fno, reads=rd, writes=[bankB[bo], bankB[bd]])

        AHEAD = 2
        th = [alloc]
        for gi in range(min(AHEAD, len(groups))):
            th.append(lambda gi=gi: emit_S(gi))
        for gi in range(len(groups)):
            if gi + AHEAD < len(groups):
                th.append(lambda gi=gi: emit_S(gi + AHEAD))
            th.append(lambda gi=gi: emit_PV(gi))
        return th

    def b_gate(n):
        j, tt, i = steps[n]
        c0, w = TILES[tt]
        s = n % 2
        P.op("act", lambda e: e.activation(out=ez[:, 0:w], in_=szi[s][:, 0:w], func=AF.Exp, scale=-1.0),
             reads=[sziB[s]], writes=[ezB, stageB[1]])

    def b_tail_D(n):
        j, tt, i = steps[n]
        c0, w = TILES[tt]
        bd = st8[n]["bd"]
        P.op("dve", lambda e: e.scalar_tensor_tensor(
            out=rden[:, 0:w], in0=ez[:, 0:w], scalar=1.0, in1=banks[bd][:, 0:w], op0=ALU.add, op1=ALU.mult),
            reads=[ezB, bankB[bd]], writes=[rdenB])

    def b_tail_t0(n):
        j, tt, i = steps[n]
        c0, w = TILES[tt]
        s = n % 2
        bo = st8[n]["bo"]
        P.op("dve", lambda e: e.tensor_tensor(out=ez[:, 0:w], in0=banks[bo][:, 0:w], in1=szi[s][:, 0:w],
                                              op=ALU.mult),
             reads=[bankB[bo], sziB[s]], writes=[ezB, stageB[1]])

    def b_tail_rd(n):
        j, tt, i = steps[n]
        c0, w = TILES[tt]
        P.op("act", lambda e: e.activation(out=rden[:, 0:w], in_=rden[:, 0:w], func=AF.Ln),
             reads=[rdenB], writes=[rdenB])
        P.op("act", lambda e: e.activation(out=rden[:, 0:w], in_=rden[:, 0:w], func=AF.Exp, scale=-1.0),
             reads=[rdenB], writes=[rdenB])

    def b_tail_y(n):
        j, tt, i = steps[n]
        c0, w = TILES[tt]
        y4 = n % 4
        P.op("dve", lambda e: e.tensor_tensor(out=yg[y4][:, 0:w], in0=ez[:, 0:w], in1=rden[:, 0:w],
                                              op=ALU.mult),
             reads=[rdenB, ezB], writes=[ygB[y4]])

    def b_outproj_group(g):
        prs = [p for p in (2 * g, 2 * g + 1) if p < NSTEP]
        j, tt, _ = steps[prs[0]]
        c0, w = TILES[tt]
        for oc in range(8):
            b = next_bank(exclude=held)

            def fn(e, b=b, oc=oc):
                last = None
                for ki, p in enumerate(prs):
                    last = e.matmul(banks[b][:, 0:w], lhsT=wo_v[p % 4][:, oc * 128:(oc + 1) * 128],
                                    rhs=yg[p % 4][:, 0:w], start=(ki == 0), stop=(ki == len(prs) - 1))
                return last
            P.op("pe", fn, reads=[ygB[p % 4] for p in prs] + [woB[p % 4] for p in prs], writes=[bankB[b]])
            P.op("dve", lambda e, b=b, oc=oc: e.tensor_tensor(
                out=xT[:, oc, c0:c0 + w], in0=banks[b][:, 0:w], in1=xT[:, oc, c0:c0 + w], op=ALU.add),
                reads=[bankB[b], xTB[tt][oc]], writes=[xTB[tt][oc]])

    def b_norm(j, tt):
        c0, w = TILES[tt]
        hbuf = hb_of[(j, tt)]
        norm_tile(tt, V_NB + j * 8, lambda c: hTt[hbuf][:, c, 0:w], hTtB[hbuf], sq2, sqB, nch=2)

    b_norm(0, 0)
    b_loads(0)
    b_proj(0)
    b_ss(0)
    b_qnorm_act(0)
    b_qnorm_dve(0)
    ngroups = (NSTEP + 1) // 2
    next_g = 0
    late_release = []
    for n in range(NSTEP):
        j, tt, i = steps[n]
        while next_g < ngroups and 2 * next_g + 3 <= n:
            b_outproj_group(next_g)
            next_g += 1
        if n >= 1:
            b_tail_y(n - 1)
        if n + 1 < NSTEP:
            b_loads(n + 1)
            b_cache(n + 1)
            b_proj(n + 1)
        for b_ in late_release:
            held.discard(b_)
        late_release = []
        if i == 5 and n + 3 < NSTEP:
            jn, ttn, _ = steps[n + 3]
            b_norm(jn, ttn)
        th = b_sloop(n)
        cut = min(len(th), 1 + 2 + 2 * 2)
        for t_ in th[:cut]:
            t_()
        b_gate(n)
        for t_ in th[cut:]:
            t_()
        b_tail_D(n)
        if n + 1 < NSTEP:
            b_ss(n + 1)
        b_tail_t0(n)
        if n + 1 < NSTEP:
            b_qnorm_act(n + 1)
            b_qnorm_dve(n + 1)
        b_tail_rd(n)
        late_release = [st8[n]["bo"], st8[n]["bd"]]
    b_tail_y(NSTEP - 1)
    for b_ in late_release:
        held.discard(b_)
    while next_g < ngroups:
        b_outproj_group(next_g)
        next_g += 1
    stop_at('B')
    P.barrier()
    fst = [stage[0], stage[1]] + [arena[:, 2048 * k:2048 * (k + 1)].bitcast(F32) for k in range(6)]
    fstB = [stageB[0], stageB[1]] + [Buf("fst%d" % k) for k in range(6)]
    for st in range(17):
        rows = 128 if st < 16 else 64
        tt = st // 4
        sl = st % 8
        for half in range(2):
            b = next_bank()

            def fn(e, b=b, half=half, st=st, rows=rows):
                last = None
                for c4 in range(4):
                    c = half * 4 + c4
                    last = e.transpose(out=banks[b][:, c4 * 128:(c4 + 1) * 128],
                                       in_=xT[:, c, st * 128:(st + 1) * 128], identity=ident[:, :])
                return last
            P.op("pe", fn, reads=xTB[tt][half * 4:(half + 1) * 4] + [constB], writes=[bankB[b]])
            if half == 0:
                P.op("act", lambda e, b=b, sl=sl, rows=rows: e.copy(out=fst[sl][0:rows, 0:512],
                                                                   in_=banks[b][0:rows, :]),
                     reads=[bankB[b]], writes=[fstB[sl]])
            else:
                P.op("dve", lambda e, b=b, sl=sl, rows=rows: e.tensor_copy(out=fst[sl][0:rows, 512:1024],
                                                                          in_=banks[b][0:rows, :]),
                     reads=[bankB[b]], writes=[fstB[sl]])
        dst = yp[st * 128:(st + 1) * 128, :] if st < 16 else ys[:, :]
        P.dma("sp", "fst%d" % sl, lambda e, dst=dst, sl=sl, rows=rows: e.dma_start(
            out=dst, in_=fst[sl][0:rows, :]), reads=[fstB[sl]])
    P.barrier()

    emit()
    es.close()


def _consts():
    bf = ml_dtypes.bfloat16
    cb = np.zeros((128, NCB), np.float32)
    cb[:, C_ONES:C_ONES + 128] = 1.0
    cb[0:64, C_OBD:C_OBD + 64] = 1.0
    cb[64:128, C_OBD + 64:C_OBD + 128] = 1.0
    for r in range(64):
        cb[r, C_JBD + 63 - r] = 1.0
        cb[64 + r, C_JBD + 64 + 63 - r] = 1.0
    s = np.arange(128)[:, None]
    t = np.arange(144)[None, :]
    dt = t - s
    for g, wdw in enumerate(POOL_W):
        band = ((dt >= 0) & (dt <= wdw - 1)).astype(np.float32)
        dlt = (dt == 0).astype(np.float32)
        o = C_M + (g * 3) * 144
        cb[:, o:o + 144] = band / wdw - dlt
        cnt = np.minimum(t + 1, wdw).astype(np.float32)
        first = band / wdw - dlt
        first[:, :16] = (band - cnt * dlt)[:, :16]
        cb[:, o + 144:o + 288] = first
        dth = t + 15 - s
        bh = ((dth >= 1) & (dth <= wdw - 1) & (s < 15)).astype(np.float32)
        cb[:, o + 288:o + 432] = bh / wdw
    return cb.astype(bf), np.eye(128, dtype=np.float32)


_NC_CACHE = {}


def kernel(x_prompt, x_sample, state_pool, cache_k, cache_v, norm_a, w_in_a, w_grp_a, scale_a, w_out_a,
           norm_kv, w_kv, g_k, norm_b, w_in_b, g_q, rel_bias_b, w_out_b):
    f = lambda a: np.ascontiguousarray(np.asarray(a, dtype=np.float32))
    x_prompt, x_sample, state_pool, cache_k, cache_v = map(f, (x_prompt, x_sample, state_pool, cache_k, cache_v))
    norm_a, w_in_a, w_grp_a, scale_a, w_out_a = map(f, (norm_a, w_in_a, w_grp_a, scale_a, w_out_a))
    norm_kv, w_kv, g_k, norm_b, w_in_b, g_q, rel_bias_b, w_out_b = map(
        f, (norm_kv, w_kv, g_k, norm_b, w_in_b, g_q, rel_bias_b, w_out_b))
    cbf_np, ident_np = _consts()
    vecs = np.zeros((128, NV), np.float32)
    for l in range(2):
        vecs[:, V_NA + l * 8:V_NA + l * 8 + 8] = norm_a[l].reshape(8, 128).T
        vecs[:, V_SC + l * 16:V_SC + l * 16 + 16] = scale_a[l].reshape(16, 128).T
        vecs[:, V_NB + l * 8:V_NB + l * 8 + 8] = norm_b[l].reshape(8, 128).T
        vecs[:, V_GQ + l] = np.concatenate([g_q[l], g_q[l]])
    vecs[:, V_NKV:V_NKV + 8] = norm_kv.reshape(8, 128).T
    vecs[:, V_GK] = np.concatenate([g_k, g_k])
    for g, wdw in enumerate(POOL_W):
        vecs[:, V_IC + g * 16:V_IC + g * 16 + 16] = (1.0 / np.minimum(np.arange(16) + 1, wdw)).astype(np.float32)[None, :]
    gkrow = np.ascontiguousarray(np.broadcast_to(g_k[None, :], (128, 64)))
    wqz = np.ascontiguousarray(w_in_b.reshape(2, 8, 128, 16, 128).transpose(0, 3, 2, 1, 4).reshape(2, 16, 128, 1024))
    if "nc" not in _NC_CACHE:
        _NC_CACHE["nc"] = build_program()
    nc = _NC_CACHE["nc"]
    in_maps = []
    for b in range(8):
        in_maps.append({
            "xp": x_prompt[b], "xs": x_sample[b], "hist": np.ascontiguousarray(state_pool[:, b]),
            "ck": cache_k[b].reshape(512, 1024), "cv": cache_v[b].reshape(512, 1024),
            "w_in_a": w_in_a, "w_grp_a": w_grp_a, "w_out_a": w_out_a, "w_kv": w_kv, "wqz": wqz,
            "w_out_b": w_out_b, "rel_bias": rel_bias_b, "vecs": vecs, "gkrow": gkrow, "ident": ident_np,
            "cbf": cbf_np,
        })
    res = run_bass_kernel_spmd(nc, in_maps, core_ids=list(range(8)))
    r = res.results
    y_prompt = np.stack([r[b]["yp"] for b in range(8)]).astype(np.float32)
    y_sample = np.stack([r[b]["ys"] for b in range(8)]).astype(np.float32)
    new_pool_p = np.stack([r[b]["npp"] for b in range(8)], axis=1).astype(np.float32)
    new_pool_s = np.stack([r[b]["nps"] for b in range(8)], axis=1).astype(np.float32)
    new_k_p = np.stack([r[b]["nkp"] for b in range(8)]).reshape(8, 512, 16, 64).astype(np.float32)
    new_v_p = np.stack([r[b]["nvp"] for b in range(8)]).reshape(8, 512, 16, 64).astype(np.float32)
    new_k_s = np.stack([r[b]["nks"] for b in range(8)]).reshape(8, 64, 16, 64).astype(np.float32)
    new_v_s = np.stack([r[b]["nvs"] for b in range(8)]).reshape(8, 64, 16, 64).astype(np.float32)
    return (y_prompt, y_sample, new_pool_p, new_pool_s, new_k_p, new_v_p, new_k_s, new_v_s)
```
